# Optimizing a Trainium2 kernel written in Bass

```python
import math
import jax, jax.numpy as jnp
from jax import lax
import numpy as np

D_MODEL = 1024
BATCH = 2
SEQ = 8192
DEPTH = 4

N_MEM = 256
N_A_LAYERS = DEPTH // 2
N_B_LAYERS = DEPTH - N_A_LAYERS
QBLOCK = 128

TOK_HEADS = 12
HEAD_DIM = 64
MEM_HEADS = 4
MEM_HEAD_DIM = 64
TOK_WIDTH = TOK_HEADS * HEAD_DIM
MEM_WIDTH = MEM_HEADS * MEM_HEAD_DIM
MIX_WIDTH = TOK_WIDTH + MEM_WIDTH

MLA_Q_RANK = 256
MLA_KV_RANK = 128
MLA_NOPE = 64
MLA_ROPE = 32
MLA_V = HEAD_DIM
ROPE_THETA = 10000.0
MLA_IN = MLA_Q_RANK + MLA_KV_RANK + MLA_ROPE + MEM_WIDTH

NSA_GROUPS = 2
NSA_HPG = TOK_HEADS // NSA_GROUPS
NSA_BRANCHES = 3
CMP_LEN = 32
CMP_STRIDE = 16
CMP_HIDDEN = 256
SEL_LEN = 64
SEL_TOPK = 16
WINDOW = 512
NSA_IN = TOK_WIDTH + TOK_HEADS * NSA_BRANCHES + MEM_WIDTH
NSA_KV = NSA_BRANCHES * 2 * NSA_GROUPS * HEAD_DIM

D_FF = 2816

DN_ALPHA = (2 * DEPTH) ** 0.25
DN_BETA = (8 * DEPTH) ** -0.25

LN_EPS = 1e-5
RMS_EPS = 1e-6
NEG = -1e30
FORCE_BONUS = 1e4

kernel_name = "yoco_mla_nsa_macaron_deepnorm_mem"


def layer_norm(x, g, b):
    xf = x.astype(jnp.float32)
    mu = jnp.mean(xf, -1, keepdims=True)
    var = jnp.mean(jnp.square(xf - mu), -1, keepdims=True)
    return ((xf - mu) * lax.rsqrt(var + LN_EPS) * g + b).astype(x.dtype)


def rms_norm(x, g):
    xf = x.astype(jnp.float32)
    return (xf * lax.rsqrt(jnp.mean(xf * xf, -1, keepdims=True) + RMS_EPS) * g).astype(x.dtype)


def masked_softmax(s, mask, axis=-1):
    s = jnp.where(mask, s.astype(jnp.float32), NEG)
    m = jnp.max(s, axis=axis, keepdims=True)
    e = jnp.exp(s - m) * mask
    return e / jnp.maximum(jnp.sum(e, axis=axis, keepdims=True), 1e-30)


def swiglu(x, w_gu, w_down):
    g, u = jnp.split(x @ w_gu, 2, axis=-1)
    return (jax.nn.silu(g) * u) @ w_down


def rope(x, pos):
    half = x.shape[-1] // 2
    freq = ROPE_THETA ** (-jnp.arange(half, dtype=jnp.float32) / half)
    ang = pos.astype(jnp.float32)[:, None] * freq[None, :]
    cos, sin = jnp.cos(ang), jnp.sin(ang)
    x1, x2 = x[..., :half], x[..., half:]
    return jnp.concatenate([x1 * cos - x2 * sin, x2 * cos + x1 * sin], -1).astype(x.dtype)


def alibi_slopes(n):
    def pow2(k):
        start = 2.0 ** (-8.0 / k)
        return [start ** (i + 1) for i in range(k)]
    if math.log2(n).is_integer():
        s = pow2(n)
    else:
        c = 2 ** math.floor(math.log2(n))
        s = pow2(c) + pow2(2 * c)[0::2][: n - c]
    return np.asarray(s, np.float32)


def memory_attention(q_mem, mem, w_mem_kv):
    B, T, _ = q_mem.shape
    M = mem.shape[1]
    k, v = jnp.split(mem @ w_mem_kv, 2, axis=-1)
    q = q_mem.reshape(B, T, MEM_HEADS, MEM_HEAD_DIM)
    k = k.reshape(B, M, MEM_HEADS, MEM_HEAD_DIM)
    v = v.reshape(B, M, MEM_HEADS, MEM_HEAD_DIM)
    s = jnp.einsum('bthd,bmhd->bhtm', q, k).astype(jnp.float32) * (MEM_HEAD_DIM ** -0.5)
    p = jax.nn.softmax(s, axis=-1).astype(v.dtype)
    return jnp.einsum('bhtm,bmhd->bthd', p, v).reshape(B, T, MEM_WIDTH)


def mla_mix(x, w_in, q_norm_g, kv_norm_g, w_uq, w_ukv, pos):
    B, T, _ = x.shape
    H = TOK_HEADS
    c_q, c_kv, k_r, q_mem = jnp.split(
        x @ w_in, [MLA_Q_RANK, MLA_Q_RANK + MLA_KV_RANK, MLA_Q_RANK + MLA_KV_RANK + MLA_ROPE], axis=-1)
    q = (rms_norm(c_q, q_norm_g) @ w_uq).reshape(B, T, H, MLA_NOPE + MLA_ROPE).transpose(0, 2, 1, 3)
    q = jnp.concatenate([q[..., :MLA_NOPE], rope(q[..., MLA_NOPE:], pos)], -1)
    kv = (rms_norm(c_kv, kv_norm_g) @ w_ukv).reshape(B, T, H, MLA_NOPE + MLA_V).transpose(0, 2, 1, 3)
    k_nope, v = kv[..., :MLA_NOPE], kv[..., MLA_NOPE:]
    k_r = rope(k_r, pos)
    k = jnp.concatenate([k_nope, jnp.broadcast_to(k_r[:, None], (B, H, T, MLA_ROPE))], -1)
    scale = (MLA_NOPE + MLA_ROPE) ** -0.5
    key_pos = jnp.arange(T)

    def block(c):
        t0 = c * QBLOCK
        tq = t0 + jnp.arange(QBLOCK)
        qb = lax.dynamic_slice_in_dim(q, t0, QBLOCK, axis=2)
        s = jnp.einsum('bhqd,bhkd->bhqk', qb, k) * scale
        p = masked_softmax(s, key_pos[None, :] <= tq[:, None]).astype(v.dtype)
        return jnp.einsum('bhqk,bhkd->bqhd', p, v)

    o = lax.map(block, jnp.arange(T // QBLOCK))
    o = o.transpose(1, 0, 2, 3, 4).reshape(B, T, TOK_WIDTH)
    return o, q_mem


def nsa_shared_kv(h, w_kv, cmp_pos, cmp_w1, cmp_b1, cmp_w2):
    B, T, _ = h.shape
    G = NSA_GROUPS
    kv = (h @ w_kv).reshape(B, T, NSA_BRANCHES, 2, G, HEAD_DIM).transpose(2, 3, 0, 4, 1, 5)
    n_cmp = (T - CMP_LEN) // CMP_STRIDE + 1
    idx = np.arange(n_cmp)[:, None] * CMP_STRIDE + np.arange(CMP_LEN)[None, :]

    def compress(z, j):
        blocks = z[:, :, idx] + cmp_pos[j]
        flat = blocks.reshape(B, G, n_cmp, CMP_LEN * HEAD_DIM)
        return jax.nn.gelu(flat @ cmp_w1[j] + cmp_b1[j]) @ cmp_w2[j]

    kc = compress(kv[0, 0], 0)
    vc = compress(kv[0, 1], 1)
    n_sel = T // SEL_LEN
    ks = kv[1, 0].reshape(B, G, n_sel, SEL_LEN, HEAD_DIM)
    vs = kv[1, 1].reshape(B, G, n_sel, SEL_LEN, HEAD_DIM)
    pad = ((0, 0), (0, 0), (WINDOW, 0), (0, 0))
    kw_pad = jnp.pad(kv[2, 0], pad)
    vw_pad = jnp.pad(kv[2, 1], pad)
    return kc, vc, ks, vs, kw_pad, vw_pad


def nsa_mix(x, w_in, kc, vc, ks, vs, kw_pad, vw_pad, slopes):
    B, T, _ = x.shape
    G, HPG = NSA_GROUPS, NSA_HPG
    q, gates, q_mem = jnp.split(x @ w_in, [TOK_WIDTH, TOK_WIDTH + TOK_HEADS * NSA_BRANCHES], axis=-1)
    q = q.reshape(B, T, G, HPG, HEAD_DIM).transpose(0, 2, 3, 1, 4)
    gates = jax.nn.sigmoid(gates.reshape(B, T, G, HPG, NSA_BRANCHES).transpose(0, 2, 3, 1, 4))
    m = slopes.reshape(G, HPG)[None, :, :, None, None]
    n_cmp = kc.shape[2]
    n_sel = ks.shape[2]
    topk = min(SEL_TOPK, n_sel)
    cmp_start = np.arange(n_cmp) * CMP_STRIDE
    cmp_end = jnp.asarray(cmp_start + CMP_LEN - 1)
    cmp_center = jnp.asarray(cmp_start + (CMP_LEN - 1) * 0.5, dtype=jnp.float32)
    sel_start = np.arange(n_sel) * SEL_LEN
    overlap = np.clip(np.minimum(cmp_start[:, None] + CMP_LEN, sel_start[None, :] + SEL_LEN)
                      - np.maximum(cmp_start[:, None], sel_start[None, :]), 0, None)
    agg = jnp.asarray(overlap / CMP_LEN, dtype=jnp.float32)
    blk = jnp.arange(n_sel)
    scale = HEAD_DIM ** -0.5
    gather = jax.vmap(jax.vmap(lambda blocks, i: blocks[i]))

    def block(c):
        t0 = c * QBLOCK
        tq = t0 + jnp.arange(QBLOCK)
        qb = lax.dynamic_slice_in_dim(q, t0, QBLOCK, axis=3)
        gb = lax.dynamic_slice_in_dim(gates, t0, QBLOCK, axis=3)
        dist_c = tq[:, None].astype(jnp.float32) - cmp_center[None, :]
        s_c = jnp.einsum('bghqd,bgnd->bghqn', qb, kc) * scale - m * dist_c
        p_c = masked_softmax(s_c, cmp_end[None, :] <= tq[:, None])
        o_c = jnp.einsum('bghqn,bgnd->bghqd', p_c.astype(vc.dtype), vc)
        imp = jnp.einsum('bghqn,nj->bgqj', p_c, agg)
        cur = tq // SEL_LEN
        forced = (blk[None] == 0) | (blk[None] == cur[:, None]) | (blk[None] == cur[:, None] - 1)
        imp = jnp.where(forced, imp + FORCE_BONUS, imp)
        imp = jnp.where(blk[None] * SEL_LEN <= tq[:, None], imp, NEG)
        _, sel = lax.top_k(imp, topk)
        k_sel = gather(ks, sel)
        v_sel = gather(vs, sel)
        pos_s = sel[..., None] * SEL_LEN + jnp.arange(SEL_LEN)
        dist_s = (tq[None, None, :, None, None] - pos_s)[:, :, None]
        s_s = jnp.einsum('bghqd,bgqkld->bghqkl', qb, k_sel) * scale - m[..., None] * dist_s
        p_s = masked_softmax(s_s, dist_s >= 0, axis=(-2, -1))
        o_s = jnp.einsum('bghqkl,bgqkld->bghqd', p_s.astype(v_sel.dtype), v_sel)
        kw = lax.dynamic_slice_in_dim(kw_pad, t0, WINDOW + QBLOCK, axis=2)
        vw = lax.dynamic_slice_in_dim(vw_pad, t0, WINDOW + QBLOCK, axis=2)
        pos_w = t0 - WINDOW + jnp.arange(WINDOW + QBLOCK)
        dist_w = tq[:, None] - pos_w[None, :]
        mask_w = (dist_w >= 0) & (dist_w < WINDOW) & (pos_w[None, :] >= 0)
        s_w = jnp.einsum('bghqd,bgkd->bghqk', qb, kw) * scale - m * dist_w
        p_w = masked_softmax(s_w, mask_w)
        o_w = jnp.einsum('bghqk,bgkd->bghqd', p_w.astype(vw.dtype), vw)
        o = gb[..., 0:1] * o_c + gb[..., 1:2] * o_s + gb[..., 2:3] * o_w
        return o.transpose(0, 3, 1, 2, 4).reshape(B, QBLOCK, TOK_WIDTH)

    o = lax.map(block, jnp.arange(T // QBLOCK))
    o = o.transpose(1, 0, 2, 3).reshape(B, T, TOK_WIDTH)
    return o, q_mem


def setup_inputs(seed: int = 0) -> dict:
    key = jax.random.key(seed)
    ks = jax.random.split(key, 20)

    def nrm(k, shape, fan_in, gain=1.0):
        return jax.random.normal(k, shape, jnp.float32) * (gain * fan_in ** -0.5)

    def gain(k, shape):
        return 1.0 + 0.02 * jax.random.normal(k, shape, jnp.float32)

    return {
        "x": jax.random.normal(ks[0], (BATCH, SEQ, D_MODEL), jnp.float32),
        "mem": jax.random.normal(ks[1], (BATCH, N_MEM, D_MODEL), jnp.float32),
        "ln_g": gain(ks[2], (DEPTH, 3, D_MODEL)),
        "ln_b": 0.02 * jax.random.normal(ks[3], (DEPTH, 3, D_MODEL), jnp.float32),
        "ffn_w_gu": nrm(ks[4], (DEPTH, 2, D_MODEL, 2 * D_FF), D_MODEL),
        "ffn_w_down": nrm(ks[5], (DEPTH, 2, D_FF, D_MODEL), D_FF, DN_BETA),
        "w_mem_kv": nrm(ks[6], (DEPTH, D_MODEL, 2 * MEM_WIDTH), D_MODEL),
        "w_out": nrm(ks[7], (DEPTH, MIX_WIDTH, D_MODEL), MIX_WIDTH, DN_BETA),
        "mla_w_in": nrm(ks[8], (N_A_LAYERS, D_MODEL, MLA_IN), D_MODEL),
        "mla_q_norm_g": gain(ks[9], (N_A_LAYERS, MLA_Q_RANK)),
        "mla_kv_norm_g": gain(ks[10], (N_A_LAYERS, MLA_KV_RANK)),
        "mla_w_uq": nrm(ks[11], (N_A_LAYERS, MLA_Q_RANK, TOK_HEADS * (MLA_NOPE + MLA_ROPE)), MLA_Q_RANK),
        "mla_w_ukv": nrm(ks[12], (N_A_LAYERS, MLA_KV_RANK, TOK_HEADS * (MLA_NOPE + MLA_V)), MLA_KV_RANK),
        "nsa_w_in": nrm(ks[13], (N_B_LAYERS, D_MODEL, NSA_IN), D_MODEL),
        "nsa_w_kv": nrm(ks[14], (D_MODEL, NSA_KV), D_MODEL),
        "cmp_pos": 0.1 * jax.random.normal(ks[15], (2, CMP_LEN, HEAD_DIM), jnp.float32),
        "cmp_w1": nrm(ks[16], (2, CMP_LEN * HEAD_DIM, CMP_HIDDEN), CMP_LEN * HEAD_DIM),
        "cmp_b1": 0.02 * jax.random.normal(ks[17], (2, CMP_HIDDEN), jnp.float32),
        "cmp_w2": nrm(ks[18], (2, CMP_HIDDEN, HEAD_DIM), CMP_HIDDEN),
    }


def reference(x, mem, ln_g, ln_b, ffn_w_gu, ffn_w_down, w_mem_kv, w_out,
              mla_w_in, mla_q_norm_g, mla_kv_norm_g, mla_w_uq, mla_w_ukv,
              nsa_w_in, nsa_w_kv, cmp_pos, cmp_w1, cmp_b1, cmp_w2):
    T = x.shape[1]
    pos = jnp.arange(T)
    slopes = jnp.asarray(alibi_slopes(TOK_HEADS))
    shared = None
    for layer in range(DEPTH):
        x = layer_norm(DN_ALPHA * x + 0.5 * swiglu(x, ffn_w_gu[layer, 0], ffn_w_down[layer, 0]),
                       ln_g[layer, 0], ln_b[layer, 0])
        if layer < N_A_LAYERS:
            o_tok, q_mem = mla_mix(x, mla_w_in[layer], mla_q_norm_g[layer], mla_kv_norm_g[layer],
                                   mla_w_uq[layer], mla_w_ukv[layer], pos)
        else:
            b = layer - N_A_LAYERS
            kc, vc, ks_, vs_, kw_pad, vw_pad = shared
            o_tok, q_mem = nsa_mix(x, nsa_w_in[b], kc, vc, ks_, vs_, kw_pad, vw_pad, slopes)
        o_mem = memory_attention(q_mem, mem, w_mem_kv[layer])
        mix = jnp.concatenate([o_tok, o_mem], axis=-1) @ w_out[layer]
        x = layer_norm(DN_ALPHA * x + mix, ln_g[layer, 1], ln_b[layer, 1])
        x = layer_norm(DN_ALPHA * x + 0.5 * swiglu(x, ffn_w_gu[layer, 1], ffn_w_down[layer, 1]),
                       ln_g[layer, 2], ln_b[layer, 2])
        if layer == N_A_LAYERS - 1:
            shared = nsa_shared_kv(x, nsa_w_kv, cmp_pos, cmp_w1, cmp_b1, cmp_w2)
    return x
```

```python
import numpy as np
import ml_dtypes
import concourse.bass as bass
import concourse.mybir as mybir
from concourse.bass_utils import run_bass_kernel_spmd
from contextlib import ExitStack

F32 = mybir.dt.float32
BF16 = mybir.dt.bfloat16
F16 = mybir.dt.float16
AF = mybir.ActivationFunctionType
ALU = mybir.AluOpType

D = 1024
T = 8192
NB = 2
NCORE = 8
TLOC = 2048
NG = TLOC // 512
DFF = 2816
NF = DFF // 128
DEPTH = 4
DN_ALPHA = (2 * DEPTH) ** 0.25
LN_EPS = 1e-5
RMS_EPS = 1e-6

ZIG = [0, 7, 8, 15]


def global_block(j, i):
    s, r = divmod(i, 4)
    off = j if r in (0, 2) else -j
    return 16 * s + ZIG[r] + off


def core_token_index(j):
    return np.concatenate([np.arange(128) + 128 * global_block(j, i) for i in range(16)])


class Buf:
    __slots__ = ("name", "w", "r")

    def __init__(self, name):
        self.name = name
        self.w = None
        self.r = {}


class Sched:
    ENGS = ("pe", "act", "dve", "pool", "sp")
    NDMA = 24

    def __init__(self, nc, stack):
        self.nc = nc
        self.sem = {}
        for e in ("pe", "act", "dve", "pool"):
            self.sem[e] = stack.enter_context(nc.semaphore("s_" + e))
        for k in range(self.NDMA):
            self.sem[("d", k)] = stack.enter_context(nc.semaphore("s_d%d" % k))
        self.cnt = {k: 0 for k in self.sem}
        self.ops = {e: [] for e in self.ENGS}
        self.seen = {e: {} for e in self.ENGS}
        self.dma_rr = 0
        self.nops = 0

    def _collect(self, eng, reads, writes):
        waits = {}

        def add(ev):
            if ev is None:
                return
            k, v = ev
            if k == eng == "pe":
                return
            if waits.get(k, 0) < v:
                waits[k] = v

        for b in reads:
            add(b.w)
        for b in writes:
            add(b.w)
            for k, v in b.r.items():
                add((k, v))
        out = []
        seen = self.seen[eng]
        for k, v in waits.items():
            if seen.get(k, 0) < v:
                seen[k] = v
                out.append((k, v))
        return out

    def _commit(self, ev, reads, writes):
        k, v = ev
        for b in reads:
            if b.r.get(k, 0) < v:
                b.r[k] = v
        for b in writes:
            b.w = ev
            b.r = {}

    def op(self, eng, fn, reads=(), writes=()):
        waits = self._collect(eng, reads, writes)
        self.cnt[eng] += 1
        ev = (eng, self.cnt[eng])
        self.ops[eng].append((waits, fn, (eng, 1)))
        self._commit(ev, reads, writes)
        self.nops += 1
        return ev

    def dma(self, eng, fn, reads=(), writes=()):
        k = ("d", self.dma_rr)
        self.dma_rr = (self.dma_rr + 1) % self.NDMA
        waits = self._collect(eng, reads, writes)
        prev = self.cnt[k]
        if prev and self.seen[eng].get(k, 0) < prev:
            self.seen[eng][k] = prev
            waits.append((k, prev))
        self.cnt[k] += 16
        ev = (k, self.cnt[k])
        self.ops[eng].append((waits, fn, (k, 16)))
        self._commit(ev, reads, writes)
        self.nops += 1
        return ev

    def I(self, eng, method, reads=(), writes=(), **kw):
        return self.op(eng, lambda e: getattr(e, method)(**kw), reads, writes)

    def DMA(self, eng, reads=(), writes=(), **kw):
        return self.dma(eng, lambda e: e.dma_start(**kw), reads, writes)

    def wait_all(self, eng):
        waits = []
        for k, v in self.cnt.items():
            if v and k != eng and self.seen[eng].get(k, 0) < v:
                self.seen[eng][k] = v
                waits.append((k, v))
        if waits:
            self.ops[eng].append((waits, None, None))

    def barrier(self):
        for e in self.ENGS:
            self.wait_all(e)

    def emit(self, block):
        sem = self.sem

        def run(engine, lst):
            for waits, fn, inc in lst:
                for k, v in waits:
                    engine.wait_ge(sem[k], v)
                if fn is not None:
                    ins = fn(engine)
                    ins.then_inc(sem[inc[0]], inc[1])

        ops = self.ops

        @block.tensor
        def _(e):
            run(e, ops["pe"])

        @block.scalar
        def _(e):
            run(e, ops["act"])

        @block.vector
        def _(e):
            run(e, ops["dve"])

        @block.gpsimd
        def _(e):
            run(e, ops["pool"])

        @block.sync
        def _(e):
            run(e, ops["sp"])


class Arena:
    def __init__(self, base_bf16, nbytes):
        self.base = base_bf16
        self.nbytes = nbytes
        self.top = 0
        self.peak = 0

    def alloc(self, nelem, dtype):
        sz = 4 if dtype == F32 else 2
        nb = (nelem * sz + 63) // 64 * 64
        off = self.top
        self.top += nb
        self.peak = max(self.peak, self.top)
        assert self.top <= self.nbytes, ("arena overflow", self.top, self.nbytes)
        v = self.base[:, off // 2:(off + nelem * sz) // 2]
        if dtype != BF16:
            v = v.bitcast(dtype)
        return v

    def mark(self):
        return self.top

    def release(self, m):
        self.top = m


class Prog:
    def __init__(self, nc, stack, arena_bytes=207 * 1024):
        self.nc = nc
        self.S = Sched(nc, stack)
        arena_t = stack.enter_context(nc.sbuf_tensor("arena", [128, arena_bytes // 2], BF16))
        self.A = Arena(arena_t[:, :], arena_bytes)
        self.psum = []
        self.pbuf = []
        for i in range(8):
            p = stack.enter_context(nc.psum_tensor("ps%d" % i, [128, 512], F32))
            self.psum.append(p)
            self.pbuf.append(Buf("ps%d" % i))
        self.dram = {}

    def ps(self, i):
        return self.psum[i][:, :]

    def din(self, name, shape, dtype=F32):
        t = self.nc.dram_tensor(name, list(shape), dtype, kind="ExternalInput").ap()
        self.dram[name] = t
        return t

    def dscr(self, name, shape, dtype=F32):
        t = self.nc.dram_tensor(name, list(shape), dtype, kind="Internal").ap()
        self.dram[name] = t
        return t

    def dout(self, name, shape, dtype=F32):
        t = self.nc.dram_tensor(name, list(shape), dtype, kind="ExternalOutput").ap()
        self.dram[name] = t
        return t

    def alloc_state(self, lnp_dram):
        A, S = self.A, self.S
        self.x32 = A.alloc(8 * TLOC, F32).rearrange("p (k t) -> p k t", k=8)
        self.xT = A.alloc(8 * TLOC, BF16).rearrange("p (k t) -> p k t", k=8)
        self.x32_b = [[Buf("x32_%d_%d" % (g, c)) for c in range(8)] for g in range(NG)]
        self.xT_b = [[Buf("xT_%d_%d" % (g, c)) for c in range(8)] for g in range(NG)]
        self.ones32 = A.alloc(128, F32)
        self.ones_b = Buf("ones")
        S.op("dve", lambda e: e.memset(self.ones32, 1.0 / 1024.0), writes=[self.ones_b])
        self.lnp = A.alloc(2 * 12 * 8, F32).rearrange("p (a l k) -> p a l k", a=2, l=12)
        self.lnp_b = Buf("lnp")
        S.dma("sp", lambda e: e.dma_start(out=self.lnp, in_=lnp_dram.rearrange("p (a l k) -> p a l k", a=2, l=12)),
              writes=[self.lnp_b])

    def load_xT(self, xT_dram):
        S = self.S
        for k in range(8):
            for g in range(NG):
                dst = self.x32[:, k, g * 512:(g + 1) * 512]
                src = xT_dram[k, :, g * 512:(g + 1) * 512]
                S.dma("sp", lambda e, dst=dst, src=src: e.dma_start(out=dst, in_=src), writes=[self.x32_b[g][k]])
                xt = self.xT[:, k, g * 512:(g + 1) * 512]
                S.op("act", lambda e, dst=dst, xt=xt: e.activation(out=xt, in_=dst, func=AF.Copy),
                     reads=[self.x32_b[g][k]], writes=[self.xT_b[g][k]])

    def store_xT(self, out_dram):
        S = self.S
        for k in range(8):
            for g in range(NG):
                src = self.x32[:, k, g * 512:(g + 1) * 512]
                dst = out_dram[k, :, g * 512:(g + 1) * 512]
                S.dma("sp", lambda e, dst=dst, src=src: e.dma_start(out=dst, in_=src), reads=[self.x32_b[g][k]])

    def alloc_ln_tmp(self):
        A = self.A
        d = {}
        d["sq"] = [A.alloc(512, F32) for _ in range(2)]
        d["sq_b"] = [Buf("lnsq0"), Buf("lnsq1")]
        d["mean"] = A.alloc(512, F32)
        d["rstd"] = A.alloc(512, F32)
        d["mr_b"] = Buf("lnmr")
        d["t"] = [A.alloc(512, F32) for _ in range(2)]
        d["t_b"] = [Buf("lnt0"), Buf("lnt1")]
        d["cnt"] = 0
        return d

    def resid_chunk(self, g, c, ybank, tmp):
        S = self.S
        xs = self.x32[:, c, g * 512:(g + 1) * 512]
        S.op("dve", lambda e: e.scalar_tensor_tensor(out=xs, in0=xs, scalar=DN_ALPHA, in1=self.ps(ybank),
                                                     op0=ALU.mult, op1=ALU.add),
             reads=[self.pbuf[ybank], self.x32_b[g][c]], writes=[self.x32_b[g][c]])
        i = tmp["cnt"] % 2
        tmp["cnt"] += 1
        sq, sq_b = tmp["sq"][i], tmp["sq_b"][i]
        S.op("act", lambda e: e.activation(out=sq, in_=xs, func=AF.Square), reads=[self.x32_b[g][c]], writes=[sq_b])

        def stats(sbank):
            S.op("pe", lambda e: e.matmul(self.ps(sbank), lhsT=self.ones32, rhs=xs, start=(c == 0), stop=(c == 7)),
                 reads=[self.ones_b, self.x32_b[g][c]], writes=[self.pbuf[sbank]])
            S.op("pe", lambda e: e.matmul(self.ps(sbank + 1), lhsT=self.ones32, rhs=sq, start=(c == 0), stop=(c == 7)),
                 reads=[self.ones_b, sq_b], writes=[self.pbuf[sbank + 1]])
        return stats

    def ln_finalize(self, g, sbank, ln_idx, tmp):
        for step in self.ln_steps(g, sbank, ln_idx, tmp):
            step()

    def ln_steps(self, g, sbank, ln_idx, tmp):
        yield lambda: self._ln_stats(g, sbank, tmp)
        for c in range(8):
            yield lambda c=c: self._ln_chunk(g, c, ln_idx, tmp)

    def _ln_stats(self, g, sbank, tmp):
        S = self.S
        mean, rstd, mr_b = tmp["mean"], tmp["rstd"], tmp["mr_b"]
        S.op("act", lambda e: e.activation(out=mean, in_=self.ps(sbank), func=AF.Copy),
             reads=[self.pbuf[sbank]], writes=[mr_b])
        S.op("dve", lambda e: e.tensor_tensor(out=rstd, in0=mean, in1=mean, op=ALU.mult), reads=[mr_b], writes=[mr_b])
        S.op("dve", lambda e: e.tensor_tensor(out=rstd, in0=self.ps(sbank + 1), in1=rstd, op=ALU.subtract),
             reads=[mr_b, self.pbuf[sbank + 1]], writes=[mr_b])
        S.op("dve", lambda e: e.tensor_scalar(out=rstd, in0=rstd, scalar1=LN_EPS, scalar2=None, op0=ALU.add),
             reads=[mr_b], writes=[mr_b])
        S.op("act", lambda e: e.activation(out=rstd, in_=rstd, func=AF.Sqrt), reads=[mr_b], writes=[mr_b])
        S.op("dve", lambda e: e.reciprocal(out=rstd, in_=rstd), reads=[mr_b], writes=[mr_b])

    def _ln_chunk(self, g, c, ln_idx, tmp):
        S = self.S
        mean, rstd, mr_b = tmp["mean"], tmp["rstd"], tmp["mr_b"]
        if True:
            xs = self.x32[:, c, g * 512:(g + 1) * 512]
            xt = self.xT[:, c, g * 512:(g + 1) * 512]
            i = c % 2
            t, t_b = tmp["t"][i], tmp["t_b"][i]
            S.op("dve", lambda e, t=t, xs=xs: e.tensor_tensor(out=t, in0=xs, in1=mean, op=ALU.subtract),
                 reads=[self.x32_b[g][c], mr_b], writes=[t_b])
            S.op("dve", lambda e, t=t: e.tensor_tensor(out=t, in0=t, in1=rstd, op=ALU.mult), reads=[t_b, mr_b], writes=[t_b])
            gs = self.lnp[:, 0, ln_idx, c:c + 1]
            bs = self.lnp[:, 1, ln_idx, c:c + 1]
            S.op("dve", lambda e, t=t, xs=xs, gs=gs, bs=bs: e.tensor_scalar(out=xs, in0=t, scalar1=gs, scalar2=bs,
                                                                            op0=ALU.mult, op1=ALU.add),
                 reads=[t_b, self.lnp_b], writes=[self.x32_b[g][c]])
            S.op("act", lambda e, xs=xs, xt=xt: e.activation(out=xt, in_=xs, func=AF.Copy),
                 reads=[self.x32_b[g][c]], writes=[self.xT_b[g][c]])

    def ffn(self, wgu_dram, wd_dram, ln_idx):
        S, A = self.S, self.A
        m = A.mark()
        NSUB = 2
        tmp = self.alloc_ln_tmp()
        hT = A.alloc(NF * NSUB * 512, BF16).rearrange("p (j t) -> p j t", j=NF)
        hT_b = [[Buf("hT%d_%d" % (j, s)) for s in range(NSUB)] for j in range(NF)]
        NW = 3
        wgu = [A.alloc(8 * 256, BF16).rearrange("p (k c) -> p k c", k=8) for _ in range(NW)]
        wgu_b = [Buf("wgu%d" % i) for i in range(NW)]
        wd = [A.alloc(NF * 128, BF16).rearrange("p (j n) -> p j n", j=NF) for _ in range(NW)]
        wd_b = [Buf("wd%d" % i) for i in range(NW)]
        sg = [A.alloc(512, F32) for _ in range(2)]
        sg_b = [Buf("sg0"), Buf("sg1")]
        cnt = 0
        wcnt = 0
        dcnt = 0
        ln_pending = []
        for p in range(TLOC // (NSUB * 512)):
            for j in range(NF):
                if j >= 1 and ln_pending:
                    ln_pending.pop(0)()
                w = wcnt % NW
                wcnt += 1
                if not (FFN_EXP == "skipdma" and p == 1):
                    S.dma("pool", lambda e, w=w, j=j: e.dma_start(
                        out=wgu[w], in_=wgu_dram[j].rearrange("p (k c) -> p k c", k=8), max_dma_last_dim=4096),
                        writes=[wgu_b[w]])
                for s in range(NSUB):
                    g = p * NSUB + s
                    pg, pu = (cnt % 2) * 2, (cnt % 2) * 2 + 1
                    for half, pb in ((0, pg), (1, pu)):
                        for k in range(8):
                            S.op("pe", lambda e, w=w, k=k, half=half, pb=pb, g=g: e.matmul(
                                self.ps(pb), lhsT=wgu[w][:, k, half * 128:(half + 1) * 128],
                                rhs=self.xT[:, k, g * 512:(g + 1) * 512], start=(k == 0), stop=(k == 7)),
                                reads=[wgu_b[w], self.xT_b[g][k]], writes=[self.pbuf[pb]])
                    si = cnt % 2
                    S.op("act", lambda e, si=si, pg=pg: e.activation(out=sg[si], in_=self.ps(pg), func=AF.Silu),
                         reads=[self.pbuf[pg]], writes=[sg_b[si]])
                    S.op("dve", lambda e, si=si, pu=pu, j=j, s=s: e.scalar_tensor_tensor(
                        out=hT[:, j, s * 512:(s + 1) * 512], in0=sg[si], scalar=0.5, in1=self.ps(pu),
                        op0=ALU.mult, op1=ALU.mult),
                        reads=[sg_b[si], self.pbuf[pu]], writes=[hT_b[j][s]])
                    cnt += 1
            pending = None
            for c in range(8):
                w = dcnt % NW
                dcnt += 1
                if not (FFN_EXP == "skipdma" and p == 1):
                    S.dma("pool", lambda e, w=w, c=c: e.dma_start(
                        out=wd[w], in_=wd_dram[c].rearrange("p (j n) -> p j n", j=NF), max_dma_last_dim=4096),
                        writes=[wd_b[w]])
                for s in range(NSUB):
                    g = p * NSUB + s
                    yb = (c * NSUB + s) % 2
                    for j in range(NF):
                        S.op("pe", lambda e, w=w, j=j, s=s, yb=yb: e.matmul(
                            self.ps(yb), lhsT=wd[w][:, j, :], rhs=hT[:, j, s * 512:(s + 1) * 512],
                            start=(j == 0), stop=(j == NF - 1)),
                            reads=[wd_b[w], hT_b[j][s]], writes=[self.pbuf[yb]])
                    if pending is not None:
                        pending[0](pending[1])
                    st = self.resid_chunk(g, c, yb, tmp)
                    pending = (st, 4 + 2 * s)
            pending[0](pending[1])
            for s in range(NSUB):
                ln_pending.extend(self.ln_steps(p * NSUB + s, 4 + 2 * s, ln_idx, tmp))
        for step in ln_pending:
            step()
        S.barrier()
        A.release(m)


def prep_ffn_weights(w_gu, w_down):
    a = w_gu.reshape(8, 128, 2, NF, 128).transpose(3, 1, 0, 2, 4).reshape(NF, 128, 8 * 256)
    b = w_down.reshape(NF, 128, 8, 128).transpose(2, 1, 0, 3).reshape(8, 128, NF * 128)
    return np.ascontiguousarray(a), np.ascontiguousarray(b)


def prep_ln(ln_g, ln_b):
    g = ln_g.reshape(12, 8, 128).transpose(2, 0, 1)
    b = ln_b.reshape(12, 8, 128).transpose(2, 0, 1)
    return np.ascontiguousarray(np.stack([g, b], axis=1).reshape(128, 2 * 12 * 8))


def shard_xT(x):
    outs = []
    for c in range(NCORE):
        b, j = divmod(c, 4)
        idx = core_token_index(j)
        xs = x[b][idx]
        outs.append(np.ascontiguousarray(xs.T.reshape(8, 128, TLOC)))
    return outs


def unshard_xT(outs):
    y = np.empty((NB, T, D), np.float32)
    for c in range(NCORE):
        b, j = divmod(c, 4)
        idx = core_token_index(j)
        y[b][idx] = outs[c].reshape(D, TLOC).T
    return y


MLA_SCALE = 96.0 ** -0.5
MEM_SCALE = 64.0 ** -0.5
ROPE_THETA = 10000.0
CAST = dict(max_dma_last_dim=4096)
DEBUG_STOP = None
_DBG = None
FFN_EXP = None
_TRACE = False


def rstd_from_psum(P, sbank, out, out_b, mult, eps):
    S = P.S
    S.I("dve", "tensor_scalar", [P.pbuf[sbank]], [out_b], out=out, in0=P.ps(sbank), scalar1=mult, scalar2=eps,
        op0=ALU.mult, op1=ALU.add)
    S.I("act", "activation", [out_b], [out_b], out=out, in_=out, func=AF.Sqrt)
    S.I("dve", "reciprocal", [out_b], [out_b], out=out, in_=out)


def mem_kv(P, dr, A):
    S = P.S
    memT = A.alloc(8 * 256, BF16).rearrange("p (k m) -> p k m", k=8)
    memT_b = Buf("memT")
    S.DMA("pool", [], [memT_b], out=memT, in_=dr["memT"].rearrange("k p m -> p k m"), **CAST)
    w_mkv = A.alloc(8 * 512, BF16).rearrange("p (k c) -> p k c", k=8)
    w_mkv_b = Buf("w_mkv")
    S.DMA("pool", [], [w_mkv_b], out=w_mkv, in_=dr["w_mkv"].rearrange("p (k c) -> p k c", k=8), **CAST)
    mkv = A.alloc(2048, BF16)
    mkv_b = Buf("mkv")
    S.I("dve", "memset", [], [mkv_b], ap=mkv, constant=0.0)
    kpad = mkv[:, 0:1024].rearrange("p (h m) -> p h m", h=4)
    vaug = mkv[:, 1024:2048].rearrange("p (h b c) -> p h b c", h=4, b=2)
    for pr in range(2):
        for k in range(8):
            S.I("pe", "matmul", [w_mkv_b, memT_b], [P.pbuf[0]], out=P.ps(0)[:, 0:256],
                lhsT=w_mkv[:, k, pr * 128:(pr + 1) * 128], rhs=memT[:, k, :], start=(k == 0), stop=(k == 7))
        S.I("act", "activation", [P.pbuf[0]], [mkv_b], out=kpad[0:64, 2 * pr, :], in_=P.ps(0)[0:64, 0:256], func=AF.Copy)
        S.I("act", "activation", [P.pbuf[0]], [mkv_b], out=kpad[64:128, 2 * pr + 1, :], in_=P.ps(0)[64:128, 0:256], func=AF.Copy)
    for kb in range(2):
        for k in range(8):
            S.I("pe", "matmul", [w_mkv_b, memT_b], [P.pbuf[1]], out=P.ps(1)[:, 0:256],
                lhsT=memT[:, k, kb * 128:(kb + 1) * 128], rhs=w_mkv[:, k, 256:512], start=(k == 0), stop=(k == 7))
        for h in range(4):
            c0 = 0 if h % 2 == 0 else 64
            S.I("act", "activation", [P.pbuf[1]], [mkv_b], out=vaug[:, h, kb, c0:c0 + 64],
                in_=P.ps(1)[:, h * 64:(h + 1) * 64], func=AF.Copy)
    for h in range(4):
        oc = 64 if h % 2 == 0 else 0
        S.I("dve", "memset", [], [mkv_b], ap=vaug[:, h, :, oc:oc + 1], constant=1.0)
    S.DMA("sp", [mkv_b], [], out=dr["memkv"], in_=mkv)


def mla_pre(P, dr):
    S, A = P.S, P.A
    m = A.mark()
    w_in = A.alloc(8 * 672, BF16).rearrange("p (k c) -> p k c", k=8)
    w_in_b = Buf("w_in")
    S.DMA("pool", [], [w_in_b], out=w_in, in_=dr["w_in"].rearrange("p (k c) -> p k c", k=8), **CAST)
    w_rot = A.alloc(8 * 128, BF16).rearrange("p (k c) -> p k c", k=8)
    w_rot_b = Buf("w_rot")
    S.I("dve", "memset", [], [w_rot_b], ap=w_rot, constant=0.0)
    S.I("dve", "tensor_scalar", [w_in_b], [w_rot_b], out=w_rot[:, :, 0:16], in0=w_in[:, :, 400:416], scalar1=-1.0,
        scalar2=None, op0=ALU.mult)
    S.I("dve", "tensor_copy", [w_in_b], [w_rot_b], out=w_rot[:, :, 16:32], in_=w_in[:, :, 384:400])
    w_uq = A.alloc(2 * 1152, BF16).rearrange("p (k c) -> p k c", k=2)
    w_uq_b = Buf("w_uq")
    S.DMA("pool", [], [w_uq_b], out=w_uq, in_=dr["w_uq"].rearrange("p (k c) -> p k c", k=2), **CAST)
    w_uqr = A.alloc(2 * 1152, BF16).rearrange("p (k c) -> p k c", k=2)
    w_uqr_b = Buf("w_uqr")
    S.I("dve", "memset", [], [w_uqr_b], ap=w_uqr, constant=0.0)
    for k in range(2):
        src = w_uq[:, k, :].rearrange("p (h c) -> p h c", h=12)
        dst = w_uqr[:, k, :].rearrange("p (h c) -> p h c", h=12)
        S.I("dve", "tensor_scalar", [w_uq_b], [w_uqr_b], out=dst[:, :, 64:80], in0=src[:, :, 80:96], scalar1=-1.0,
            scalar2=None, op0=ALU.mult)
        S.I("dve", "tensor_copy", [w_uq_b], [w_uqr_b], out=dst[:, :, 80:96], in_=src[:, :, 64:80])
    ng = A.alloc(3, F32)
    ng_b = Buf("ng")
    S.DMA("sp", [], [ng_b], out=ng, in_=dr["ng"])
    tabc = A.alloc(TLOC, BF16)
    tabs = A.alloc(TLOC, BF16)
    tab_b = Buf("tab")
    S.DMA("pool", [], [tab_b], out=tabc, in_=dr["tabc"], **CAST)
    S.DMA("pool", [], [tab_b], out=tabs, in_=dr["tabs"], **CAST)
    onesb = A.alloc(128, BF16)
    onesb_b = Buf("onesb")
    S.I("dve", "memset", [], [onesb_b], ap=onesb, constant=1.0)

    if DEBUG_STOP == "setup":
        S.barrier(); A.release(m); return
    mem_kv(P, dr, A)
    if DEBUG_STOP == "memkv":
        S.barrier(); A.release(m); return

    cqn = A.alloc(2 * TLOC, BF16).rearrange("p (k t) -> p k t", k=2)
    cqn_b = [Buf("cqn%d" % g) for g in range(NG)]
    cq_sb = A.alloc(2 * 512, F32).rearrange("p (k t) -> p k t", k=2)
    cq_sb_b = Buf("cq_sb")
    sqb = [A.alloc(512, BF16) for _ in range(2)]
    sqb_b = [Buf("sqb0"), Buf("sqb1")]
    rr = A.alloc(512, F32)
    rr_b = Buf("rr")
    ckv_sb = A.alloc(512, F32)
    ckv_sb_b = Buf("ckv_sb")
    latc = A.alloc(TLOC, BF16)
    latc_b = [Buf("latc%d" % g) for g in range(NG)]
    latr = A.alloc(TLOC, BF16)
    latr_b = [Buf("latr%d" % g) for g in range(NG)]
    qmT = A.alloc(2 * TLOC, BF16).rearrange("p (k t) -> p k t", k=2)
    qmT_b = [Buf("qmT%d" % g) for g in range(NG)]
    t1 = [A.alloc(512, F32) for _ in range(2)]
    t1_b = [Buf("t1a"), Buf("t1b")]
    t2 = [A.alloc(512, F32) for _ in range(2)]
    t2_b = [Buf("t2a"), Buf("t2b")]
    NQ = 3
    qst = [A.alloc(512, BF16) for _ in range(NQ)]
    qst_b = [Buf("qst%d" % i) for i in range(NQ)]

    def proj(g, bank, wt, wb, c0, c1):
        tk = slice(g * 512, (g + 1) * 512)
        for k in range(8):
            S.I("pe", "matmul", [wb, P.xT_b[g][k]], [P.pbuf[bank]], out=P.ps(bank), lhsT=wt[:, k, c0:c1],
                rhs=P.xT[:, k, tk], start=(k == 0), stop=(k == 7))

    for g in range(NG):
        tk = slice(g * 512, (g + 1) * 512)
        for k2 in range(2):
            proj(g, k2, w_in, w_in_b, k2 * 128, (k2 + 1) * 128)
            S.I("act", "activation", [P.pbuf[k2]], [sqb_b[k2]], out=sqb[k2], in_=P.ps(k2), func=AF.Square)
            S.I("act", "activation", [P.pbuf[k2]], [cq_sb_b], out=cq_sb[:, k2, :], in_=P.ps(k2), func=AF.Copy)
        for k2 in range(2):
            S.I("pe", "matmul", [onesb_b, sqb_b[k2]], [P.pbuf[2]], out=P.ps(2), lhsT=onesb, rhs=sqb[k2],
                start=(k2 == 0), stop=(k2 == 1))
        rstd_from_psum(P, 2, rr, rr_b, 1.0 / 256.0, RMS_EPS)
        for k2 in range(2):
            S.I("dve", "scalar_tensor_tensor", [cq_sb_b, ng_b, rr_b], [cqn_b[g]], out=cqn[:, k2, tk], in0=cq_sb[:, k2, :],
                scalar=ng[:, k2:k2 + 1], in1=rr, op0=ALU.mult, op1=ALU.mult)
        proj(g, 3, w_in, w_in_b, 256, 384)
        S.I("act", "activation", [P.pbuf[3]], [sqb_b[0]], out=sqb[0], in_=P.ps(3), func=AF.Square)
        S.I("act", "activation", [P.pbuf[3]], [ckv_sb_b], out=ckv_sb, in_=P.ps(3), func=AF.Copy)
        S.I("pe", "matmul", [onesb_b, sqb_b[0]], [P.pbuf[2]], out=P.ps(2), lhsT=onesb, rhs=sqb[0], start=True, stop=True)
        rstd_from_psum(P, 2, rr, rr_b, 1.0 / 128.0, RMS_EPS)
        S.I("dve", "scalar_tensor_tensor", [ckv_sb_b, ng_b, rr_b], [latc_b[g]], out=latc[:, tk], in0=ckv_sb,
            scalar=ng[:, 2:3], in1=rr, op0=ALU.mult, op1=ALU.mult)
        S.DMA("sp", [latc_b[g]], [], out=dr["lat_c"][:, tk], in_=latc[:, tk])
        proj(g, 4, w_in, w_in_b, 384, 512)
        proj(g, 5, w_rot, w_rot_b, 0, 128)
        S.I("dve", "tensor_tensor", [P.pbuf[4], tab_b], [t1_b[0]], out=t1[0][0:32, :], in0=P.ps(4)[0:32, :],
            in1=tabc[0:32, tk], op=ALU.mult)
        S.I("dve", "tensor_tensor", [P.pbuf[5], tab_b], [t2_b[0]], out=t2[0][0:32, :], in0=P.ps(5)[0:32, :],
            in1=tabs[0:32, tk], op=ALU.mult)
        S.I("dve", "tensor_tensor", [t1_b[0], t2_b[0]], [latr_b[g]], out=latr[0:32, tk], in0=t1[0][0:32, :],
            in1=t2[0][0:32, :], op=ALU.add)
        S.DMA("sp", [latr_b[g]], [], out=dr["lat_r"][:, tk], in_=latr[0:32, tk])
        for k2 in range(2):
            proj(g, 6 + k2, w_in, w_in_b, 416 + k2 * 128, 416 + (k2 + 1) * 128)
            S.I("act", "activation", [P.pbuf[6 + k2]], [qmT_b[g]], out=qmT[:, k2, tk], in_=P.ps(6 + k2), func=AF.Copy)
            S.DMA("sp", [qmT_b[g]], [], out=dr["qm"][k2, :, tk], in_=qmT[:, k2, tk])
    if DEBUG_STOP == "groups":
        S.barrier(); A.release(m); return
    qi = 0
    for g in range(NG):
        tk = slice(g * 512, (g + 1) * 512)
        for h in range(12):
            b0 = (qi % 2) * 2
            for bank, wt, wb in ((b0, w_uq, w_uq_b), (b0 + 1, w_uqr, w_uqr_b)):
                for k in range(2):
                    S.I("pe", "matmul", [wb, cqn_b[g]], [P.pbuf[bank]], out=P.ps(bank)[0:96, :],
                        lhsT=wt[:, k, h * 96:(h + 1) * 96], rhs=cqn[:, k, tk], start=(k == 0), stop=(k == 1))
            q = qst[qi % NQ]
            qb = qst_b[qi % NQ]
            i2 = qi % 2
            S.I("act", "activation", [P.pbuf[b0]], [qb], out=q[0:64, :], in_=P.ps(b0)[0:64, :], func=AF.Copy)
            if DEBUG_STOP != "qnorope":
                S.I("dve", "tensor_tensor", [P.pbuf[b0], tab_b], [t1_b[i2]], out=t1[i2][64:96, :], in0=P.ps(b0)[64:96, :],
                    in1=tabc[64:96, tk], op=ALU.mult)
                S.I("dve", "tensor_tensor", [P.pbuf[b0 + 1], tab_b], [t2_b[i2]], out=t2[i2][64:96, :],
                    in0=P.ps(b0 + 1)[64:96, :], in1=tabs[64:96, tk], op=ALU.mult)
                if False:
                    S.I("pool", "tensor_tensor", [t1_b[i2], t2_b[i2]], [qb], out=q[64:96, :], in0=t1[i2][64:96, :],
                        in1=t2[i2][64:96, :], op=ALU.add)
                else:
                    S.I("dve", "tensor_tensor", [t1_b[i2], t2_b[i2]], [qb], out=q[64:96, :], in0=t1[i2][64:96, :],
                        in1=t2[i2][64:96, :], op=ALU.add)
            if DEBUG_STOP != "qnodma":
                S.DMA("sp", [qb], [], out=dr["q_scr"][h, :, tk], in_=q[0:96, :])
            qi += 1
            if DEBUG_STOP == "q1":
                break
        if DEBUG_STOP == "q1":
            break
    S.barrier()
    A.release(m)


def resid_ln_from_oT(P, dr_w_out, ln_idx):
    S, A = P.S, P.A
    m = A.mark()
    w_out = A.alloc(8 * 1024, BF16).rearrange("p (k c) -> p k c", k=8)
    w_out_b = Buf("w_out")
    S.DMA("pool", [], [w_out_b], out=w_out, in_=dr_w_out.rearrange("p (k c) -> p k c", k=8), **CAST)
    tmp = P.alloc_ln_tmp()
    lnq = []
    for g in range(NG):
        tk = slice(g * 512, (g + 1) * 512)
        sb = 4 + 2 * (g % 2)
        pending = None
        for c in range(8):
            yb = c % 2
            for _ in range(2 if c == 0 else 1):
                if lnq:
                    lnq.pop(0)()
            for pr in range(8):
                S.I("pe", "matmul", [w_out_b, P.xT_b[g][pr]], [P.pbuf[yb]], out=P.ps(yb),
                    lhsT=w_out[:, pr, c * 128:(c + 1) * 128], rhs=P.xT[:, pr, tk], start=(pr == 0), stop=(pr == 7))
            if pending is not None:
                pending(sb)
            pending = P.resid_chunk(g, c, yb, tmp)
        pending(sb)
        for step in lnq:
            step()
        lnq = list(P.ln_steps(g, sb, ln_idx, tmp))
    for step in lnq:
        step()
    S.barrier()
    A.release(m)


DEFER = 6


def pend_tick(st):
    pend = st.setdefault("pending", [])
    for ent in list(pend):
        ent[0] -= 1
        if ent[0] <= 0:
            pend.remove(ent)
            ent[2]()


def pend_flush(st, bank=None, keep=0):
    pend = st.setdefault("pending", [])
    for ent in list(pend):
        if bank is None or ent[1] == bank:
            if bank is None and len(pend) <= keep:
                break
            pend.remove(ent)
            ent[2]()


def pend_add(st, ob, fn, maxpend):
    pend = st.setdefault("pending", [])
    while len(pend) >= maxpend:
        ent = pend.pop(0)
        ent[2]()
    pend.append([DEFER, ob, fn])


def attn_epilogue(P, ob, osb, osb_b, sel, sel_b, par, pair, g):
    S = P.S
    tk = slice(g * 512, (g + 1) * 512)
    dr_ = 64 if par == 0 else 0
    r0 = 0 if par == 0 else 64
    S.I("act", "activation", [P.pbuf[ob]], [osb_b], out=osb, in_=P.ps(ob), func=AF.Copy)
    S.I("dve", "reciprocal", [osb_b], [osb_b], out=osb[dr_:dr_ + 1, :], in_=osb[dr_:dr_ + 1, :])

    def rest():
        S.I("pe", "matmul", [sel_b, osb_b], [P.pbuf[7]], out=P.ps(7), lhsT=sel[:, par, :], rhs=osb, start=True, stop=True)
        S.I("dve", "tensor_tensor", [osb_b, P.pbuf[7]], [P.xT_b[g][pair]], out=P.xT[r0:r0 + 64, pair, tk],
            in0=osb[r0:r0 + 64, :], in1=P.ps(7)[r0:r0 + 64, :], op=ALU.mult)
    return rest


def attn_slots(P, slots, kt_fn, q_fn, v_fn, scale, mask_fn, PT, PT_b, epi_fn, st):
    S = P.S
    n = len(slots)
    info = [None] * n
    SB = st.get("Sbanks", [0, 1, 2])
    L = len(SB) - 1
    for i in range(n + L):
        pend_tick(st)
        if st.get("bg") is not None and i % 5 == 2:
            step = next(st["bg"], None)
            if step is None:
                st["bg"] = None
            else:
                step()
        if i < n:
            g, kb, first, last = slots[i]
            sb = SB[st["s"] % len(SB)]
            st["s"] += 1
            pi = st["p"] % len(PT)
            st["p"] += 1
            kt, kt_bufs = kt_fn(kb)
            q, q_bufs = q_fn(g)
            S.I("pe", "matmul", kt_bufs + q_bufs, [P.pbuf[sb]], out=P.ps(sb), lhsT=kt, rhs=q, start=True, stop=True)
            S.I("act", "activation", [P.pbuf[sb]], [PT_b[pi]], out=PT[pi], in_=P.ps(sb), func=AF.Exp, scale=scale)
            if mask_fn is not None:
                mask_fn(g, kb, PT[pi], PT_b[pi])
            info[i] = pi
        j = i - L
        if j >= 0:
            g, kb, first, last = slots[j]
            if first:
                st["ob"] = 3 + (st["o"] % 2)
                st["o"] += 1
                pend_flush(st, bank=st["ob"])
            ob = st["ob"]
            v, v_bufs = v_fn(kb)
            pi = info[j]
            S.I("pe", "matmul", v_bufs + [PT_b[pi]], [P.pbuf[ob]], out=P.ps(ob), lhsT=v, rhs=PT[pi], start=first, stop=last)
            if last:
                rest = epi_fn(g, ob)
                if rest is not None:
                    pend_add(st, ob, rest, st.get("maxpend", 1))


def mem_heads(P, dr, kpad, vaug, mkv_b, QT, QT_b, PT, PT_b, osb, osb_b, sel, sel_b, st):
    S = P.S
    for hm in range(4):
        par = hm % 2
        qi = hm % 2
        S.DMA("sp", [], [QT_b[qi]], out=QT[qi], in_=dr["qm"][hm // 2])
        slots = []
        for g in range(NG):
            for kb in range(2):
                slots.append((g, kb, kb == 0, kb == 1))

        def kt_fn(kb, hm=hm):
            return kpad[:, hm, kb * 128:(kb + 1) * 128], [mkv_b]

        def q_fn(g, qi=qi):
            return QT[qi][:, g * 512:(g + 1) * 512], [QT_b[qi]]

        def v_fn(kb, hm=hm):
            return vaug[:, hm, kb, :], [mkv_b]

        def epi_fn(g, ob, par=par, pair=6 + hm // 2):
            e = st["e"] % len(osb)
            st["e"] += 1
            return attn_epilogue(P, ob, osb[e], osb_b[e], sel, sel_b, par, pair, g)

        attn_slots(P, slots, kt_fn, q_fn, v_fn, MEM_SCALE, None, PT, PT_b, epi_fn, st)


def mla_attn(P, dr):
    S, A = P.S, P.A
    m = A.mark()
    ckv = A.alloc(T, BF16)
    ckv_b = [Buf("ckv%d" % i) for i in range(16)]
    KT2 = [A.alloc(T, BF16) for _ in range(2)]
    KT2_b = [[Buf("KT%d_%d" % (p_, i)) for i in range(16)] for p_ in range(2)]
    KTr_b = Buf("KTr")
    for i in range(16):
        ks = slice(i * 512, (i + 1) * 512)
        S.DMA("sp", [], [ckv_b[i]], out=ckv[:, ks], in_=dr["ckv_all"][:, ks])
    for p_ in range(2):
        for i in range(4):
            ks = slice(i * 2048, (i + 1) * 2048)
            S.DMA("sp", [], [KTr_b], out=KT2[p_][64:96, ks], in_=dr["kr_all"][:, ks])
    w_ukv = A.alloc(1536, BF16)
    w_ukv_b = Buf("w_ukv")
    S.DMA("pool", [], [w_ukv_b], out=w_ukv, in_=dr["w_ukv"], **CAST)
    V = [A.alloc(64 * 128, BF16).rearrange("p (b c) -> p b c", b=64) for _ in range(2)]
    V_b = [[Buf("V%d_%d" % (p_, i)) for i in range(8)] for p_ in range(2)]
    for p_ in range(2):
        S.I("pool", "memset", [], V_b[p_], ap=V[p_], constant=0.0)
        oc = 64 if p_ == 0 else 0
        S.I("pool", "memset", [], V_b[p_], ap=V[p_][:, :, oc:oc + 1], constant=1.0)
    mkv = A.alloc(2048, BF16)
    mkv_b = Buf("mkv")
    S.DMA("sp", [], [mkv_b], out=mkv, in_=dr["memkv"])
    kpad = mkv[:, 0:1024].rearrange("p (h m) -> p h m", h=4)
    vaug = mkv[:, 1024:2048].rearrange("p (h b c) -> p h b c", h=4, b=2)
    QT = [A.alloc(TLOC, BF16) for _ in range(2)]
    QT_b = [Buf("QT0"), Buf("QT1")]
    qrel = A.alloc(512, F16)
    krel = A.alloc(16, F32)
    sel = A.alloc(256, F32).rearrange("p (a c) -> p a c", a=2)
    cst_b = Buf("cst")
    S.DMA("sp", [], [cst_b], out=qrel, in_=dr["qrel"])
    S.DMA("sp", [], [cst_b], out=krel, in_=dr["krel"])
    S.DMA("sp", [], [cst_b], out=sel, in_=dr["sel"].rearrange("p (a c) -> p a c", a=2))
    PT = [A.alloc(512, BF16) for _ in range(4)]
    PT_b = [Buf("PT%d" % i) for i in range(4)]
    osb = [A.alloc(512, F32) for _ in range(4)]
    osb_b = [Buf("osb%d" % i) for i in range(4)]
    st = {"s": 0, "p": 0, "o": 0, "ob": 3, "e": 0, "prod": 0, "maxpend": 3, "Sbanks": [0, 1, 2, 5]}

    def mask_fn(g, kb, pt, pt_b):
        i = kb - 16 * g
        if i < 0:
            return
        S.I("dve", "scalar_tensor_tensor", [cst_b, pt_b], [pt_b], out=pt, in0=qrel, scalar=krel[:, i:i + 1], in1=pt,
            op0=ALU.is_ge, op1=ALU.mult)

    def prod_gen(h):
        par = h % 2
        Vh, Vh_b = V[par], V_b[par]
        KT, KT_b = KT2[par], KT2_b[par]
        c0 = 0 if par == 0 else 64
        for kc in range(16):
            def stepk(kc=kc):
                pb = 6
                ks = slice(kc * 512, (kc + 1) * 512)
                S.I("pe", "matmul", [w_ukv_b, ckv_b[kc]], [P.pbuf[pb]], out=P.ps(pb), lhsT=w_ukv[:, h * 128:(h + 1) * 128],
                    rhs=ckv[:, ks], start=True, stop=True)
                if kc % 2 == 0:
                    S.I("act", "activation", [P.pbuf[pb]], [KT_b[kc]], out=KT[0:64, ks], in_=P.ps(pb)[0:64, :], func=AF.Copy)
                else:
                    S.I("dve", "tensor_copy", [P.pbuf[pb]], [KT_b[kc]], out=KT[0:64, ks], in_=P.ps(pb)[0:64, :])
            yield stepk
        for kg in range(8):
            def stepv(kg=kg):
                pb = 6
                for i in range(8):
                    kb = kg * 8 + i
                    S.I("pe", "matmul", [w_ukv_b, ckv_b[kb // 4]], [P.pbuf[pb]], out=P.ps(pb)[:, i * 64:(i + 1) * 64],
                        lhsT=ckv[:, kb * 128:(kb + 1) * 128], rhs=w_ukv[:, h * 128 + 64:(h + 1) * 128], start=True, stop=True)
                src_ = P.ps(pb).rearrange("p (b c) -> p b c", b=8)
                dst = Vh[:, kg * 8:(kg + 1) * 8, c0:c0 + 64]
                if kg % 2 == 0:
                    S.I("dve", "tensor_copy", [P.pbuf[pb]], [Vh_b[kg]], out=dst, in_=src_)
                else:
                    S.I("act", "activation", [P.pbuf[pb]], [Vh_b[kg]], out=dst, in_=src_, func=AF.Copy)
            yield stepv

    for step in prod_gen(0):
        step()
    for h in range(12):
        par = h % 2
        Vh, Vh_b = V[par], V_b[par]
        KT, KT_b = KT2[par], KT2_b[par]
        st["bg"] = prod_gen(h + 1) if h + 1 < 12 else None
        qi = h % 2
        S.DMA("sp", [], [QT_b[qi]], out=QT[qi][0:96, :], in_=dr["q_scr"][h])
        slots = []
        for g in range(NG):
            nkb = 16 * g + 16
            for kb in range(nkb):
                slots.append((g, kb, kb == 0, kb == nkb - 1))

        def kt_fn(kb, KT=KT, KT_b=KT_b):
            return KT[0:96, kb * 128:(kb + 1) * 128], [KT_b[kb // 4], KTr_b]

        def q_fn(g, qi=qi):
            return QT[qi][0:96, g * 512:(g + 1) * 512], [QT_b[qi]]

        def v_fn(kb, Vh=Vh, Vh_b=Vh_b):
            return Vh[:, kb, :], [Vh_b[kb // 8]]

        def epi_fn(g, ob, par=par, pair=h // 2):
            e = st["e"] % len(osb)
            st["e"] += 1
            return attn_epilogue(P, ob, osb[e], osb_b[e], sel, cst_b, par, pair, g)

        attn_slots(P, slots, kt_fn, q_fn, v_fn, MLA_SCALE, mask_fn, PT, PT_b, epi_fn, st)
        if st["bg"] is not None:
            for step in st["bg"]:
                step()
            st["bg"] = None
    mem_heads(P, dr, kpad, vaug, mkv_b, QT, QT_b, PT, PT_b, osb, osb_b, sel, cst_b, st)
    pend_flush(st)
    S.barrier()
    A.release(m)


def kmajor(w):
    K_, N_ = w.shape
    return np.ascontiguousarray(w.reshape(K_ // 128, 128, N_).transpose(1, 0, 2).reshape(128, (K_ // 128) * N_))


def rope_tables(j):
    pos = core_token_index(j).astype(np.float32)
    freq = ROPE_THETA ** (-np.arange(16, dtype=np.float32) / 16)
    ang = pos[None, :] * freq[:, None]
    c = np.zeros((128, TLOC), np.float32)
    s = np.zeros((128, TLOC), np.float32)
    for base in (0, 64):
        c[base:base + 16] = np.cos(ang)
        c[base + 16:base + 32] = np.cos(ang)
        s[base:base + 16] = np.sin(ang)
        s[base + 16:base + 32] = np.sin(ang)
    return c, s


def attn_consts(j):
    qpos = core_token_index(j)[:512]
    qrel = np.broadcast_to(qpos[None, :].astype(np.float16), (128, 512)).copy()
    krel = (128.0 * np.arange(16)[None, :] + np.arange(128)[:, None]).astype(np.float32)
    sel = np.zeros((128, 2, 128), np.float32)
    sel[64, 0, :] = 1.0
    sel[0, 1, :] = 1.0
    return qrel, krel, sel.reshape(128, 256)


U8 = mybir.dt.uint8
NSA_SCALE = 0.125
NEG_BIG = -1.0e30
MASK_BIG = 3.0e38


def run_slots(P, slots, PT, PT_b, st):
    S = P.S
    n = len(slots)
    info = [None] * n
    SB = st.get("Sbanks", [0, 1, 2])
    L = len(SB) - 1
    for i in range(n + L):
        pend_tick(st)
        if i < n:
            s = slots[i]
            if s.get("pre") is not None:
                s["pre"]()
            sb = SB[st["s"] % len(SB)]
            st["s"] += 1
            if s.get("pt") is not None:
                pt, ptb = s["pt"], s["ptb"]
            else:
                pi = st["p"] % len(PT)
                st["p"] += 1
                pt, ptb = PT[pi], PT_b[pi]
            S.I("pe", "matmul", s["kb"] + s["qb"], [P.pbuf[sb]], out=P.ps(sb), lhsT=s["k"], rhs=s["q"], start=True, stop=True)
            S.I("act", "activation", [P.pbuf[sb]], [ptb], out=pt, in_=P.ps(sb), func=AF.Exp, scale=s["scale"])
            if s.get("post") is not None:
                s["post"](pt, ptb)
            info[i] = (pt, ptb)
        j = i - L
        if j >= 0:
            s = slots[j]
            pt, ptb = info[j]
            ob = s["ob"]
            if s["first"]:
                pend_flush(st, bank=ob)
            S.I("pe", "matmul", s["vb"] + [ptb], [P.pbuf[ob]], out=P.ps(ob)[0:96, :], lhsT=s["v"], rhs=pt,
                start=s["first"], stop=s["last"])
            if s["last"] and s.get("epi") is not None:
                rest = s["epi"]()
                if rest is not None:
                    pend_add(st, ob, rest, 3)


def nsa_kv_stage(P, dr):
    S, A = P.S, P.A
    m = A.mark()
    w_kv = A.alloc(8 * 768, BF16).rearrange("p (k c) -> p k c", k=8)
    w_kv_b = Buf("w_kv")
    S.DMA("pool", [], [w_kv_b], out=w_kv, in_=dr["w_kv"].rearrange("p (k c) -> p k c", k=8), **CAST)
    stg = [A.alloc(512, BF16) for _ in range(3)]
    stg_b = [Buf("stg%d" % i) for i in range(3)]
    n = 0
    for g in range(NG):
        tk = slice(g * 512, (g + 1) * 512)
        for c in range(6):
            bank = n % 4
            for k in range(8):
                S.I("pe", "matmul", [w_kv_b, P.xT_b[g][k]], [P.pbuf[bank]], out=P.ps(bank), lhsT=w_kv[:, k, c * 128:(c + 1) * 128],
                    rhs=P.xT[:, k, tk], start=(k == 0), stop=(k == 7))
            si = n % 3
            if n % 2 == 0:
                S.I("act", "activation", [P.pbuf[bank]], [stg_b[si]], out=stg[si], in_=P.ps(bank), func=AF.Copy)
            else:
                S.I("dve", "tensor_copy", [P.pbuf[bank]], [stg_b[si]], out=stg[si], in_=P.ps(bank))
            S.DMA("sp", [stg_b[si]], [], out=dr["kvT"][c * 128:(c + 1) * 128, tk], in_=stg[si])
            n += 1
    for t in range(16):
        g = t // 4
        for bi, br in enumerate((1, 2)):
            bank = 4 + (n % 2)
            for k in range(8):
                S.I("pe", "matmul", [w_kv_b, P.xT_b[g][k]], [P.pbuf[bank]], out=P.ps(bank)[:, 0:128],
                    lhsT=P.xT[:, k, t * 128:(t + 1) * 128], rhs=w_kv[:, k, br * 256 + 128:br * 256 + 256],
                    start=(k == 0), stop=(k == 7))
            si = n % 3
            S.I("act", "activation", [P.pbuf[bank]], [stg_b[si]], out=stg[si][:, 0:128], in_=P.ps(bank)[:, 0:128], func=AF.Copy)
            S.DMA("sp", [stg_b[si]], [], out=dr["vtok"][t * 128:(t + 1) * 128, bi * 128:(bi + 1) * 128], in_=stg[si][:, 0:128])
            n += 1
    S.barrier()
    A.release(m)


def nsa_compress(P, dr, KTc, Vc, kc_b):
    S, A = P.S, P.A
    m = A.mark()
    S.I("dve", "memset", [], [kc_b], ap=KTc, constant=0.0)
    S.I("dve", "memset", [], [kc_b], ap=Vc, constant=0.0)
    S.I("dve", "memset", [], [kc_b], ap=Vc[:, :, :, 64:65], constant=1.0)
    for g in range(2):
        S.DMA("sp", [], [kc_b], out=KTc[64:71, g, :], in_=dr["kaug_c"])
    zp = A.alloc(T, BF16)
    zp_b = Buf("zp")
    w1 = A.alloc(16 * 256, BF16).rearrange("p (l c) -> p l c", l=16)
    w1_b = Buf("w1")
    w2 = A.alloc(2 * 128, BF16).rearrange("p (h c) -> p h c", h=2)
    w2_b = Buf("w2")
    pp = A.alloc(16, BF16)
    b1 = A.alloc(2, F32)
    posb = A.alloc(2, F32)
    pb_b = Buf("posb")
    hid = [A.alloc(512, BF16) for _ in range(2)]
    hid_b = [Buf("hid0"), Buf("hid1")]
    xg = A.alloc(512, F32)
    ug = A.alloc(512, F32)
    xg_b = Buf("xg")
    S.I("dve", "memset", [], [zp_b], ap=zp[:, T - 64:T], constant=0.0)
    for hc in range(2):
        S.I("dve", "memset", [], [hid_b[hc]], ap=hid[hc], constant=0.0)
    for j in range(2):
        S.I("dve", "memset", [], [w2_b], ap=w2, constant=0.0)
        S.DMA("pool", [], [w1_b], out=w1, in_=dr["cmp_w1"][j].rearrange("p (l c) -> p l c", l=16), **CAST)
        S.DMA("pool", [], [w2_b], out=w2[:, :, 0:64], in_=dr["cmp_w2"][j].rearrange("p (h c) -> p h c", h=2), **CAST)
        S.DMA("pool", [], [pb_b], out=pp, in_=dr["cmp_pp"][j], **CAST)
        S.DMA("sp", [], [pb_b], out=b1, in_=dr["cmp_b1"][j])
        for hc in range(2):
            for l in range(16):
                S.I("pe", "matmul", [w1_b, pb_b], [P.pbuf[6]], out=P.ps(6)[:, 0:1], lhsT=w1[:, l, hc * 128:(hc + 1) * 128],
                    rhs=pp[:, l:l + 1], start=(l == 0), stop=(l == 15))
            S.I("dve", "tensor_tensor", [P.pbuf[6], pb_b], [pb_b], out=posb[:, hc:hc + 1], in0=P.ps(6)[:, 0:1],
                in1=b1[:, hc:hc + 1], op=ALU.add)
        for g in range(2):
            r0 = j * 128 + g * 64
            for q4 in range(4):
                ks = slice(q4 * 2048, (q4 + 1) * 2048)
                S.DMA("sp", [], [zp_b], out=zp[0:64, ks], in_=dr["kvT_all"][r0:r0 + 64, ks])
                hi = min((q4 + 1) * 2048, T - 1)
                S.DMA("sp", [], [zp_b], out=zp[64:128, q4 * 2048:hi], in_=dr["kvT_all"][r0:r0 + 64, q4 * 2048 + 1:hi + 1])
            for hc in range(2):
                bank = hc
                for l in range(16):
                    rhs = zp[:, 2 * l:2 * l + 16 * 510 + 1:16]
                    S.I("pe", "matmul", [w1_b, zp_b], [P.pbuf[bank]], out=P.ps(bank)[:, 0:511], lhsT=w1[:, l, hc * 128:(hc + 1) * 128],
                        rhs=rhs, start=(l == 0), stop=(l == 15))
                S.I("dve", "tensor_scalar", [P.pbuf[bank], pb_b], [xg_b], out=xg[:, 0:511], in0=P.ps(bank)[:, 0:511],
                    scalar1=posb[:, hc:hc + 1], scalar2=None, op0=ALU.add)
                S.I("dve", "tensor_tensor", [xg_b], [xg_b], out=ug[:, 0:511], in0=xg[:, 0:511], in1=xg[:, 0:511], op=ALU.mult)
                S.I("dve", "tensor_scalar", [xg_b], [xg_b], out=ug[:, 0:511], in0=ug[:, 0:511], scalar1=0.044715, scalar2=1.0,
                    op0=ALU.mult, op1=ALU.add)
                S.I("dve", "tensor_tensor", [xg_b], [xg_b], out=ug[:, 0:511], in0=ug[:, 0:511], in1=xg[:, 0:511], op=ALU.mult)
                S.I("act", "activation", [xg_b], [xg_b], out=ug[:, 0:511], in_=ug[:, 0:511], func=AF.Tanh, scale=0.7978845608028654)
                S.I("dve", "tensor_scalar", [xg_b], [xg_b], out=ug[:, 0:511], in0=ug[:, 0:511], scalar1=1.0, scalar2=0.5,
                    op0=ALU.add, op1=ALU.mult)
                S.I("dve", "tensor_tensor", [xg_b], [hid_b[hc]], out=hid[hc][:, 0:511], in0=ug[:, 0:511], in1=xg[:, 0:511], op=ALU.mult)
            if j == 0:
                for hc in range(2):
                    S.I("pe", "matmul", [w2_b, hid_b[hc]], [P.pbuf[2]], out=P.ps(2), lhsT=w2[:, hc, :], rhs=hid[hc],
                        start=(hc == 0), stop=(hc == 1))
                S.I("act", "activation", [P.pbuf[2]], [kc_b], out=KTc[0:64, g, 0:511], in_=P.ps(2)[0:64, 0:511], func=AF.Copy)
            else:
                for nb in range(4):
                    for hc in range(2):
                        S.I("pe", "matmul", [w2_b, hid_b[hc]], [P.pbuf[3]], out=P.ps(3)[:, nb * 64:(nb + 1) * 64],
                            lhsT=hid[hc][:, nb * 128:(nb + 1) * 128], rhs=w2[:, hc, 0:64], start=(hc == 0), stop=(hc == 1))
                S.I("act", "activation", [P.pbuf[3]], [kc_b], out=Vc[:, g, :, 0:64],
                    in_=P.ps(3)[:, 0:256].rearrange("p (n c) -> p n c", n=4), func=AF.Copy)
    S.barrier()
    A.release(m)


def nsa_pre(P, dr):
    S, A = P.S, P.A
    m = A.mark()
    w_in = A.alloc(8 * 1060, BF16).rearrange("p (k c) -> p k c", k=8)
    w_in_b = Buf("w_in")
    S.DMA("pool", [], [w_in_b], out=w_in, in_=dr["w_in"].rearrange("p (k c) -> p k c", k=8), **CAST)
    mem_kv(P, dr, A)
    qst = [A.alloc(512, BF16) for _ in range(3)]
    qst_b = [Buf("qst%d" % i) for i in range(3)]
    gt = A.alloc(16 * 36, F32).rearrange("p (t c) -> p t c", t=16)
    gt_b = Buf("gt")
    n = 0
    for g in range(NG):
        tk = slice(g * 512, (g + 1) * 512)
        for hh in range(14):
            c0 = hh * 64 if hh < 12 else 804 + (hh - 12) * 128
            bank = n % 4
            for k in range(8):
                S.I("pe", "matmul", [w_in_b, P.xT_b[g][k]], [P.pbuf[bank]], out=P.ps(bank), lhsT=w_in[:, k, c0:c0 + 128],
                    rhs=P.xT[:, k, tk], start=(k == 0), stop=(k == 7))
            si = n % 3
            if n % 2 == 0:
                S.I("act", "activation", [P.pbuf[bank]], [qst_b[si]], out=qst[si], in_=P.ps(bank), func=AF.Copy)
            else:
                S.I("dve", "tensor_copy", [P.pbuf[bank]], [qst_b[si]], out=qst[si], in_=P.ps(bank))
            if hh < 12:
                S.DMA("sp", [qst_b[si]], [], out=dr["q_scr"][hh, :, tk], in_=qst[si][0:64, :])
            else:
                S.DMA("sp", [qst_b[si]], [], out=dr["qm"][hh - 12, :, tk], in_=qst[si])
            n += 1
    for t in range(16):
        g = t // 4
        bank = 4 + t % 2
        for k in range(8):
            S.I("pe", "matmul", [w_in_b, P.xT_b[g][k]], [P.pbuf[bank]], out=P.ps(bank)[:, 0:36],
                lhsT=P.xT[:, k, t * 128:(t + 1) * 128], rhs=w_in[:, k, 768:804], start=(k == 0), stop=(k == 7))
        S.I("act", "activation", [P.pbuf[bank]], [gt_b], out=gt[:, t, :], in_=P.ps(bank)[:, 0:36], func=AF.Sigmoid)
    S.DMA("sp", [gt_b], [], out=dr["gt"], in_=gt.rearrange("p t c -> p (t c)"))
    S.barrier()
    A.release(m)


def nsa_attn(P, dr, KTc, Vc, kc_b):
    S, A = P.S, P.A
    m = A.mark()
    KTs = A.alloc(T, BF16)
    KTs_b = [Buf("KTs%d" % i) for i in range(4)]
    KTa_b = Buf("KTsaug")
    Vs = A.alloc(64 * 96, BF16).rearrange("p (b c) -> p b c", b=64)
    Vs_b = [Buf("Vs%d" % i) for i in range(4)]
    KTw = A.alloc(20 * 128, BF16)
    KTw_b = Buf("KTw")
    Vw = A.alloc(20 * 96, BF16).rearrange("p (b c) -> p b c", b=20)
    Vw_b = Buf("Vw")
    Wide = A.alloc(T, BF16)
    wmask = A.alloc(20 * 512 // 2, BF16).bitcast(U8).rearrange("p (i n) -> p i n", i=20)
    cmask = A.alloc(4 * 512 // 2, BF16).bitcast(U8).rearrange("p (i n) -> p i n", i=4)
    cmask_b = Buf("cmask")
    qrel = A.alloc(512, F16)
    krel = A.alloc(16, F32)
    sel = A.alloc(256, F32).rearrange("p (a c) -> p a c", a=2)
    agg = A.alloc(4 * 128, BF16).rearrange("p (i n) -> p i n", i=4)
    ident32 = A.alloc(128, F32)
    identb = A.alloc(128, BF16)
    cst_b = Buf("cst")
    wide_b = Buf("wide")
    S.DMA("sp", [], [wide_b], out=Wide, in_=dr["wide"])
    for dst, name in ((qrel, "qrel"), (krel, "krel"), (ident32, "ident32"), (identb, "identb")):
        S.DMA("sp", [], [cst_b], out=dst, in_=dr[name])
    S.DMA("sp", [], [cst_b], out=wmask, in_=dr["wmask"].rearrange("p (i n) -> p i n", i=20))
    S.DMA("sp", [], [cst_b], out=sel, in_=dr["sel"].rearrange("p (a c) -> p a c", a=2))
    S.DMA("sp", [], [cst_b], out=agg, in_=dr["agg"].rearrange("p (i n) -> p i n", i=4))
    selT = A.alloc(512, BF16)
    selT_b = Buf("selT")
    smask = [A.alloc(512, BF16) for _ in range(2)]
    smask_b = [Buf("smask0"), Buf("smask1")]
    PT = [A.alloc(512, BF16) for _ in range(4)]
    PT_b = [Buf("PT%d" % i) for i in range(4)]
    PTc = A.alloc(4 * 512, BF16).rearrange("p (i n) -> p i n", i=4)
    PTc_b = [Buf("PTc%d" % i) for i in range(4)]
    QT = A.alloc(6 * 512, BF16)
    QTv = QT.rearrange("p (h n) -> p h n", h=6)
    QT_b = Buf("QT")
    gt = A.alloc(16 * 36, F32).rearrange("p (t c) -> p t c", t=16)
    gt_b = Buf("gt")
    S.DMA("sp", [], [gt_b], out=gt, in_=dr["gt"].rearrange("p (t c) -> p t c", t=16))
    osb = [A.alloc(512, BF16) for _ in range(4)]
    osb_b = [Buf("osb%d" % i) for i in range(4)]
    accp = [A.alloc(4 * 128, F32).rearrange("p (t c) -> p t c", t=4) for _ in range(3)]
    accp_b = [Buf("accp%d" % i) for i in range(3)]
    impacc = A.alloc(4 * 128, F32).rearrange("p (t c) -> p t c", t=4)
    impacc_b = Buf("impacc")
    fbias = A.alloc(4 * 128, F32).rearrange("p (t c) -> p t c", t=4)
    fbias_b = Buf("fbias")
    NRD = 8
    rd = [A.alloc(4, F32) for _ in range(NRD)]
    rd_b = [Buf("rd%d" % i) for i in range(NRD)]
    fsc = [A.alloc(4, F32) for _ in range(2)]
    fsc_b = [Buf("fsc0"), Buf("fsc1")]
    impb = A.alloc(128, F32)
    impb2 = A.alloc(128, F32)
    m8 = A.alloc(16, F32)
    thr = A.alloc(1, F32)
    selm = A.alloc(128, BF16)
    tk_b = Buf("topk")
    st = {"s": 0, "p": 0, "o": 0, "ob": 3, "e": 0, "rd": 0, "f": 0, "sm": 0, "Sbanks": [0, 1, 2, 6]}

    S.I("pool", "memset", [], [KTa_b], ap=KTs[64:96, :], constant=0.0)
    S.I("pool", "memset", [], [QT_b], ap=QT[64:96, :], constant=0.0)
    S.DMA("sp", [], [KTa_b], out=KTs[64:71, :], in_=dr["kaug"])
    S.I("pool", "memset", [], Vs_b, ap=Vs, constant=0.0)
    S.I("pool", "memset", [], Vs_b, ap=Vs[:, :, 64:65], constant=1.0)
    S.I("pool", "memset", [], [Vw_b], ap=Vw, constant=0.0)
    S.I("pool", "memset", [], [Vw_b], ap=Vw[:, :, 64:65], constant=1.0)
    S.I("pool", "memset", [], [KTw_b], ap=KTw, constant=0.0)

    def epilogue(ob, hh, br, qg, g, first_branch, defer=True):
        e = st["e"] % 4
        st["e"] += 1
        ri = st["rd"] % NRD
        st["rd"] += 1
        S.I("act", "activation", [P.pbuf[ob]], [osb_b[e]], out=osb[e][0:72, :], in_=P.ps(ob)[0:72, :], func=AF.Copy)

        def rest():
            epilogue_rest(e, ri, hh, br, qg, g, first_branch)
        if defer:
            return ri, rest
        rest()
        return ri, None

    def epilogue_rest(e, ri, hh, br, qg, g, first_branch):
        fi = st["f"] % 2
        st["f"] += 1
        ps7b = P.ps(7).bitcast(BF16)
        ps7 = ps7b[:, 0:288].rearrange("p (t c) -> p t c", t=4)
        for t in range(4):
            S.I("pe", "transpose", [osb_b[e], cst_b], [P.pbuf[7]], out=ps7b[:, t * 72:(t + 1) * 72],
                in_=osb[e][0:72, t * 128:(t + 1) * 128], identity=identb[0:72, 0:72])
        S.I("dve", "tensor_scalar", [P.pbuf[7]], [rd_b[ri]], out=rd[ri], in0=ps7[:, :, 64], scalar1=1e-30, scalar2=None,
            op0=ALU.max)
        S.I("dve", "reciprocal", [rd_b[ri]], [rd_b[ri]], out=rd[ri], in_=rd[ri])
        S.I("dve", "tensor_tensor", [rd_b[ri], gt_b], [fsc_b[fi]], out=fsc[fi], in0=rd[ri],
            in1=gt[:, 4 * qg:4 * qg + 4, hh * 3 + br], op=ALU.mult)
        pslot = (hh // 2) - 3 * g
        c0 = 64 * (hh % 2)
        for t in range(4):
            dst = accp[pslot][:, t, c0:c0 + 64]
            if first_branch:
                S.I("dve", "tensor_scalar", [P.pbuf[7], fsc_b[fi]], [accp_b[pslot]], out=dst, in0=ps7[:, t, 0:64],
                    scalar1=fsc[fi][:, t:t + 1], scalar2=None, op0=ALU.mult)
            else:
                S.I("dve", "scalar_tensor_tensor", [P.pbuf[7], fsc_b[fi], accp_b[pslot]], [accp_b[pslot]], out=dst,
                    in0=ps7[:, t, 0:64], scalar=fsc[fi][:, t:t + 1], in1=dst, op0=ALU.mult, op1=ALU.add)

    for g in range(2):
        for i in range(4):
            ks = slice(i * 2048, (i + 1) * 2048)
            S.DMA("sp", [], [KTs_b[i]], out=KTs[0:64, ks], in_=dr["kvT_all"][256 + g * 64:256 + (g + 1) * 64, ks])
            S.DMA("sp", [], [Vs_b[i]], out=Vs[:, i * 16:(i + 1) * 16, 0:64],
                  in_=dr["vtok_all"][ks, g * 64:(g + 1) * 64].rearrange("(b p) c -> p b c", p=128))
        for qg in range(NG):
            tk = slice(qg * 512, (qg + 1) * 512)
            S.DMA("sp", [], [QT_b], out=QTv[0:64, :, :], in_=dr["q_scr"][g * 6:(g + 1) * 6, :, tk].rearrange("h d n -> d h n"))
            S.DMA("sp", [], [QT_b], out=QTv[64:71, :, :], in_=dr["qaug"][g * 6:(g + 1) * 6, :, tk].rearrange("h d n -> d h n"))
            i0 = 4 if qg == 0 else 0
            k0 = (16 * qg - 4 + i0) * 128
            k1 = (16 * qg + 16) * 128
            S.DMA("sp", [], [KTw_b], out=KTw[0:64, i0 * 128:2560], in_=dr["kvT_all"][512 + g * 64:512 + (g + 1) * 64, k0:k1])
            S.DMA("sp", [], [KTw_b], out=KTw[64:71, i0 * 128:2560], in_=dr["kaug"][:, k0:k1])
            S.DMA("sp", [], [Vw_b], out=Vw[:, i0:20, 0:64],
                  in_=dr["vtok_all"][k0:k1, 128 + g * 64:128 + (g + 1) * 64].rearrange("(b p) c -> p b c", p=128))
            S.DMA("sp", [], [cmask_b], out=cmask, in_=dr["cmask"][:, :, tk])
            S.DMA("sp", [], [fbias_b], out=fbias, in_=dr["fbias"][:, 4 * qg:4 * qg + 4, :])
            for h in range(6):
                hh = g * 6 + h
                ob = 3 + (st["o"] % 3)
                st["o"] += 1
                hold = {}

                def post_c(pt, ptb, nb=None):
                    pass
                slots = []
                for nb in range(4):
                    def post(pt, ptb, nb=nb):
                        S.I("dve", "scalar_tensor_tensor", [ptb, cmask_b], [ptb], out=pt, in0=pt, scalar=1.0e30, in1=cmask[:, nb, :],
                            op0=ALU.min, op1=ALU.mult)
                    slots.append(dict(k=KTc[0:96, g, nb * 128:(nb + 1) * 128], kb=[kc_b], q=QTv[0:96, h, :], qb=[QT_b],
                                      v=Vc[:, g, nb, :], vb=[kc_b], ob=ob, first=(nb == 0), last=(nb == 3), scale=NSA_SCALE,
                                      post=post, pt=PTc[:, nb, :], ptb=PTc_b[nb]))

                def epi(ob=ob, hh=hh, hold=hold):
                    hold["ri"], _ = epilogue(ob, hh, 0, qg, g, True, defer=False)
                    return None
                slots[-1]["epi"] = epi
                run_slots(P, slots, PT, PT_b, st)
                ri = hold["ri"]
                ps6 = P.ps(7).rearrange("p (t c) -> p t c", t=4)
                for t in range(4):
                    for nb in range(4):
                        S.I("pe", "matmul", [PTc_b[nb], cst_b], [P.pbuf[7]], out=ps6[:, t, :], lhsT=PTc[:, nb, t * 128:(t + 1) * 128],
                            rhs=agg[:, nb, :], start=(nb == 0), stop=(nb == 3))
                for t in range(4):
                    if h == 0:
                        S.I("dve", "tensor_scalar", [P.pbuf[7], rd_b[ri]], [impacc_b], out=impacc[:, t, :], in0=ps6[:, t, :],
                            scalar1=rd[ri][:, t:t + 1], scalar2=None, op0=ALU.mult)
                    else:
                        S.I("dve", "scalar_tensor_tensor", [P.pbuf[7], rd_b[ri], impacc_b], [impacc_b], out=impacc[:, t, :],
                            in0=ps6[:, t, :], scalar=rd[ri][:, t:t + 1], in1=impacc[:, t, :], op0=ALU.mult, op1=ALU.add)
            ps6b = P.ps(7).bitcast(BF16)
            for t in range(4):
                S.I("dve", "tensor_tensor", [impacc_b, fbias_b, tk_b], [tk_b], out=impb, in0=impacc[:, t, :], in1=fbias[:, t, :], op=ALU.add)
                S.I("dve", "max", [tk_b], [tk_b], out=m8[:, 0:8], in_=impb)
                S.I("dve", "match_replace", [tk_b], [tk_b], out=impb2, in_to_replace=m8[:, 0:8], in_values=impb, imm_value=NEG_BIG)
                S.I("dve", "max", [tk_b], [tk_b], out=m8[:, 8:16], in_=impb2)
                S.I("dve", "tensor_scalar", [tk_b], [tk_b], out=thr, in0=m8[:, 15:16], scalar1=-1.0e29, scalar2=None, op0=ALU.max)
                S.I("dve", "tensor_scalar", [tk_b], [tk_b], out=selm, in0=impb, scalar1=thr[:, 0:1], scalar2=MASK_BIG, op0=ALU.is_ge, op1=ALU.mult)
                S.I("pe", "transpose", [tk_b, cst_b], [P.pbuf[7]], out=ps6b[:, t * 128:(t + 1) * 128], in_=selm, identity=identb)
            S.I("act", "activation", [P.pbuf[7]], [selT_b], out=selT, in_=ps6b[:, 0:512], func=AF.Copy)
            nkb = 16 * qg + 16
            for trio in range(2):
                heads = [trio * 3 + i for i in range(3)]
                slots = []
                cur = {}
                for kb in range(nkb):
                    def pre(kb=kb, cur=cur):
                        si = st["sm"] % 2
                        st["sm"] += 1
                        S.I("pe", "matmul", [wide_b, selT_b], [P.pbuf[7]], out=P.ps(7), lhsT=Wide[:, kb * 128:(kb + 1) * 128],
                            rhs=selT, start=True, stop=True)
                        i = kb - 16 * qg
                        if i >= 0:
                            S.I("dve", "scalar_tensor_tensor", [cst_b, P.pbuf[7]], [smask_b[si]], out=smask[si], in0=qrel,
                                scalar=krel[:, i:i + 1], in1=P.ps(7), op0=ALU.is_ge, op1=ALU.mult)
                        else:
                            S.I("act", "activation", [P.pbuf[7]], [smask_b[si]], out=smask[si], in_=P.ps(7), func=AF.Copy)
                        cur["si"] = si
                    for ii, h in enumerate(heads):
                        def post(pt, ptb, cur=cur):
                            si = cur["si"]
                            S.I("dve", "tensor_tensor", [ptb, smask_b[si]], [ptb], out=pt, in0=pt, in1=smask[si], op=ALU.min)
                        d = dict(k=KTs[0:96, kb * 128:(kb + 1) * 128], kb=[KTs_b[kb // 16], KTa_b], q=QTv[0:96, h, :], qb=[QT_b],
                                 v=Vs[:, kb, :], vb=[Vs_b[kb // 16]], ob=3 + ii, first=(kb == 0), last=(kb == nkb - 1),
                                 scale=NSA_SCALE, post=post)
                        if ii == 0:
                            d["pre"] = pre
                        if kb == nkb - 1:
                            def epi(ob=3 + ii, hh=g * 6 + h):
                                return epilogue(ob, hh, 1, qg, g, False)[1]
                            d["epi"] = epi
                        slots.append(d)
                run_slots(P, slots, PT, PT_b, st)
            for h in range(6):
                hh = g * 6 + h
                ob = 3 + (st["o"] % 3)
                st["o"] += 1
                slots = []
                for i in range(i0, 20):
                    def post(pt, ptb, i=i):
                        S.I("dve", "scalar_tensor_tensor", [ptb, cst_b], [ptb], out=pt, in0=pt, scalar=1.0e30, in1=wmask[:, i, :],
                            op0=ALU.min, op1=ALU.mult)
                    slots.append(dict(k=KTw[0:96, i * 128:(i + 1) * 128], kb=[KTw_b], q=QTv[0:96, h, :], qb=[QT_b],
                                      v=Vw[:, i, :], vb=[Vw_b], ob=ob, first=(i == i0), last=(i == 19), scale=NSA_SCALE, post=post))

                def epi(ob=ob, hh=hh):
                    return epilogue(ob, hh, 2, qg, g, False)[1]
                slots[-1]["epi"] = epi
                run_slots(P, slots, PT, PT_b, st)
            pend_flush(st)
            for pslot in range(3):
                pair = 3 * g + pslot
                for t in range(4):
                    S.I("pe", "transpose", [accp_b[pslot], cst_b], [P.pbuf[7]], out=P.ps(7)[:, t * 128:(t + 1) * 128],
                        in_=accp[pslot][:, t, :], identity=ident32)
                S.I("act", "activation", [P.pbuf[7]], [P.xT_b[qg][pair]], out=P.xT[:, pair, tk], in_=P.ps(7), func=AF.Copy)
    mkv = Wide[:, 0:2048]
    mkv_b = wide_b
    S.DMA("sp", [], [mkv_b], out=mkv, in_=dr["memkv"])
    kpad = mkv[:, 0:1024].rearrange("p (h m) -> p h m", h=4)
    vaug = mkv[:, 1024:2048].rearrange("p (h b c) -> p h b c", h=4, b=2)
    QTm = [KTs[:, 0:2048], KTs[:, 2048:4096]]
    QTm_b = [KTs_b[0], KTs_b[1]]
    osbm = [KTs[:, 4096:5120].bitcast(F32), KTs[:, 5120:6144].bitcast(F32)]
    osbm_b = [KTs_b[2], Buf("osbm1")]
    st2 = {"s": st["s"], "p": st["p"], "o": 0, "ob": 3, "e": 0, "maxpend": 1}
    mem_heads(P, dr, kpad, vaug, mkv_b, QTm, QTm_b, PT, PT_b, osbm, osbm_b, sel, cst_b, st2)
    pend_flush(st2)
    S.barrier()
    A.release(m)


BF = ml_dtypes.bfloat16


def alibi_slopes12():
    def pow2(k):
        start = 2.0 ** (-8.0 / k)
        return [start ** (i + 1) for i in range(k)]
    s = pow2(8) + pow2(16)[0::2][:4]
    return np.asarray(s, np.float32)


def split_bf(v, n):
    out = []
    r = np.asarray(v, np.float64)
    for _ in range(n):
        p = r.astype(np.float32).astype(BF)
        out.append(p)
        r = r - p.astype(np.float64)
    return out


def kaug_table(pos):
    hi, lo = split_bf(pos, 2)
    one = np.ones_like(hi)
    return np.stack([hi, lo, hi, lo, one, one, one], axis=0)


def nsa_consts_global():
    kaug = kaug_table(np.arange(T, dtype=np.float64))
    cpos = np.arange(512, dtype=np.float64) * 16 + 15.5
    cpos[511] = 0.0
    kaug_c = kaug_table(cpos)
    wide = (np.arange(128)[:, None] == (np.arange(T)[None, :] // 64)).astype(np.float32).astype(BF)
    cn = np.arange(512)
    cs = cn * 16
    ss = np.arange(128) * 64
    ov = np.clip(np.minimum(cs[:, None] + 32, ss[None, :] + 64) - np.maximum(cs[:, None], ss[None, :]), 0, None) / 32.0
    ov[511] = 0.0
    agg = ov.reshape(4, 128, 128).transpose(1, 0, 2).reshape(128, 512).astype(np.float32).astype(BF)
    ident32 = np.eye(128, dtype=np.float32)
    identb = np.eye(128, dtype=np.float32).astype(BF)
    return dict(kaug=kaug, kaug_c=kaug_c, wide=wide, agg=agg, ident32=ident32, identb=identb)


def nsa_consts_core(j):
    qpos = core_token_index(j).astype(np.int64)
    slopes = alibi_slopes12().astype(np.float64)
    qaug = np.zeros((12, 7, TLOC), BF)
    for h in range(12):
        mhi, mlo = split_bf(np.array([slopes[h]]), 2)
        mp = float(mhi[0]) + float(mlo[0])
        v = -8.0 * mp * qpos.astype(np.float64)
        vh, vm, vl = split_bf(v, 3)
        m8hi = (np.float32(8.0) * mhi.astype(np.float32)).astype(BF)[0]
        m8lo = (np.float32(8.0) * mlo.astype(np.float32)).astype(BF)[0]
        qaug[h, 0] = m8hi
        qaug[h, 1] = m8hi
        qaug[h, 2] = m8lo
        qaug[h, 3] = m8lo
        qaug[h, 4] = vh
        qaug[h, 5] = vm
        qaug[h, 6] = vl
    qrel = qpos[:512]
    krelw = (128 * (np.arange(20) - 4))[None, :, None] + np.arange(128)[:, None, None]
    dq = qrel[None, None, :] - krelw
    wmask = ((dq >= 0) & (dq < 512)).astype(np.uint8).reshape(128, 20 * 512)
    cnn = (128 * np.arange(4))[None, :, None] + np.arange(128)[:, None, None]
    cmask = ((cnn <= 510) & (16 * cnn + 31 <= qpos[None, None, :])).astype(np.uint8)
    tq = qpos.reshape(16, 128).T
    jj = np.arange(128)[None, None, :]
    cur = (tq // 64)[:, :, None]
    forced = (jj == 0) | (jj == cur) | (jj == cur - 1)
    valid = jj * 64 <= tq[:, :, None]
    fb = np.where(valid, np.where(forced, 1.0e4, 0.0), NEG_BIG).astype(np.float32).reshape(128, 16 * 128)
    return dict(qaug=qaug, wmask=wmask, cmask=np.ascontiguousarray(cmask), fbias=fb)


def prep_cmp(cmp_pos, cmp_w1, cmp_b1, cmp_w2):
    w1 = np.stack([kmajor(cmp_w1[j]) for j in range(2)])
    w2 = np.stack([kmajor(cmp_w2[j]) for j in range(2)])
    b1 = np.ascontiguousarray(cmp_b1.reshape(2, 2, 128).transpose(0, 2, 1))
    pp = np.ascontiguousarray(cmp_pos.reshape(2, 16, 2, 64).transpose(0, 2, 3, 1).reshape(2, 128, 16))
    return w1, w2, b1, pp


def _mla_pre_io(P, pre):
    return {
        "w_in": P.din(pre + "w_in", [128, 8 * 672]), "w_uq": P.din(pre + "w_uq", [128, 2 * 1152]), "ng": P.din(pre + "ng", [128, 3]),
        "tabc": P.din("tabc", [128, TLOC]) if "tabc" not in P.dram else P.dram["tabc"],
        "tabs": P.din("tabs", [128, TLOC]) if "tabs" not in P.dram else P.dram["tabs"],
        "memT": P.din("memT", [8, 128, 256]) if "memT" not in P.dram else P.dram["memT"],
        "w_mkv": P.din(pre + "w_mkv", [128, 8 * 512]),
        "lat_c": P.dout("o_lat_c", [128, TLOC], BF16), "lat_r": P.dout("o_lat_r", [32, TLOC], BF16),
        "q_scr": P.dout("o_q_scr", [12, 96, TLOC], BF16), "qm": P.dout("o_qm", [2, 128, TLOC], BF16),
        "memkv": P.dout("o_memkv", [128, 2048], BF16),
    }


def _mla_attn_io(P, pre):
    return {
        "ckv_all": P.din("ckv_all", [128, T], BF16), "kr_all": P.din("kr_all", [32, T], BF16),
        "q_scr": P.din("i_q_scr", [12, 96, TLOC], BF16), "qm": P.din("i_qm", [2, 128, TLOC], BF16),
        "memkv": P.din("i_memkv", [128, 2048], BF16), "w_ukv": P.din(pre + "w_ukv", [128, 1536]),
        "qrel": P.din("qrel", [128, 512], F16), "krel": P.din("krel", [128, 16]), "sel": P.din("sel", [128, 256]),
        "w_out": P.din(pre + "w_out", [128, 8 * 1024]),
    }


def _ffn_io(P, l, i):
    return P.din("wgu%d%d" % (l, i), [NF, 128, 2048]), P.din("wd%d%d" % (l, i), [8, 128, NF * 128])


def build_phase(ph):
    nc = bass.Bass("TRN2", target_bir_lowering=False)
    with ExitStack() as stack:
        P = Prog(nc, stack)
        xT_d = P.din("xT", [8, 128, TLOC])
        lnp_d = P.din("lnp", [128, 192])
        P.alloc_state(lnp_d)
        if ph == 0:
            P.load_xT(xT_d)
            P.ffn(*_ffn_io(P, 0, 0), 0)
            mla_pre(P, _mla_pre_io(P, "l0_"))
        elif ph == 1:
            P.load_xT(xT_d)
            dr = _mla_attn_io(P, "l0_")
            mla_attn(P, dr)
            resid_ln_from_oT(P, dr["w_out"], 1)
            P.ffn(*_ffn_io(P, 0, 1), 2)
            P.ffn(*_ffn_io(P, 1, 0), 3)
            mla_pre(P, _mla_pre_io(P, "l1_"))
        elif ph == 2:
            P.load_xT(xT_d)
            dr = _mla_attn_io(P, "l1_")
            mla_attn(P, dr)
            resid_ln_from_oT(P, dr["w_out"], 4)
            P.ffn(*_ffn_io(P, 1, 1), 5)
            nsa_kv_stage(P, {"w_kv": P.din("w_kv", [128, 8 * 768]), "kvT": P.dout("o_kvT", [768, TLOC], BF16),
                             "vtok": P.dout("o_vtok", [TLOC, 256], BF16)})
        else:
            KTc = P.A.alloc(2 * 512, BF16).rearrange("p (g n) -> p g n", g=2)
            Vc = P.A.alloc(2 * 4 * 96, BF16).rearrange("p (g b c) -> p g b c", g=2, b=4)
            kc_b = Buf("kc")
            P.load_xT(xT_d)
            dr = {
                "kvT_all": P.din("kvT_all", [768, T], BF16), "vtok_all": P.din("vtok_all", [T, 256], BF16),
                "cmp_w1": P.din("cmp_w1", [2, 128, 16 * 256]), "cmp_w2": P.din("cmp_w2", [2, 128, 128]),
                "cmp_b1": P.din("cmp_b1", [2, 128, 2]), "cmp_pp": P.din("cmp_pp", [2, 128, 16]),
                "kaug": P.din("kaug", [7, T], BF16), "kaug_c": P.din("kaug_c", [7, 512], BF16),
                "wide": P.din("wide", [128, T], BF16), "agg": P.din("agg", [128, 512], BF16),
                "ident32": P.din("ident32", [128, 128]), "identb": P.din("identb", [128, 128], BF16),
                "qaug": P.din("qaug", [12, 7, TLOC], BF16), "wmask": P.din("wmask", [128, 20 * 512], U8),
                "cmask": P.din("cmask", [128, 4, TLOC], U8), "fbias": P.din("fbias", [128, 16, 128]),
                "qrel": P.din("qrel", [128, 512], F16), "krel": P.din("krel", [128, 16]), "sel": P.din("sel", [128, 256]),
                "memT": P.din("memT", [8, 128, 256]),
                "q_scr": P.dscr("q_scr", [12, 64, TLOC], BF16), "qm": P.dscr("qm", [2, 128, TLOC], BF16),
                "gt": P.dscr("gt", [128, 16 * 36]), "memkv": P.dscr("memkv", [128, 2048], BF16),
            }
            nsa_compress(P, dr, KTc, Vc, kc_b)
            for l in (2, 3):
                P.ffn(*_ffn_io(P, l, 0), 3 * l)
                drl = dict(dr)
                drl["w_in"] = P.din("l%d_w_in" % l, [128, 8 * 1060])
                drl["w_mkv"] = P.din("l%d_w_mkv" % l, [128, 8 * 512])
                nsa_pre(P, drl)
                nsa_attn(P, drl, KTc, Vc, kc_b)
                resid_ln_from_oT(P, P.din("l%d_w_out" % l, [128, 8 * 1024]), 3 * l + 1)
                P.ffn(*_ffn_io(P, l, 1), 3 * l + 2)
        if ph < 3:
            P.store_xT(P.dout("o_x", [8, 128, TLOC]))
        else:
            P.store_xT(P.dout("out", [8, 128, TLOC]))
        P.S.barrier()
        block = stack.enter_context(nc.Block())
        P.S.emit(block)
    return nc


def _gather_tokens_T(parts, rows, dtype):
    outs = [np.zeros((rows, T), dtype) for _ in range(NB)]
    for c in range(NCORE):
        b, j = divmod(c, 4)
        outs[b][:, core_token_index(j)] = parts[c]
    return outs


def kernel(x, mem, ln_g, ln_b, ffn_w_gu, ffn_w_down, w_mem_kv, w_out, mla_w_in, mla_q_norm_g, mla_kv_norm_g,
           mla_w_uq, mla_w_ukv, nsa_w_in, nsa_w_kv, cmp_pos, cmp_w1, cmp_b1, cmp_w2):
    f = lambda a: np.ascontiguousarray(np.asarray(a, dtype=np.float32))
    x, mem = f(x), f(mem)
    lnp = prep_ln(f(ln_g), f(ln_b))
    ffw = {}
    for l in range(DEPTH):
        for i in range(2):
            ffw[(l, i)] = prep_ffn_weights(f(ffn_w_gu[l, i]), f(ffn_w_down[l, i]))
    memT = [np.ascontiguousarray(mem[b].T.reshape(8, 128, 256)) for b in range(NB)]
    tabs_ = [rope_tables(j) for j in range(4)]
    acon = [attn_consts(j) for j in range(4)]
    cores = list(range(NCORE))

    def mla_pre_in(l):
        ng = np.stack([mla_q_norm_g[l][:128], mla_q_norm_g[l][128:], mla_kv_norm_g[l]], axis=1).astype(np.float32)
        return {"l%d_w_in" % l: kmajor(f(mla_w_in[l])), "l%d_w_uq" % l: kmajor(f(mla_w_uq[l])), "l%d_ng" % l: np.ascontiguousarray(ng),
                "l%d_w_mkv" % l: kmajor(f(w_mem_kv[l]))}

    def percore_pre(c):
        b, j = divmod(c, 4)
        return {"tabc": tabs_[j][0], "tabs": tabs_[j][1], "memT": memT[b]}

    def mla_attn_in(l, prev, c, ckv_all, kr_all):
        b, j = divmod(c, 4)
        return {"ckv_all": ckv_all[b], "kr_all": kr_all[b], "i_q_scr": prev[c]["o_q_scr"], "i_qm": prev[c]["o_qm"],
                "i_memkv": prev[c]["o_memkv"], "l%d_w_ukv" % l: f(mla_w_ukv[l]), "qrel": acon[j][0], "krel": acon[j][1],
                "sel": acon[j][2], "l%d_w_out" % l: kmajor(f(w_out[l]))}

    def ffn_in(l, i):
        return {"wgu%d%d" % (l, i): ffw[(l, i)][0], "wd%d%d" % (l, i): ffw[(l, i)][1]}

    xs = shard_xT(x)
    com = mla_pre_in(0)
    com.update(ffn_in(0, 0))
    maps = []
    for c in cores:
        m_ = {"xT": xs[c], "lnp": lnp}
        m_.update(com)
        m_.update(percore_pre(c))
        maps.append(m_)
    rr0 = run_bass_kernel_spmd(build_phase(0), maps, core_ids=cores, trace=_TRACE)
    r0 = rr0.results
    if _TRACE:
        print('PHASE 0 exec_ns', rr0.exec_time_ns, flush=True)
    if _DBG is not None:
        _DBG(0, r0)
    ckv_all = _gather_tokens_T([r0[c]["o_lat_c"] for c in cores], 128, BF)
    kr_all = _gather_tokens_T([r0[c]["o_lat_r"] for c in cores], 32, BF)
    com = mla_pre_in(1)
    com.update(ffn_in(0, 1))
    com.update(ffn_in(1, 0))
    maps = []
    for c in cores:
        m_ = {"xT": r0[c]["o_x"], "lnp": lnp}
        m_.update(com)
        m_.update(percore_pre(c))
        m_.update(mla_attn_in(0, r0, c, ckv_all, kr_all))
        maps.append(m_)
    rr1 = run_bass_kernel_spmd(build_phase(1), maps, core_ids=cores, trace=_TRACE)
    r1 = rr1.results
    if _TRACE:
        print('PHASE 1 exec_ns', rr1.exec_time_ns, flush=True)
    if _DBG is not None:
        _DBG(1, r1)
    ckv_all = _gather_tokens_T([r1[c]["o_lat_c"] for c in cores], 128, BF)
    kr_all = _gather_tokens_T([r1[c]["o_lat_r"] for c in cores], 32, BF)
    com = ffn_in(1, 1)
    com["w_kv"] = kmajor(f(nsa_w_kv))
    maps = []
    for c in cores:
        m_ = {"xT": r1[c]["o_x"], "lnp": lnp}
        m_.update(com)
        m_.update(mla_attn_in(1, r1, c, ckv_all, kr_all))
        maps.append(m_)
    rr2 = run_bass_kernel_spmd(build_phase(2), maps, core_ids=cores, trace=_TRACE)
    r2 = rr2.results
    if _TRACE:
        print('PHASE 2 exec_ns', rr2.exec_time_ns, flush=True)
    if _DBG is not None:
        _DBG(2, r2)
    kvT_all = _gather_tokens_T([r2[c]["o_kvT"] for c in cores], 768, BF)
    vt = _gather_tokens_T([np.ascontiguousarray(np.asarray(r2[c]["o_vtok"]).T) for c in cores], 256, BF)
    vtok_all = [np.ascontiguousarray(v.T) for v in vt]
    gc = nsa_consts_global()
    w1, w2, b1, pp = prep_cmp(f(cmp_pos), f(cmp_w1), f(cmp_b1), f(cmp_w2))
    com = {"cmp_w1": w1, "cmp_w2": w2, "cmp_b1": b1, "cmp_pp": pp, "kaug": gc["kaug"], "kaug_c": gc["kaug_c"],
           "wide": gc["wide"], "agg": gc["agg"], "ident32": gc["ident32"], "identb": gc["identb"]}
    for l in (2, 3):
        com.update(ffn_in(l, 0))
        com.update(ffn_in(l, 1))
        com["l%d_w_in" % l] = kmajor(f(nsa_w_in[l - 2]))
        com["l%d_w_mkv" % l] = kmajor(f(w_mem_kv[l]))
        com["l%d_w_out" % l] = kmajor(f(w_out[l]))
    ccon = [nsa_consts_core(j) for j in range(4)]
    maps = []
    for c in cores:
        b, j = divmod(c, 4)
        m_ = {"xT": r2[c]["o_x"], "lnp": lnp, "kvT_all": kvT_all[b], "vtok_all": vtok_all[b], "qaug": ccon[j]["qaug"],
              "wmask": ccon[j]["wmask"], "cmask": ccon[j]["cmask"], "fbias": ccon[j]["fbias"].reshape(128, 16, 128),
              "qrel": acon[j][0], "krel": acon[j][1], "sel": acon[j][2], "memT": memT[b]}
        m_.update(com)
        maps.append(m_)
    rr3 = run_bass_kernel_spmd(build_phase(3), maps, core_ids=cores, trace=_TRACE)
    r3 = rr3.results
    if _TRACE:
        print('PHASE 3 exec_ns', rr3.exec_time_ns, flush=True)
    if _DBG is not None:
        _DBG(3, r3)
    return unshard_xT([np.asarray(r3[c]["out"]) for c in cores])
```

```python
import numpy as np
import ml_dtypes
import concourse.bass as bass
import concourse.mybir as mybir
from concourse.bass_utils import run_bass_kernel_spmd
from contextlib import ExitStack

F32 = mybir.dt.float32
BF16 = mybir.dt.bfloat16
F16 = mybir.dt.float16
AF = mybir.ActivationFunctionType
ALU = mybir.AluOpType

D = 1024
T = 8192
NB = 2
NCORE = 8
TLOC = 2048
NG = TLOC // 512
DFF = 2816
NF = DFF // 128
DEPTH = 4
DN_ALPHA = (2 * DEPTH) ** 0.25
LN_EPS = 1e-5
RMS_EPS = 1e-6

ZIG = [0, 7, 8, 15]


def global_block(j, i):
    s, r = divmod(i, 4)
    return 16 * s + 4 * j + r


def core_token_index(j):
    return np.concatenate([np.arange(128) + 128 * global_block(j, i) for i in range(16)])


class Buf:
    __slots__ = ("name", "w", "r")

    def __init__(self, name):
        self.name = name
        self.w = None
        self.r = {}


class Sched:
    ENGS = ("pe", "act", "dve", "pool", "sp")
    NDMA = 24

    def __init__(self, nc, stack):
        self.nc = nc
        self.sem = {}
        for e in ("pe", "act", "dve", "pool"):
            self.sem[e] = stack.enter_context(nc.semaphore("s_" + e))
        for k in range(self.NDMA):
            self.sem[("d", k)] = stack.enter_context(nc.semaphore("s_d%d" % k))
        self.cnt = {k: 0 for k in self.sem}
        self.ops = {e: [] for e in self.ENGS}
        self.seen = {e: {} for e in self.ENGS}
        self.dma_rr = 0
        self.nops = 0

    def _collect(self, eng, reads, writes):
        waits = {}

        def add(ev):
            if ev is None:
                return
            k, v = ev
            if k == eng == "pe":
                return
            if waits.get(k, 0) < v:
                waits[k] = v

        for b in reads:
            add(b.w)
        for b in writes:
            add(b.w)
            for k, v in b.r.items():
                add((k, v))
        out = []
        seen = self.seen[eng]
        for k, v in waits.items():
            if seen.get(k, 0) < v:
                seen[k] = v
                out.append((k, v))
        return out

    def _commit(self, ev, reads, writes):
        k, v = ev
        for b in reads:
            if b.r.get(k, 0) < v:
                b.r[k] = v
        for b in writes:
            b.w = ev
            b.r = {}

    def op(self, eng, fn, reads=(), writes=()):
        waits = self._collect(eng, reads, writes)
        self.cnt[eng] += 1
        ev = (eng, self.cnt[eng])
        self.ops[eng].append((waits, fn, (eng, 1)))
        self._commit(ev, reads, writes)
        self.nops += 1
        return ev

    def dma(self, eng, fn, reads=(), writes=()):
        k = ("d", self.dma_rr)
        self.dma_rr = (self.dma_rr + 1) % self.NDMA
        waits = self._collect(eng, reads, writes)
        prev = self.cnt[k]
        if prev and self.seen[eng].get(k, 0) < prev:
            self.seen[eng][k] = prev
            waits.append((k, prev))
        self.cnt[k] += 16
        ev = (k, self.cnt[k])
        self.ops[eng].append((waits, fn, (k, 16)))
        self._commit(ev, reads, writes)
        self.nops += 1
        return ev

    def I(self, eng, method, reads=(), writes=(), **kw):
        return self.op(eng, lambda e: getattr(e, method)(**kw), reads, writes)

    def DMA(self, eng, reads=(), writes=(), **kw):
        return self.dma(eng, lambda e: e.dma_start(**kw), reads, writes)

    def wait_all(self, eng):
        waits = []
        for k, v in self.cnt.items():
            if v and k != eng and self.seen[eng].get(k, 0) < v:
                self.seen[eng][k] = v
                waits.append((k, v))
        if waits:
            self.ops[eng].append((waits, None, None))

    def barrier(self):
        for e in self.ENGS:
            self.wait_all(e)

    def emit(self, block):
        sem = self.sem

        def run(engine, lst):
            for waits, fn, inc in lst:
                for k, v in waits:
                    engine.wait_ge(sem[k], v)
                if fn is not None:
                    ins = fn(engine)
                    ins.then_inc(sem[inc[0]], inc[1])

        ops = self.ops

        @block.tensor
        def _(e):
            run(e, ops["pe"])

        @block.scalar
        def _(e):
            run(e, ops["act"])

        @block.vector
        def _(e):
            run(e, ops["dve"])

        @block.gpsimd
        def _(e):
            run(e, ops["pool"])

        @block.sync
        def _(e):
            run(e, ops["sp"])


class Arena:
    def __init__(self, base_bf16, nbytes):
        self.base = base_bf16
        self.nbytes = nbytes
        self.top = 0
        self.peak = 0

    def alloc(self, nelem, dtype):
        sz = 4 if dtype == F32 else 2
        nb = (nelem * sz + 63) // 64 * 64
        off = self.top
        self.top += nb
        self.peak = max(self.peak, self.top)
        assert self.top <= self.nbytes, ("arena overflow", self.top, self.nbytes)
        v = self.base[:, off // 2:(off + nelem * sz) // 2]
        if dtype != BF16:
            v = v.bitcast(dtype)
        return v

    def mark(self):
        return self.top

    def release(self, m):
        self.top = m


class Prog:
    def __init__(self, nc, stack, arena_bytes=207 * 1024):
        self.nc = nc
        self.S = Sched(nc, stack)
        arena_t = stack.enter_context(nc.sbuf_tensor("arena", [128, arena_bytes // 2], BF16))
        self.A = Arena(arena_t[:, :], arena_bytes)
        self.psum = []
        self.pbuf = []
        for i in range(8):
            p = stack.enter_context(nc.psum_tensor("ps%d" % i, [128, 512], F32))
            self.psum.append(p)
            self.pbuf.append(Buf("ps%d" % i))
        self.dram = {}

    def ps(self, i):
        return self.psum[i][:, :]

    def din(self, name, shape, dtype=F32):
        t = self.nc.dram_tensor(name, list(shape), dtype, kind="ExternalInput").ap()
        self.dram[name] = t
        return t

    def dscr(self, name, shape, dtype=F32):
        t = self.nc.dram_tensor(name, list(shape), dtype, kind="Internal").ap()
        self.dram[name] = t
        return t

    def dout(self, name, shape, dtype=F32):
        t = self.nc.dram_tensor(name, list(shape), dtype, kind="ExternalOutput").ap()
        self.dram[name] = t
        return t

    def alloc_state(self, lnp_dram):
        A, S = self.A, self.S
        self.x32 = A.alloc(8 * TLOC, F32).rearrange("p (k t) -> p k t", k=8)
        self.xT = A.alloc(8 * TLOC, BF16).rearrange("p (k t) -> p k t", k=8)
        self.x32_b = [[Buf("x32_%d_%d" % (g, c)) for c in range(8)] for g in range(NG)]
        self.xT_b = [[Buf("xT_%d_%d" % (g, c)) for c in range(8)] for g in range(NG)]
        self.ones32 = A.alloc(128, F32)
        self.ones_b = Buf("ones")
        S.op("dve", lambda e: e.memset(self.ones32, 1.0 / 1024.0), writes=[self.ones_b])
        self.lnp = A.alloc(2 * 12 * 8, F32).rearrange("p (a l k) -> p a l k", a=2, l=12)
        self.lnp_b = Buf("lnp")
        S.dma("sp", lambda e: e.dma_start(out=self.lnp, in_=lnp_dram.rearrange("p (a l k) -> p a l k", a=2, l=12)),
              writes=[self.lnp_b])

    def load_xT(self, xT_dram):
        S = self.S
        for k in range(8):
            for g in range(NG):
                dst = self.x32[:, k, g * 512:(g + 1) * 512]
                src = xT_dram[k, :, g * 512:(g + 1) * 512]
                S.dma("sp", lambda e, dst=dst, src=src: e.dma_start(out=dst, in_=src), writes=[self.x32_b[g][k]])
                xt = self.xT[:, k, g * 512:(g + 1) * 512]
                S.op("act", lambda e, dst=dst, xt=xt: e.activation(out=xt, in_=dst, func=AF.Copy),
                     reads=[self.x32_b[g][k]], writes=[self.xT_b[g][k]])

    def store_xT(self, out_dram):
        S = self.S
        for k in range(8):
            for g in range(NG):
                src = self.x32[:, k, g * 512:(g + 1) * 512]
                dst = out_dram[k, :, g * 512:(g + 1) * 512]
                S.dma("sp", lambda e, dst=dst, src=src: e.dma_start(out=dst, in_=src), reads=[self.x32_b[g][k]])

    def alloc_ln_tmp(self):
        A = self.A
        d = {}
        d["sq"] = [A.alloc(512, F32) for _ in range(2)]
        d["sq_b"] = [Buf("lnsq0"), Buf("lnsq1")]
        d["mean"] = A.alloc(512, F32)
        d["rstd"] = A.alloc(512, F32)
        d["mr_b"] = Buf("lnmr")
        d["t"] = [A.alloc(512, F32) for _ in range(2)]
        d["t_b"] = [Buf("lnt0"), Buf("lnt1")]
        d["cnt"] = 0
        return d

    def resid_chunk(self, g, c, ybank, tmp):
        S = self.S
        xs = self.x32[:, c, g * 512:(g + 1) * 512]
        S.op("dve", lambda e: e.scalar_tensor_tensor(out=xs, in0=xs, scalar=DN_ALPHA, in1=self.ps(ybank),
                                                     op0=ALU.mult, op1=ALU.add),
             reads=[self.pbuf[ybank], self.x32_b[g][c]], writes=[self.x32_b[g][c]])
        i = tmp["cnt"] % 2
        tmp["cnt"] += 1
        sq, sq_b = tmp["sq"][i], tmp["sq_b"][i]
        S.op("act", lambda e: e.activation(out=sq, in_=xs, func=AF.Square), reads=[self.x32_b[g][c]], writes=[sq_b])

        def stats(sbank):
            S.op("pe", lambda e: e.matmul(self.ps(sbank), lhsT=self.ones32, rhs=xs, start=(c == 0), stop=(c == 7)),
                 reads=[self.ones_b, self.x32_b[g][c]], writes=[self.pbuf[sbank]])
            S.op("pe", lambda e: e.matmul(self.ps(sbank + 1), lhsT=self.ones32, rhs=sq, start=(c == 0), stop=(c == 7)),
                 reads=[self.ones_b, sq_b], writes=[self.pbuf[sbank + 1]])
        return stats

    def ln_finalize(self, g, sbank, ln_idx, tmp):
        for step in self.ln_steps(g, sbank, ln_idx, tmp):
            step()

    def ln_steps(self, g, sbank, ln_idx, tmp):
        yield lambda: self._ln_stats(g, sbank, tmp)
        for c in range(8):
            yield lambda c=c: self._ln_chunk(g, c, ln_idx, tmp)

    def _ln_stats(self, g, sbank, tmp):
        S = self.S
        mean, rstd, mr_b = tmp["mean"], tmp["rstd"], tmp["mr_b"]
        S.op("act", lambda e: e.activation(out=mean, in_=self.ps(sbank), func=AF.Copy),
             reads=[self.pbuf[sbank]], writes=[mr_b])
        S.op("dve", lambda e: e.tensor_tensor(out=rstd, in0=mean, in1=mean, op=ALU.mult), reads=[mr_b], writes=[mr_b])
        S.op("dve", lambda e: e.tensor_tensor(out=rstd, in0=self.ps(sbank + 1), in1=rstd, op=ALU.subtract),
             reads=[mr_b, self.pbuf[sbank + 1]], writes=[mr_b])
        S.op("dve", lambda e: e.tensor_scalar(out=rstd, in0=rstd, scalar1=LN_EPS, scalar2=None, op0=ALU.add),
             reads=[mr_b], writes=[mr_b])
        S.op("act", lambda e: e.activation(out=rstd, in_=rstd, func=AF.Sqrt), reads=[mr_b], writes=[mr_b])
        S.op("dve", lambda e: e.reciprocal(out=rstd, in_=rstd), reads=[mr_b], writes=[mr_b])

    def _ln_chunk(self, g, c, ln_idx, tmp):
        S = self.S
        mean, rstd, mr_b = tmp["mean"], tmp["rstd"], tmp["mr_b"]
        if True:
            xs = self.x32[:, c, g * 512:(g + 1) * 512]
            xt = self.xT[:, c, g * 512:(g + 1) * 512]
            i = c % 2
            t, t_b = tmp["t"][i], tmp["t_b"][i]
            S.op("dve", lambda e, t=t, xs=xs: e.tensor_tensor(out=t, in0=xs, in1=mean, op=ALU.subtract),
                 reads=[self.x32_b[g][c], mr_b], writes=[t_b])
            S.op("dve", lambda e, t=t: e.tensor_tensor(out=t, in0=t, in1=rstd, op=ALU.mult), reads=[t_b, mr_b], writes=[t_b])
            gs = self.lnp[:, 0, ln_idx, c:c + 1]
            bs = self.lnp[:, 1, ln_idx, c:c + 1]
            S.op("dve", lambda e, t=t, xs=xs, gs=gs, bs=bs: e.tensor_scalar(out=xs, in0=t, scalar1=gs, scalar2=bs,
                                                                            op0=ALU.mult, op1=ALU.add),
                 reads=[t_b, self.lnp_b], writes=[self.x32_b[g][c]])
            S.op("act", lambda e, xs=xs, xt=xt: e.activation(out=xt, in_=xs, func=AF.Copy),
                 reads=[self.x32_b[g][c]], writes=[self.xT_b[g][c]])

    def ffn(self, wgu_dram, wd_dram, ln_idx):
        S, A = self.S, self.A
        m = A.mark()
        NSUB = 2
        tmp = self.alloc_ln_tmp()
        hT = A.alloc(NF * NSUB * 512, BF16).rearrange("p (j t) -> p j t", j=NF)
        hT_b = [[Buf("hT%d_%d" % (j, s)) for s in range(NSUB)] for j in range(NF)]
        NW = 3
        wgu = [A.alloc(8 * 256, BF16).rearrange("p (k c) -> p k c", k=8) for _ in range(NW)]
        wgu_b = [Buf("wgu%d" % i) for i in range(NW)]
        wd = [A.alloc(NF * 128, BF16).rearrange("p (j n) -> p j n", j=NF) for _ in range(NW)]
        wd_b = [Buf("wd%d" % i) for i in range(NW)]
        sg = [A.alloc(512, F32) for _ in range(2)]
        sg_b = [Buf("sg0"), Buf("sg1")]
        cnt = 0
        wcnt = 0
        dcnt = 0
        ln_pending = []
        for p in range(TLOC // (NSUB * 512)):
            for j in range(NF):
                if j >= 1 and ln_pending:
                    ln_pending.pop(0)()
                w = wcnt % NW
                wcnt += 1
                if not (FFN_EXP == "skipdma" and p == 1):
                    S.dma("pool", lambda e, w=w, j=j: e.dma_start(
                        out=wgu[w], in_=wgu_dram[j].rearrange("p (k c) -> p k c", k=8), max_dma_last_dim=4096),
                        writes=[wgu_b[w]])
                for s in range(NSUB):
                    g = p * NSUB + s
                    pg, pu = (cnt % 2) * 2, (cnt % 2) * 2 + 1
                    for half, pb in ((0, pg), (1, pu)):
                        for k in range(8):
                            S.op("pe", lambda e, w=w, k=k, half=half, pb=pb, g=g: e.matmul(
                                self.ps(pb), lhsT=wgu[w][:, k, half * 128:(half + 1) * 128],
                                rhs=self.xT[:, k, g * 512:(g + 1) * 512], start=(k == 0), stop=(k == 7)),
                                reads=[wgu_b[w], self.xT_b[g][k]], writes=[self.pbuf[pb]])
                    si = cnt % 2
                    S.op("act", lambda e, si=si, pg=pg: e.activation(out=sg[si], in_=self.ps(pg), func=AF.Silu),
                         reads=[self.pbuf[pg]], writes=[sg_b[si]])
                    S.op("dve", lambda e, si=si, pu=pu, j=j, s=s: e.scalar_tensor_tensor(
                        out=hT[:, j, s * 512:(s + 1) * 512], in0=sg[si], scalar=0.5, in1=self.ps(pu),
                        op0=ALU.mult, op1=ALU.mult),
                        reads=[sg_b[si], self.pbuf[pu]], writes=[hT_b[j][s]])
                    cnt += 1
            pending = None
            for c in range(8):
                w = dcnt % NW
                dcnt += 1
                if not (FFN_EXP == "skipdma" and p == 1):
                    S.dma("pool", lambda e, w=w, c=c: e.dma_start(
                        out=wd[w], in_=wd_dram[c].rearrange("p (j n) -> p j n", j=NF), max_dma_last_dim=4096),
                        writes=[wd_b[w]])
                for s in range(NSUB):
                    g = p * NSUB + s
                    yb = (c * NSUB + s) % 2
                    for j in range(NF):
                        S.op("pe", lambda e, w=w, j=j, s=s, yb=yb: e.matmul(
                            self.ps(yb), lhsT=wd[w][:, j, :], rhs=hT[:, j, s * 512:(s + 1) * 512],
                            start=(j == 0), stop=(j == NF - 1)),
                            reads=[wd_b[w], hT_b[j][s]], writes=[self.pbuf[yb]])
                    if pending is not None:
                        pending[0](pending[1])
                    st = self.resid_chunk(g, c, yb, tmp)
                    pending = (st, 4 + 2 * s)
            pending[0](pending[1])
            for s in range(NSUB):
                ln_pending.extend(self.ln_steps(p * NSUB + s, 4 + 2 * s, ln_idx, tmp))
        for step in ln_pending:
            step()
        S.barrier()
        A.release(m)


def prep_ffn_weights(w_gu, w_down):
    a = w_gu.reshape(8, 128, 2, NF, 128).transpose(3, 1, 0, 2, 4).reshape(NF, 128, 8 * 256)
    b = w_down.reshape(NF, 128, 8, 128).transpose(2, 1, 0, 3).reshape(8, 128, NF * 128)
    return np.ascontiguousarray(a), np.ascontiguousarray(b)


def prep_ln(ln_g, ln_b):
    g = ln_g.reshape(12, 8, 128).transpose(2, 0, 1)
    b = ln_b.reshape(12, 8, 128).transpose(2, 0, 1)
    return np.ascontiguousarray(np.stack([g, b], axis=1).reshape(128, 2 * 12 * 8))


def shard_xT(x):
    outs = []
    for c in range(NCORE):
        b, j = divmod(c, 4)
        idx = core_token_index(j)
        xs = x[b][idx]
        outs.append(np.ascontiguousarray(xs.T.reshape(8, 128, TLOC)))
    return outs


def unshard_xT(outs):
    y = np.empty((NB, T, D), np.float32)
    for c in range(NCORE):
        b, j = divmod(c, 4)
        idx = core_token_index(j)
        y[b][idx] = outs[c].reshape(D, TLOC).T
    return y


MLA_SCALE = 96.0 ** -0.5
MEM_SCALE = 64.0 ** -0.5
ROPE_THETA = 10000.0
CAST = dict(max_dma_last_dim=4096)
DEBUG_STOP = None
_DBG = None
FFN_EXP = None
_TRACE = False


def rstd_from_psum(P, sbank, out, out_b, mult, eps):
    S = P.S
    S.I("dve", "tensor_scalar", [P.pbuf[sbank]], [out_b], out=out, in0=P.ps(sbank), scalar1=mult, scalar2=eps,
        op0=ALU.mult, op1=ALU.add)
    S.I("act", "activation", [out_b], [out_b], out=out, in_=out, func=AF.Sqrt)
    S.I("dve", "reciprocal", [out_b], [out_b], out=out, in_=out)


def mem_kv(P, dr, A):
    S = P.S
    memT = A.alloc(8 * 256, BF16).rearrange("p (k m) -> p k m", k=8)
    memT_b = Buf("memT")
    S.DMA("pool", [], [memT_b], out=memT, in_=dr["memT"].rearrange("k p m -> p k m"), **CAST)
    w_mkv = A.alloc(8 * 512, BF16).rearrange("p (k c) -> p k c", k=8)
    w_mkv_b = Buf("w_mkv")
    S.DMA("pool", [], [w_mkv_b], out=w_mkv, in_=dr["w_mkv"].rearrange("p (k c) -> p k c", k=8), **CAST)
    mkv = A.alloc(2048, BF16)
    mkv_b = Buf("mkv")
    S.I("dve", "memset", [], [mkv_b], ap=mkv, constant=0.0)
    kpad = mkv[:, 0:1024].rearrange("p (h m) -> p h m", h=4)
    vaug = mkv[:, 1024:2048].rearrange("p (h b c) -> p h b c", h=4, b=2)
    for pr in range(2):
        for k in range(8):
            S.I("pe", "matmul", [w_mkv_b, memT_b], [P.pbuf[0]], out=P.ps(0)[:, 0:256],
                lhsT=w_mkv[:, k, pr * 128:(pr + 1) * 128], rhs=memT[:, k, :], start=(k == 0), stop=(k == 7))
        S.I("act", "activation", [P.pbuf[0]], [mkv_b], out=kpad[0:64, 2 * pr, :], in_=P.ps(0)[0:64, 0:256], func=AF.Copy)
        S.I("act", "activation", [P.pbuf[0]], [mkv_b], out=kpad[64:128, 2 * pr + 1, :], in_=P.ps(0)[64:128, 0:256], func=AF.Copy)
    for kb in range(2):
        for k in range(8):
            S.I("pe", "matmul", [w_mkv_b, memT_b], [P.pbuf[1]], out=P.ps(1)[:, 0:256],
                lhsT=memT[:, k, kb * 128:(kb + 1) * 128], rhs=w_mkv[:, k, 256:512], start=(k == 0), stop=(k == 7))
        for h in range(4):
            c0 = 0 if h % 2 == 0 else 64
            S.I("act", "activation", [P.pbuf[1]], [mkv_b], out=vaug[:, h, kb, c0:c0 + 64],
                in_=P.ps(1)[:, h * 64:(h + 1) * 64], func=AF.Copy)
    for h in range(4):
        oc = 64 if h % 2 == 0 else 0
        S.I("dve", "memset", [], [mkv_b], ap=vaug[:, h, :, oc:oc + 1], constant=1.0)
    S.DMA("sp", [mkv_b], [], out=dr["memkv"], in_=mkv)


def mla_pre(P, dr):
    S, A = P.S, P.A
    m = A.mark()
    w_in = A.alloc(8 * 672, BF16).rearrange("p (k c) -> p k c", k=8)
    w_in_b = Buf("w_in")
    S.DMA("pool", [], [w_in_b], out=w_in, in_=dr["w_in"].rearrange("p (k c) -> p k c", k=8), **CAST)
    w_rot = A.alloc(8 * 128, BF16).rearrange("p (k c) -> p k c", k=8)
    w_rot_b = Buf("w_rot")
    S.I("dve", "memset", [], [w_rot_b], ap=w_rot, constant=0.0)
    S.I("dve", "tensor_scalar", [w_in_b], [w_rot_b], out=w_rot[:, :, 0:16], in0=w_in[:, :, 400:416], scalar1=-1.0,
        scalar2=None, op0=ALU.mult)
    S.I("dve", "tensor_copy", [w_in_b], [w_rot_b], out=w_rot[:, :, 16:32], in_=w_in[:, :, 384:400])
    w_uq = A.alloc(2 * 1152, BF16).rearrange("p (k c) -> p k c", k=2)
    w_uq_b = Buf("w_uq")
    S.DMA("pool", [], [w_uq_b], out=w_uq, in_=dr["w_uq"].rearrange("p (k c) -> p k c", k=2), **CAST)
    w_uqr = A.alloc(2 * 1152, BF16).rearrange("p (k c) -> p k c", k=2)
    w_uqr_b = Buf("w_uqr")
    S.I("dve", "memset", [], [w_uqr_b], ap=w_uqr, constant=0.0)
    for k in range(2):
        src = w_uq[:, k, :].rearrange("p (h c) -> p h c", h=12)
        dst = w_uqr[:, k, :].rearrange("p (h c) -> p h c", h=12)
        S.I("dve", "tensor_scalar", [w_uq_b], [w_uqr_b], out=dst[:, :, 64:80], in0=src[:, :, 80:96], scalar1=-1.0,
            scalar2=None, op0=ALU.mult)
        S.I("dve", "tensor_copy", [w_uq_b], [w_uqr_b], out=dst[:, :, 80:96], in_=src[:, :, 64:80])
    ng = A.alloc(3, F32)
    ng_b = Buf("ng")
    S.DMA("sp", [], [ng_b], out=ng, in_=dr["ng"])
    tabc = A.alloc(TLOC, BF16)
    tabs = A.alloc(TLOC, BF16)
    tab_b = Buf("tab")
    S.DMA("pool", [], [tab_b], out=tabc, in_=dr["tabc"], **CAST)
    S.DMA("pool", [], [tab_b], out=tabs, in_=dr["tabs"], **CAST)
    onesb = A.alloc(128, BF16)
    onesb_b = Buf("onesb")
    S.I("dve", "memset", [], [onesb_b], ap=onesb, constant=1.0)

    if DEBUG_STOP == "setup":
        S.barrier(); A.release(m); return
    mem_kv(P, dr, A)
    if DEBUG_STOP == "memkv":
        S.barrier(); A.release(m); return

    cqn = A.alloc(2 * TLOC, BF16).rearrange("p (k t) -> p k t", k=2)
    cqn_b = [Buf("cqn%d" % g) for g in range(NG)]
    cq_sb = A.alloc(2 * 512, F32).rearrange("p (k t) -> p k t", k=2)
    cq_sb_b = Buf("cq_sb")
    sqb = [A.alloc(512, BF16) for _ in range(2)]
    sqb_b = [Buf("sqb0"), Buf("sqb1")]
    rr = A.alloc(512, F32)
    rr_b = Buf("rr")
    ckv_sb = A.alloc(512, F32)
    ckv_sb_b = Buf("ckv_sb")
    latc = A.alloc(TLOC, BF16)
    latc_b = [Buf("latc%d" % g) for g in range(NG)]
    latr = A.alloc(TLOC, BF16)
    latr_b = [Buf("latr%d" % g) for g in range(NG)]
    qmT = A.alloc(2 * TLOC, BF16).rearrange("p (k t) -> p k t", k=2)
    qmT_b = [Buf("qmT%d" % g) for g in range(NG)]
    t1 = [A.alloc(512, F32) for _ in range(2)]
    t1_b = [Buf("t1a"), Buf("t1b")]
    t2 = [A.alloc(512, F32) for _ in range(2)]
    t2_b = [Buf("t2a"), Buf("t2b")]
    NQ = 3
    qst = [A.alloc(512, BF16) for _ in range(NQ)]
    qst_b = [Buf("qst%d" % i) for i in range(NQ)]

    def proj(g, bank, wt, wb, c0, c1):
        tk = slice(g * 512, (g + 1) * 512)
        for k in range(8):
            S.I("pe", "matmul", [wb, P.xT_b[g][k]], [P.pbuf[bank]], out=P.ps(bank), lhsT=wt[:, k, c0:c1],
                rhs=P.xT[:, k, tk], start=(k == 0), stop=(k == 7))

    for g in range(NG):
        tk = slice(g * 512, (g + 1) * 512)
        for k2 in range(2):
            proj(g, k2, w_in, w_in_b, k2 * 128, (k2 + 1) * 128)
            S.I("act", "activation", [P.pbuf[k2]], [sqb_b[k2]], out=sqb[k2], in_=P.ps(k2), func=AF.Square)
            S.I("act", "activation", [P.pbuf[k2]], [cq_sb_b], out=cq_sb[:, k2, :], in_=P.ps(k2), func=AF.Copy)
        for k2 in range(2):
            S.I("pe", "matmul", [onesb_b, sqb_b[k2]], [P.pbuf[2]], out=P.ps(2), lhsT=onesb, rhs=sqb[k2],
                start=(k2 == 0), stop=(k2 == 1))
        rstd_from_psum(P, 2, rr, rr_b, 1.0 / 256.0, RMS_EPS)
        for k2 in range(2):
            S.I("dve", "scalar_tensor_tensor", [cq_sb_b, ng_b, rr_b], [cqn_b[g]], out=cqn[:, k2, tk], in0=cq_sb[:, k2, :],
                scalar=ng[:, k2:k2 + 1], in1=rr, op0=ALU.mult, op1=ALU.mult)
        proj(g, 3, w_in, w_in_b, 256, 384)
        S.I("act", "activation", [P.pbuf[3]], [sqb_b[0]], out=sqb[0], in_=P.ps(3), func=AF.Square)
        S.I("act", "activation", [P.pbuf[3]], [ckv_sb_b], out=ckv_sb, in_=P.ps(3), func=AF.Copy)
        S.I("pe", "matmul", [onesb_b, sqb_b[0]], [P.pbuf[2]], out=P.ps(2), lhsT=onesb, rhs=sqb[0], start=True, stop=True)
        rstd_from_psum(P, 2, rr, rr_b, 1.0 / 128.0, RMS_EPS)
        S.I("dve", "scalar_tensor_tensor", [ckv_sb_b, ng_b, rr_b], [latc_b[g]], out=latc[:, tk], in0=ckv_sb,
            scalar=ng[:, 2:3], in1=rr, op0=ALU.mult, op1=ALU.mult)
        S.DMA("sp", [latc_b[g]], [], out=dr["lat_c"][:, tk], in_=latc[:, tk])
        proj(g, 4, w_in, w_in_b, 384, 512)
        proj(g, 5, w_rot, w_rot_b, 0, 128)
        S.I("dve", "tensor_tensor", [P.pbuf[4], tab_b], [t1_b[0]], out=t1[0][0:32, :], in0=P.ps(4)[0:32, :],
            in1=tabc[0:32, tk], op=ALU.mult)
        S.I("dve", "tensor_tensor", [P.pbuf[5], tab_b], [t2_b[0]], out=t2[0][0:32, :], in0=P.ps(5)[0:32, :],
            in1=tabs[0:32, tk], op=ALU.mult)
        S.I("dve", "tensor_tensor", [t1_b[0], t2_b[0]], [latr_b[g]], out=latr[0:32, tk], in0=t1[0][0:32, :],
            in1=t2[0][0:32, :], op=ALU.add)
        S.DMA("sp", [latr_b[g]], [], out=dr["lat_r"][:, tk], in_=latr[0:32, tk])
        for k2 in range(2):
            proj(g, 6 + k2, w_in, w_in_b, 416 + k2 * 128, 416 + (k2 + 1) * 128)
            S.I("act", "activation", [P.pbuf[6 + k2]], [qmT_b[g]], out=qmT[:, k2, tk], in_=P.ps(6 + k2), func=AF.Copy)
            S.DMA("sp", [qmT_b[g]], [], out=dr["qm"][k2, :, tk], in_=qmT[:, k2, tk])
    if DEBUG_STOP == "groups":
        S.barrier(); A.release(m); return
    qi = 0
    for g in range(NG):
        tk = slice(g * 512, (g + 1) * 512)
        for h in range(12):
            b0 = (qi % 2) * 2
            for bank, wt, wb in ((b0, w_uq, w_uq_b), (b0 + 1, w_uqr, w_uqr_b)):
                for k in range(2):
                    S.I("pe", "matmul", [wb, cqn_b[g]], [P.pbuf[bank]], out=P.ps(bank)[0:96, :],
                        lhsT=wt[:, k, h * 96:(h + 1) * 96], rhs=cqn[:, k, tk], start=(k == 0), stop=(k == 1))
            q = qst[qi % NQ]
            qb = qst_b[qi % NQ]
            i2 = qi % 2
            S.I("act", "activation", [P.pbuf[b0]], [qb], out=q[0:64, :], in_=P.ps(b0)[0:64, :], func=AF.Copy)
            if DEBUG_STOP != "qnorope":
                S.I("dve", "tensor_tensor", [P.pbuf[b0], tab_b], [t1_b[i2]], out=t1[i2][64:96, :], in0=P.ps(b0)[64:96, :],
                    in1=tabc[64:96, tk], op=ALU.mult)
                S.I("dve", "tensor_tensor", [P.pbuf[b0 + 1], tab_b], [t2_b[i2]], out=t2[i2][64:96, :],
                    in0=P.ps(b0 + 1)[64:96, :], in1=tabs[64:96, tk], op=ALU.mult)
                if False:
                    S.I("pool", "tensor_tensor", [t1_b[i2], t2_b[i2]], [qb], out=q[64:96, :], in0=t1[i2][64:96, :],
                        in1=t2[i2][64:96, :], op=ALU.add)
                else:
                    S.I("dve", "tensor_tensor", [t1_b[i2], t2_b[i2]], [qb], out=q[64:96, :], in0=t1[i2][64:96, :],
                        in1=t2[i2][64:96, :], op=ALU.add)
            if DEBUG_STOP != "qnodma":
                S.DMA("sp", [qb], [], out=dr["q_scr"][h, :, tk], in_=q[0:96, :])
            qi += 1
            if DEBUG_STOP == "q1":
                break
        if DEBUG_STOP == "q1":
            break
    S.barrier()
    A.release(m)


def resid_ln_from_oT(P, dr_w_out, ln_idx):
    S, A = P.S, P.A
    m = A.mark()
    w_out = A.alloc(8 * 1024, BF16).rearrange("p (k c) -> p k c", k=8)
    w_out_b = Buf("w_out")
    S.DMA("pool", [], [w_out_b], out=w_out, in_=dr_w_out.rearrange("p (k c) -> p k c", k=8), **CAST)
    tmp = P.alloc_ln_tmp()
    lnq = []
    for g in range(NG):
        tk = slice(g * 512, (g + 1) * 512)
        sb = 4 + 2 * (g % 2)
        pending = None
        for c in range(8):
            yb = c % 2
            for _ in range(2 if c == 0 else 1):
                if lnq:
                    lnq.pop(0)()
            for pr in range(8):
                S.I("pe", "matmul", [w_out_b, P.xT_b[g][pr]], [P.pbuf[yb]], out=P.ps(yb),
                    lhsT=w_out[:, pr, c * 128:(c + 1) * 128], rhs=P.xT[:, pr, tk], start=(pr == 0), stop=(pr == 7))
            if pending is not None:
                pending(sb)
            pending = P.resid_chunk(g, c, yb, tmp)
        pending(sb)
        for step in lnq:
            step()
        lnq = list(P.ln_steps(g, sb, ln_idx, tmp))
    for step in lnq:
        step()
    S.barrier()
    A.release(m)


DEFER = 6


def pend_tick(st):
    pend = st.setdefault("pending", [])
    for ent in list(pend):
        ent[0] -= 1
        if ent[0] <= 0:
            pend.remove(ent)
            ent[2]()


def pend_flush(st, bank=None, keep=0):
    pend = st.setdefault("pending", [])
    for ent in list(pend):
        if bank is None or ent[1] == bank:
            if bank is None and len(pend) <= keep:
                break
            pend.remove(ent)
            ent[2]()


def pend_add(st, ob, fn, maxpend):
    pend = st.setdefault("pending", [])
    while len(pend) >= maxpend:
        ent = pend.pop(0)
        ent[2]()
    pend.append([DEFER, ob, fn])


def attn_epilogue(P, ob, osb, osb_b, sel, sel_b, par, pair, g):
    S = P.S
    tk = slice(g * 512, (g + 1) * 512)
    dr_ = 64 if par == 0 else 0
    r0 = 0 if par == 0 else 64
    S.I("act", "activation", [P.pbuf[ob]], [osb_b], out=osb, in_=P.ps(ob), func=AF.Copy)
    S.I("dve", "reciprocal", [osb_b], [osb_b], out=osb[dr_:dr_ + 1, :], in_=osb[dr_:dr_ + 1, :])

    def rest():
        S.I("pe", "matmul", [sel_b, osb_b], [P.pbuf[7]], out=P.ps(7), lhsT=sel[:, par, :], rhs=osb, start=True, stop=True)
        S.I("dve", "tensor_tensor", [osb_b, P.pbuf[7]], [P.xT_b[g][pair]], out=P.xT[r0:r0 + 64, pair, tk],
            in0=osb[r0:r0 + 64, :], in1=P.ps(7)[r0:r0 + 64, :], op=ALU.mult)
    return rest


def attn_slots(P, slots, kt_fn, q_fn, v_fn, scale, mask_fn, PT, PT_b, epi_fn, st):
    S = P.S
    n = len(slots)
    info = [None] * n
    SB = st.get("Sbanks", [0, 1, 2])
    L = len(SB) - 1
    for i in range(n + L):
        pend_tick(st)
        if st.get("bg") is not None and i % 5 == 2:
            step = next(st["bg"], None)
            if step is None:
                st["bg"] = None
            else:
                step()
        if i < n:
            g, kb, first, last = slots[i]
            sb = SB[st["s"] % len(SB)]
            st["s"] += 1
            pi = st["p"] % len(PT)
            st["p"] += 1
            kt, kt_bufs = kt_fn(kb)
            q, q_bufs = q_fn(g)
            S.I("pe", "matmul", kt_bufs + q_bufs, [P.pbuf[sb]], out=P.ps(sb), lhsT=kt, rhs=q, start=True, stop=True)
            S.I("act", "activation", [P.pbuf[sb]], [PT_b[pi]], out=PT[pi], in_=P.ps(sb), func=AF.Exp, scale=scale)
            if mask_fn is not None:
                mask_fn(g, kb, PT[pi], PT_b[pi])
            info[i] = pi
        j = i - L
        if j >= 0:
            g, kb, first, last = slots[j]
            if first:
                st["ob"] = 3 + (st["o"] % 2)
                st["o"] += 1
                pend_flush(st, bank=st["ob"])
            ob = st["ob"]
            v, v_bufs = v_fn(kb)
            pi = info[j]
            S.I("pe", "matmul", v_bufs + [PT_b[pi]], [P.pbuf[ob]], out=P.ps(ob), lhsT=v, rhs=PT[pi], start=first, stop=last)
            if last:
                rest = epi_fn(g, ob)
                if rest is not None:
                    pend_add(st, ob, rest, st.get("maxpend", 1))


def mem_heads(P, dr, kpad, vaug, mkv_b, QT, QT_b, PT, PT_b, osb, osb_b, sel, sel_b, st):
    S = P.S
    for hm in range(4):
        par = hm % 2
        qi = hm % 2
        S.DMA("sp", [], [QT_b[qi]], out=QT[qi], in_=dr["qm"][hm // 2])
        slots = []
        for g in range(NG):
            for kb in range(2):
                slots.append((g, kb, kb == 0, kb == 1))

        def kt_fn(kb, hm=hm):
            return kpad[:, hm, kb * 128:(kb + 1) * 128], [mkv_b]

        def q_fn(g, qi=qi):
            return QT[qi][:, g * 512:(g + 1) * 512], [QT_b[qi]]

        def v_fn(kb, hm=hm):
            return vaug[:, hm, kb, :], [mkv_b]

        def epi_fn(g, ob, par=par, pair=6 + hm // 2):
            e = st["e"] % len(osb)
            st["e"] += 1
            return attn_epilogue(P, ob, osb[e], osb_b[e], sel, sel_b, par, pair, g)

        attn_slots(P, slots, kt_fn, q_fn, v_fn, MEM_SCALE, None, PT, PT_b, epi_fn, st)


def mla_attn(P, dr):
    S, A = P.S, P.A
    m = A.mark()
    ckv = A.alloc(T, BF16)
    ckv_b = [Buf("ckv%d" % i) for i in range(16)]
    KT2 = [A.alloc(T, BF16) for _ in range(2)]
    KT2_b = [[Buf("KT%d_%d" % (p_, i)) for i in range(16)] for p_ in range(2)]
    KTr_b = Buf("KTr")
    for i in range(16):
        ks = slice(i * 512, (i + 1) * 512)
        S.DMA("sp", [], [ckv_b[i]], out=ckv[:, ks], in_=dr["ckv_all"][:, ks])
    for p_ in range(2):
        for i in range(4):
            ks = slice(i * 2048, (i + 1) * 2048)
            S.DMA("sp", [], [KTr_b], out=KT2[p_][64:96, ks], in_=dr["kr_all"][:, ks])
    w_ukv = A.alloc(1536, BF16)
    w_ukv_b = Buf("w_ukv")
    S.DMA("pool", [], [w_ukv_b], out=w_ukv, in_=dr["w_ukv"], **CAST)
    V = [A.alloc(64 * 128, BF16).rearrange("p (b c) -> p b c", b=64) for _ in range(2)]
    V_b = [[Buf("V%d_%d" % (p_, i)) for i in range(8)] for p_ in range(2)]
    for p_ in range(2):
        S.I("pool", "memset", [], V_b[p_], ap=V[p_], constant=0.0)
        oc = 64 if p_ == 0 else 0
        S.I("pool", "memset", [], V_b[p_], ap=V[p_][:, :, oc:oc + 1], constant=1.0)
    mkv = A.alloc(2048, BF16)
    mkv_b = Buf("mkv")
    S.DMA("sp", [], [mkv_b], out=mkv, in_=dr["memkv"])
    kpad = mkv[:, 0:1024].rearrange("p (h m) -> p h m", h=4)
    vaug = mkv[:, 1024:2048].rearrange("p (h b c) -> p h b c", h=4, b=2)
    QT = [A.alloc(TLOC, BF16) for _ in range(2)]
    QT_b = [Buf("QT0"), Buf("QT1")]
    qrel = A.alloc(512, F16)
    krel = A.alloc(16, F32)
    sel = A.alloc(256, F32).rearrange("p (a c) -> p a c", a=2)
    cst_b = Buf("cst")
    S.DMA("sp", [], [cst_b], out=qrel, in_=dr["qrel"])
    S.DMA("sp", [], [cst_b], out=krel, in_=dr["krel"])
    S.DMA("sp", [], [cst_b], out=sel, in_=dr["sel"].rearrange("p (a c) -> p a c", a=2))
    PT = [A.alloc(512, BF16) for _ in range(4)]
    PT_b = [Buf("PT%d" % i) for i in range(4)]
    osb = [A.alloc(512, F32) for _ in range(4)]
    osb_b = [Buf("osb%d" % i) for i in range(4)]
    st = {"s": 0, "p": 0, "o": 0, "ob": 3, "e": 0, "prod": 0, "maxpend": 3, "Sbanks": [0, 1, 2, 5]}

    def mask_fn(g, kb, pt, pt_b):
        i = kb - 16 * g
        if i < 0:
            return
        S.I("dve", "scalar_tensor_tensor", [cst_b, pt_b], [pt_b], out=pt, in0=qrel, scalar=krel[:, i:i + 1], in1=pt,
            op0=ALU.is_ge, op1=ALU.mult)

    def prod_gen(h):
        par = h % 2
        Vh, Vh_b = V[par], V_b[par]
        KT, KT_b = KT2[par], KT2_b[par]
        c0 = 0 if par == 0 else 64
        for kc in range(16):
            def stepk(kc=kc):
                pb = 6
                ks = slice(kc * 512, (kc + 1) * 512)
                S.I("pe", "matmul", [w_ukv_b, ckv_b[kc]], [P.pbuf[pb]], out=P.ps(pb), lhsT=w_ukv[:, h * 128:(h + 1) * 128],
                    rhs=ckv[:, ks], start=True, stop=True)
                if kc % 2 == 0:
                    S.I("act", "activation", [P.pbuf[pb]], [KT_b[kc]], out=KT[0:64, ks], in_=P.ps(pb)[0:64, :], func=AF.Copy)
                else:
                    S.I("dve", "tensor_copy", [P.pbuf[pb]], [KT_b[kc]], out=KT[0:64, ks], in_=P.ps(pb)[0:64, :])
            yield stepk
        for kg in range(8):
            def stepv(kg=kg):
                pb = 6
                for i in range(8):
                    kb = kg * 8 + i
                    S.I("pe", "matmul", [w_ukv_b, ckv_b[kb // 4]], [P.pbuf[pb]], out=P.ps(pb)[:, i * 64:(i + 1) * 64],
                        lhsT=ckv[:, kb * 128:(kb + 1) * 128], rhs=w_ukv[:, h * 128 + 64:(h + 1) * 128], start=True, stop=True)
                src_ = P.ps(pb).rearrange("p (b c) -> p b c", b=8)
                dst = Vh[:, kg * 8:(kg + 1) * 8, c0:c0 + 64]
                if kg % 2 == 0:
                    S.I("dve", "tensor_copy", [P.pbuf[pb]], [Vh_b[kg]], out=dst, in_=src_)
                else:
                    S.I("act", "activation", [P.pbuf[pb]], [Vh_b[kg]], out=dst, in_=src_, func=AF.Copy)
            yield stepv

    for step in prod_gen(0):
        step()
    for h in range(12):
        par = h % 2
        Vh, Vh_b = V[par], V_b[par]
        KT, KT_b = KT2[par], KT2_b[par]
        st["bg"] = prod_gen(h + 1) if h + 1 < 12 else None
        qi = h % 2
        S.DMA("sp", [], [QT_b[qi]], out=QT[qi][0:96, :], in_=dr["q_scr"][h])
        slots = []
        for g in range(NG):
            nkb = 16 * g + 16
            for kb in range(nkb):
                slots.append((g, kb, kb == 0, kb == nkb - 1))

        def kt_fn(kb, KT=KT, KT_b=KT_b):
            return KT[0:96, kb * 128:(kb + 1) * 128], [KT_b[kb // 4], KTr_b]

        def q_fn(g, qi=qi):
            return QT[qi][0:96, g * 512:(g + 1) * 512], [QT_b[qi]]

        def v_fn(kb, Vh=Vh, Vh_b=Vh_b):
            return Vh[:, kb, :], [Vh_b[kb // 8]]

        def epi_fn(g, ob, par=par, pair=h // 2):
            e = st["e"] % len(osb)
            st["e"] += 1
            return attn_epilogue(P, ob, osb[e], osb_b[e], sel, cst_b, par, pair, g)

        attn_slots(P, slots, kt_fn, q_fn, v_fn, MLA_SCALE, mask_fn, PT, PT_b, epi_fn, st)
        if st["bg"] is not None:
            for step in st["bg"]:
                step()
            st["bg"] = None
    mem_heads(P, dr, kpad, vaug, mkv_b, QT, QT_b, PT, PT_b, osb, osb_b, sel, cst_b, st)
    pend_flush(st)
    S.barrier()
    A.release(m)


def kmajor(w):
    K_, N_ = w.shape
    return np.ascontiguousarray(w.reshape(K_ // 128, 128, N_).transpose(1, 0, 2).reshape(128, (K_ // 128) * N_))


def rope_tables(j):
    pos = core_token_index(j).astype(np.float32)
    freq = ROPE_THETA ** (-np.arange(16, dtype=np.float32) / 16)
    ang = pos[None, :] * freq[:, None]
    c = np.zeros((128, TLOC), np.float32)
    s = np.zeros((128, TLOC), np.float32)
    for base in (0, 64):
        c[base:base + 16] = np.cos(ang)
        c[base + 16:base + 32] = np.cos(ang)
        s[base:base + 16] = np.sin(ang)
        s[base + 16:base + 32] = np.sin(ang)
    return c, s


def attn_consts(j):
    qpos = core_token_index(j)[:512]
    qrel = np.broadcast_to(qpos[None, :].astype(np.float16), (128, 512)).copy()
    krel = (128.0 * np.arange(16)[None, :] + np.arange(128)[:, None]).astype(np.float32)
    sel = np.zeros((128, 2, 128), np.float32)
    sel[64, 0, :] = 1.0
    sel[0, 1, :] = 1.0
    return qrel, krel, sel.reshape(128, 256)


U8 = mybir.dt.uint8
NSA_SCALE = 0.125
NEG_BIG = -1.0e30
MASK_BIG = 3.0e38


def run_slots(P, slots, PT, PT_b, st):
    S = P.S
    n = len(slots)
    info = [None] * n
    SB = st.get("Sbanks", [0, 1, 2])
    L = len(SB) - 1
    for i in range(n + L):
        pend_tick(st)
        if i < n:
            s = slots[i]
            if s.get("pre") is not None:
                s["pre"]()
            sb = SB[st["s"] % len(SB)]
            st["s"] += 1
            if s.get("pt") is not None:
                pt, ptb = s["pt"], s["ptb"]
            else:
                pi = st["p"] % len(PT)
                st["p"] += 1
                pt, ptb = PT[pi], PT_b[pi]
            S.I("pe", "matmul", s["kb"] + s["qb"], [P.pbuf[sb]], out=P.ps(sb), lhsT=s["k"], rhs=s["q"], start=True, stop=True)
            S.I("act", "activation", [P.pbuf[sb]], [ptb], out=pt, in_=P.ps(sb), func=AF.Exp, scale=s["scale"])
            if s.get("post") is not None:
                s["post"](pt, ptb)
            info[i] = (pt, ptb)
        j = i - L
        if j >= 0:
            s = slots[j]
            pt, ptb = info[j]
            ob = s["ob"]
            if s["first"]:
                pend_flush(st, bank=ob)
            S.I("pe", "matmul", s["vb"] + [ptb], [P.pbuf[ob]], out=P.ps(ob)[0:s.get("m", 96), :], lhsT=s["v"], rhs=pt,
                start=s["first"], stop=s["last"])
            if s["last"] and s.get("epi") is not None:
                rest = s["epi"]()
                if rest is not None:
                    pend_add(st, ob, rest, 3)


def nsa_kv_stage(P, dr):
    S, A = P.S, P.A
    m = A.mark()
    w_kv = A.alloc(8 * 768, BF16).rearrange("p (k c) -> p k c", k=8)
    w_kv_b = Buf("w_kv")
    S.DMA("pool", [], [w_kv_b], out=w_kv, in_=dr["w_kv"].rearrange("p (k c) -> p k c", k=8), **CAST)
    stg = [A.alloc(512, BF16) for _ in range(3)]
    stg_b = [Buf("stg%d" % i) for i in range(3)]
    n = 0
    for g in range(NG):
        tk = slice(g * 512, (g + 1) * 512)
        for c in range(6):
            bank = n % 4
            for k in range(8):
                S.I("pe", "matmul", [w_kv_b, P.xT_b[g][k]], [P.pbuf[bank]], out=P.ps(bank), lhsT=w_kv[:, k, c * 128:(c + 1) * 128],
                    rhs=P.xT[:, k, tk], start=(k == 0), stop=(k == 7))
            si = n % 3
            if n % 2 == 0:
                S.I("act", "activation", [P.pbuf[bank]], [stg_b[si]], out=stg[si], in_=P.ps(bank), func=AF.Copy)
            else:
                S.I("dve", "tensor_copy", [P.pbuf[bank]], [stg_b[si]], out=stg[si], in_=P.ps(bank))
            S.DMA("sp", [stg_b[si]], [], out=dr["kvT"][c * 128:(c + 1) * 128, tk], in_=stg[si])
            n += 1
    for t in range(16):
        g = t // 4
        for bi, br in enumerate((1, 2)):
            bank = 4 + (n % 2)
            for k in range(8):
                S.I("pe", "matmul", [w_kv_b, P.xT_b[g][k]], [P.pbuf[bank]], out=P.ps(bank)[:, 0:128],
                    lhsT=P.xT[:, k, t * 128:(t + 1) * 128], rhs=w_kv[:, k, br * 256 + 128:br * 256 + 256],
                    start=(k == 0), stop=(k == 7))
            si = n % 3
            S.I("act", "activation", [P.pbuf[bank]], [stg_b[si]], out=stg[si][:, 0:128], in_=P.ps(bank)[:, 0:128], func=AF.Copy)
            S.DMA("sp", [stg_b[si]], [], out=dr["vtok"][t * 128:(t + 1) * 128, bi * 128:(bi + 1) * 128], in_=stg[si][:, 0:128])
            n += 1
    S.barrier()
    A.release(m)


def nsa_compress(P, dr, KTc, Vc, kc_b):
    S, A = P.S, P.A
    m = A.mark()
    S.I("dve", "memset", [], [kc_b], ap=KTc, constant=0.0)
    S.I("dve", "memset", [], [kc_b], ap=Vc, constant=0.0)
    S.I("dve", "memset", [], [kc_b], ap=Vc[:, :, :, 64:65], constant=1.0)
    for g in range(2):
        S.DMA("sp", [], [kc_b], out=KTc[64:71, g, :], in_=dr["kaug_c"])
    zp = A.alloc(T, BF16)
    zp_b = Buf("zp")
    w1 = A.alloc(16 * 256, BF16).rearrange("p (l c) -> p l c", l=16)
    w1_b = Buf("w1")
    w2 = A.alloc(2 * 128, BF16).rearrange("p (h c) -> p h c", h=2)
    w2_b = Buf("w2")
    pp = A.alloc(16, BF16)
    b1 = A.alloc(2, F32)
    posb = A.alloc(2, F32)
    pb_b = Buf("posb")
    hid = [A.alloc(512, BF16) for _ in range(2)]
    hid_b = [Buf("hid0"), Buf("hid1")]
    xg = A.alloc(512, F32)
    ug = A.alloc(512, F32)
    xg_b = Buf("xg")
    S.I("dve", "memset", [], [zp_b], ap=zp[:, T - 64:T], constant=0.0)
    for hc in range(2):
        S.I("dve", "memset", [], [hid_b[hc]], ap=hid[hc], constant=0.0)
    for j in range(2):
        S.I("dve", "memset", [], [w2_b], ap=w2, constant=0.0)
        S.DMA("pool", [], [w1_b], out=w1, in_=dr["cmp_w1"][j].rearrange("p (l c) -> p l c", l=16), **CAST)
        S.DMA("pool", [], [w2_b], out=w2[:, :, 0:64], in_=dr["cmp_w2"][j].rearrange("p (h c) -> p h c", h=2), **CAST)
        S.DMA("pool", [], [pb_b], out=pp, in_=dr["cmp_pp"][j], **CAST)
        S.DMA("sp", [], [pb_b], out=b1, in_=dr["cmp_b1"][j])
        for hc in range(2):
            for l in range(16):
                S.I("pe", "matmul", [w1_b, pb_b], [P.pbuf[6]], out=P.ps(6)[:, 0:1], lhsT=w1[:, l, hc * 128:(hc + 1) * 128],
                    rhs=pp[:, l:l + 1], start=(l == 0), stop=(l == 15))
            S.I("dve", "tensor_tensor", [P.pbuf[6], pb_b], [pb_b], out=posb[:, hc:hc + 1], in0=P.ps(6)[:, 0:1],
                in1=b1[:, hc:hc + 1], op=ALU.add)
        for g in range(2):
            r0 = j * 128 + g * 64
            for q4 in range(4):
                ks = slice(q4 * 2048, (q4 + 1) * 2048)
                S.DMA("sp", [], [zp_b], out=zp[0:64, ks], in_=dr["kvT_all"][r0:r0 + 64, ks])
                hi = min((q4 + 1) * 2048, T - 1)
                S.DMA("sp", [], [zp_b], out=zp[64:128, q4 * 2048:hi], in_=dr["kvT_all"][r0:r0 + 64, q4 * 2048 + 1:hi + 1])
            for hc in range(2):
                bank = hc
                for l in range(16):
                    rhs = zp[:, 2 * l:2 * l + 16 * 510 + 1:16]
                    S.I("pe", "matmul", [w1_b, zp_b], [P.pbuf[bank]], out=P.ps(bank)[:, 0:511], lhsT=w1[:, l, hc * 128:(hc + 1) * 128],
                        rhs=rhs, start=(l == 0), stop=(l == 15))
                S.I("dve", "tensor_scalar", [P.pbuf[bank], pb_b], [xg_b], out=xg[:, 0:511], in0=P.ps(bank)[:, 0:511],
                    scalar1=posb[:, hc:hc + 1], scalar2=None, op0=ALU.add)
                S.I("dve", "tensor_tensor", [xg_b], [xg_b], out=ug[:, 0:511], in0=xg[:, 0:511], in1=xg[:, 0:511], op=ALU.mult)
                S.I("dve", "tensor_scalar", [xg_b], [xg_b], out=ug[:, 0:511], in0=ug[:, 0:511], scalar1=0.044715, scalar2=1.0,
                    op0=ALU.mult, op1=ALU.add)
                S.I("dve", "tensor_tensor", [xg_b], [xg_b], out=ug[:, 0:511], in0=ug[:, 0:511], in1=xg[:, 0:511], op=ALU.mult)
                S.I("act", "activation", [xg_b], [xg_b], out=ug[:, 0:511], in_=ug[:, 0:511], func=AF.Tanh, scale=0.7978845608028654)
                S.I("dve", "tensor_scalar", [xg_b], [xg_b], out=ug[:, 0:511], in0=ug[:, 0:511], scalar1=1.0, scalar2=0.5,
                    op0=ALU.add, op1=ALU.mult)
                S.I("dve", "tensor_tensor", [xg_b], [hid_b[hc]], out=hid[hc][:, 0:511], in0=ug[:, 0:511], in1=xg[:, 0:511], op=ALU.mult)
            if j == 0:
                for hc in range(2):
                    S.I("pe", "matmul", [w2_b, hid_b[hc]], [P.pbuf[2]], out=P.ps(2), lhsT=w2[:, hc, :], rhs=hid[hc],
                        start=(hc == 0), stop=(hc == 1))
                S.I("act", "activation", [P.pbuf[2]], [kc_b], out=KTc[0:64, g, 0:511], in_=P.ps(2)[0:64, 0:511], func=AF.Copy)
            else:
                for nb in range(4):
                    for hc in range(2):
                        S.I("pe", "matmul", [w2_b, hid_b[hc]], [P.pbuf[3]], out=P.ps(3)[:, nb * 64:(nb + 1) * 64],
                            lhsT=hid[hc][:, nb * 128:(nb + 1) * 128], rhs=w2[:, hc, 0:64], start=(hc == 0), stop=(hc == 1))
                S.I("act", "activation", [P.pbuf[3]], [kc_b], out=Vc[:, g, :, 0:64],
                    in_=P.ps(3)[:, 0:256].rearrange("p (n c) -> p n c", n=4), func=AF.Copy)
    S.barrier()
    A.release(m)


def nsa_pre(P, dr):
    S, A = P.S, P.A
    m = A.mark()
    w_in = A.alloc(8 * 1060, BF16).rearrange("p (k c) -> p k c", k=8)
    w_in_b = Buf("w_in")
    S.DMA("pool", [], [w_in_b], out=w_in, in_=dr["w_in"].rearrange("p (k c) -> p k c", k=8), **CAST)
    mem_kv(P, dr, A)
    qst = [A.alloc(512, BF16) for _ in range(3)]
    qst_b = [Buf("qst%d" % i) for i in range(3)]
    gt = A.alloc(16 * 36, F32).rearrange("p (t c) -> p t c", t=16)
    gt_b = Buf("gt")
    n = 0
    for g in range(NG):
        tk = slice(g * 512, (g + 1) * 512)
        for hh in range(14):
            c0 = hh * 64 if hh < 12 else 804 + (hh - 12) * 128
            bank = n % 4
            for k in range(8):
                S.I("pe", "matmul", [w_in_b, P.xT_b[g][k]], [P.pbuf[bank]], out=P.ps(bank), lhsT=w_in[:, k, c0:c0 + 128],
                    rhs=P.xT[:, k, tk], start=(k == 0), stop=(k == 7))
            si = n % 3
            if n % 2 == 0:
                S.I("act", "activation", [P.pbuf[bank]], [qst_b[si]], out=qst[si], in_=P.ps(bank), func=AF.Copy)
            else:
                S.I("dve", "tensor_copy", [P.pbuf[bank]], [qst_b[si]], out=qst[si], in_=P.ps(bank))
            if hh < 12:
                S.DMA("sp", [qst_b[si]], [], out=dr["q_scr"][hh, :, tk], in_=qst[si][0:64, :])
            else:
                S.DMA("sp", [qst_b[si]], [], out=dr["qm"][hh - 12, :, tk], in_=qst[si])
            n += 1
    for t in range(16):
        g = t // 4
        bank = 4 + t % 2
        for k in range(8):
            S.I("pe", "matmul", [w_in_b, P.xT_b[g][k]], [P.pbuf[bank]], out=P.ps(bank)[:, 0:36],
                lhsT=P.xT[:, k, t * 128:(t + 1) * 128], rhs=w_in[:, k, 768:804], start=(k == 0), stop=(k == 7))
        S.I("act", "activation", [P.pbuf[bank]], [gt_b], out=gt[:, t, :], in_=P.ps(bank)[:, 0:36], func=AF.Sigmoid)
    S.DMA("sp", [gt_b], [], out=dr["gt"], in_=gt.rearrange("p t c -> p (t c)"))
    S.barrier()
    A.release(m)


def nsa_attn(P, dr, KTc, Vc, kc_b):
    S, A = P.S, P.A
    m = A.mark()
    KTs = A.alloc(T, BF16)
    KTs_b = [Buf("KTs%d" % i) for i in range(4)]
    KTa_b = Buf("KTsaug")
    Vs = A.alloc(64 * 128, BF16).rearrange("p (b c) -> p b c", b=64)
    Vs_b = [Buf("Vs%d" % i) for i in range(4)]
    KTw = A.alloc(8 * 128, BF16)
    KTw_b = Buf("KTw")
    Vw = A.alloc(8 * 96, BF16).rearrange("p (b c) -> p b c", b=8)
    Vw_b = Buf("Vw")
    Wide = A.alloc(T, BF16)
    wmask = A.alloc(8 * 512 // 2, BF16).bitcast(U8).rearrange("p (i n) -> p i n", i=8)
    cmask = A.alloc(4 * 512 // 2, BF16).bitcast(U8).rearrange("p (i n) -> p i n", i=4)
    cmask_b = Buf("cmask")
    qrel = A.alloc(512, F16)
    krel = A.alloc(16, F32)
    sel = A.alloc(256, F32).rearrange("p (a c) -> p a c", a=2)
    agg = A.alloc(4 * 128, BF16).rearrange("p (i n) -> p i n", i=4)
    ident32 = A.alloc(128, F32)
    identb = A.alloc(128, BF16)
    cst_b = Buf("cst")
    wide_b = Buf("wide")
    S.DMA("sp", [], [wide_b], out=Wide, in_=dr["wide"])
    for dst, name in ((qrel, "qrel"), (krel, "krel"), (ident32, "ident32"), (identb, "identb")):
        S.DMA("sp", [], [cst_b], out=dst, in_=dr[name])
    S.DMA("sp", [], [cst_b], out=wmask, in_=dr["wmask"].rearrange("p (i n) -> p i n", i=8))
    S.DMA("sp", [], [cst_b], out=sel, in_=dr["sel"].rearrange("p (a c) -> p a c", a=2))
    S.DMA("sp", [], [cst_b], out=agg, in_=dr["agg"].rearrange("p (i n) -> p i n", i=4))
    selT = A.alloc(512, BF16)
    selT_b = Buf("selT")
    smask = [A.alloc(512, BF16) for _ in range(2)]
    smask_b = [Buf("smask0"), Buf("smask1")]
    PT = [A.alloc(512, BF16) for _ in range(4)]
    PT_b = [Buf("PT%d" % i) for i in range(4)]
    PTcs = [A.alloc(4 * 512, BF16).rearrange("p (i n) -> p i n", i=4) for _ in range(2)]
    PTcs_b = [[Buf("PTc%d_%d" % (k_, i)) for i in range(4)] for k_ in range(2)]
    QT = A.alloc(6 * 512, BF16)
    QTv = QT.rearrange("p (h n) -> p h n", h=6)
    QT_b = Buf("QT")
    gt = A.alloc(16 * 36, F32).rearrange("p (t c) -> p t c", t=16)
    gt_b = Buf("gt")
    S.DMA("sp", [], [gt_b], out=gt, in_=dr["gt"].rearrange("p (t c) -> p t c", t=16))
    osb = [A.alloc(512, BF16) for _ in range(4)]
    osb_b = [Buf("osb%d" % i) for i in range(4)]
    accp = [A.alloc(4 * 128, F32).rearrange("p (t c) -> p t c", t=4) for _ in range(3)]
    accp_b = [Buf("accp%d" % i) for i in range(3)]
    impacc = A.alloc(4 * 128, F32).rearrange("p (t c) -> p t c", t=4)
    impacc_b = Buf("impacc")
    fbias = A.alloc(4 * 128, F32).rearrange("p (t c) -> p t c", t=4)
    fbias_b = Buf("fbias")
    NRD = 8
    rd = [A.alloc(4, F32) for _ in range(NRD)]
    rd_b = [Buf("rd%d" % i) for i in range(NRD)]
    fsc = [A.alloc(4, F32) for _ in range(2)]
    fsc_b = [Buf("fsc0"), Buf("fsc1")]
    impb = A.alloc(128, F32)
    impb2 = A.alloc(128, F32)
    m8 = A.alloc(16, F32)
    thr = A.alloc(1, F32)
    selm = A.alloc(128, BF16)
    tk_b = Buf("topk")
    st = {"s": 0, "p": 0, "o": 0, "ob": 3, "e": 0, "rd": 0, "f": 0, "sm": 0, "Sbanks": [0, 1, 2, 6]}

    S.I("pool", "memset", [], [KTa_b], ap=KTs[64:96, :], constant=0.0)
    S.I("pool", "memset", [], [QT_b], ap=QT[64:96, :], constant=0.0)
    S.DMA("sp", [], [KTa_b], out=KTs[64:71, :], in_=dr["kaug"])
    S.I("pool", "memset", [], Vs_b, ap=Vs, constant=0.0)
    S.I("pool", "memset", [], Vs_b, ap=Vs[:, :, 64:65], constant=1.0)
    S.I("pool", "memset", [], [Vw_b], ap=Vw, constant=0.0)
    S.I("pool", "memset", [], [Vw_b], ap=Vw[:, :, 64:65], constant=1.0)
    S.I("pool", "memset", [], [KTw_b], ap=KTw, constant=0.0)

    def epilogue(ob, hh, br, qg, g, first_branch, defer=True):
        e = st["e"] % 4
        st["e"] += 1
        ri = st["rd"] % NRD
        st["rd"] += 1
        S.I("act", "activation", [P.pbuf[ob]], [osb_b[e]], out=osb[e][0:72, :], in_=P.ps(ob)[0:72, :], func=AF.Copy)

        def rest():
            epilogue_rest(e, ri, hh, br, qg, g, first_branch)
        if defer:
            return ri, rest
        rest()
        return ri, None

    def epilogue_rest(e, ri, hh, br, qg, g, first_branch):
        fi = st["f"] % 2
        st["f"] += 1
        ps7b = P.ps(7).bitcast(BF16)
        ps7 = ps7b[:, 0:288].rearrange("p (t c) -> p t c", t=4)
        for t in range(4):
            S.I("pe", "transpose", [osb_b[e], cst_b], [P.pbuf[7]], out=ps7b[:, t * 72:(t + 1) * 72],
                in_=osb[e][0:72, t * 128:(t + 1) * 128], identity=identb[0:72, 0:72])
        S.I("dve", "tensor_scalar", [P.pbuf[7]], [rd_b[ri]], out=rd[ri], in0=ps7[:, :, 64], scalar1=1e-30, scalar2=None,
            op0=ALU.max)
        S.I("dve", "reciprocal", [rd_b[ri]], [rd_b[ri]], out=rd[ri], in_=rd[ri])
        S.I("dve", "tensor_tensor", [rd_b[ri], gt_b], [fsc_b[fi]], out=fsc[fi], in0=rd[ri],
            in1=gt[:, 4 * qg:4 * qg + 4, hh * 3 + br], op=ALU.mult)
        pslot = (hh // 2) - 3 * g
        c0 = 64 * (hh % 2)
        for t in range(4):
            dst = accp[pslot][:, t, c0:c0 + 64]
            if first_branch:
                S.I("dve", "tensor_scalar", [P.pbuf[7], fsc_b[fi]], [accp_b[pslot]], out=dst, in0=ps7[:, t, 0:64],
                    scalar1=fsc[fi][:, t:t + 1], scalar2=None, op0=ALU.mult)
            else:
                S.I("dve", "scalar_tensor_tensor", [P.pbuf[7], fsc_b[fi], accp_b[pslot]], [accp_b[pslot]], out=dst,
                    in0=ps7[:, t, 0:64], scalar=fsc[fi][:, t:t + 1], in1=dst, op0=ALU.mult, op1=ALU.add)

    for g in range(2):
        for i in range(4):
            ks = slice(i * 2048, (i + 1) * 2048)
            S.DMA("sp", [], [KTs_b[i]], out=KTs[0:64, ks], in_=dr["kvT_all"][256 + g * 64:256 + (g + 1) * 64, ks])
            S.DMA("sp", [], [Vs_b[i]], out=Vs[:, i * 16:(i + 1) * 16, 0:64],
                  in_=dr["vtok_all"][ks, g * 64:(g + 1) * 64].rearrange("(b p) c -> p b c", p=128))
        for qg in range(NG):
            tk = slice(qg * 512, (qg + 1) * 512)
            S.DMA("sp", [], [QT_b], out=QTv[0:64, :, :], in_=dr["q_scr"][g * 6:(g + 1) * 6, :, tk].rearrange("h d n -> d h n"))
            S.DMA("sp", [], [QT_b], out=QTv[64:71, :, :], in_=dr["qaug"][g * 6:(g + 1) * 6, :, tk].rearrange("h d n -> d h n"))
            S.DMA("sp", [], [KTw_b], out=KTw[0:64, :], in_=dr["ktw"][g, qg])
            S.DMA("sp", [], [KTw_b], out=KTw[64:71, :], in_=dr["kaugw"][qg])
            S.DMA("sp", [], [Vw_b], out=Vw[:, :, 0:64], in_=dr["vw"][g, qg].rearrange("(b p) c -> p b c", p=128))
            S.DMA("sp", [], [cmask_b], out=cmask, in_=dr["cmask"][:, :, tk])
            S.DMA("sp", [], [fbias_b], out=fbias, in_=dr["fbias"][:, 4 * qg:4 * qg + 4, :])
            st["Sbanks"] = [0, 1, 2]
            for h in range(6):
                hh = g * 6 + h
                ob = 3 + (st["o"] % 3)
                st["o"] += 1
                hold = {}
                slots = []
                PTc, PTc_b = PTcs[h % 2], PTcs_b[h % 2]
                for nb in range(4):
                    def post(pt, ptb, nb=nb):
                        S.I("dve", "scalar_tensor_tensor", [ptb, cmask_b], [ptb], out=pt, in0=pt, scalar=1.0e30, in1=cmask[:, nb, :],
                            op0=ALU.min, op1=ALU.mult)
                    slots.append(dict(k=KTc[0:96, g, nb * 128:(nb + 1) * 128], kb=[kc_b], q=QTv[0:96, h, :], qb=[QT_b],
                                      v=Vc[:, g, nb, :], vb=[kc_b], ob=ob, first=(nb == 0), last=(nb == 3), scale=NSA_SCALE,
                                      post=post, pt=PTc[:, nb, :], ptb=PTc_b[nb]))

                ps6 = P.ps(6).rearrange("p (t c) -> p t c", t=4)

                def epi(ob=ob, hh=hh, h=h, ps6=ps6):
                    ri, rest = epilogue(ob, hh, 0, qg, g, True)

                    def rest2():
                        rest()
                        for t in range(4):
                            if h == 0:
                                S.I("dve", "tensor_scalar", [P.pbuf[6], rd_b[ri]], [impacc_b], out=impacc[:, t, :], in0=ps6[:, t, :],
                                    scalar1=rd[ri][:, t:t + 1], scalar2=None, op0=ALU.mult)
                            else:
                                S.I("dve", "scalar_tensor_tensor", [P.pbuf[6], rd_b[ri], impacc_b], [impacc_b], out=impacc[:, t, :],
                                    in0=ps6[:, t, :], scalar=rd[ri][:, t:t + 1], in1=impacc[:, t, :], op0=ALU.mult, op1=ALU.add)
                    return rest2
                slots[-1]["epi"] = epi
                run_slots(P, slots, PT, PT_b, st)
                pend_flush(st, keep=1)
                for t in range(4):
                    for nb in range(4):
                        S.I("pe", "matmul", [PTc_b[nb], cst_b], [P.pbuf[6]], out=ps6[:, t, :], lhsT=PTc[:, nb, t * 128:(t + 1) * 128],
                            rhs=agg[:, nb, :], start=(nb == 0), stop=(nb == 3))
            pend_flush(st)
            st["Sbanks"] = [0, 1, 2, 6]
            ps6b = P.ps(7).bitcast(BF16)
            for t in range(4):
                S.I("dve", "tensor_tensor", [impacc_b, fbias_b, tk_b], [tk_b], out=impb, in0=impacc[:, t, :], in1=fbias[:, t, :], op=ALU.add)
                S.I("dve", "max", [tk_b], [tk_b], out=m8[:, 0:8], in_=impb)
                S.I("dve", "match_replace", [tk_b], [tk_b], out=impb2, in_to_replace=m8[:, 0:8], in_values=impb, imm_value=NEG_BIG)
                S.I("dve", "max", [tk_b], [tk_b], out=m8[:, 8:16], in_=impb2)
                S.I("dve", "tensor_scalar", [tk_b], [tk_b], out=thr, in0=m8[:, 15:16], scalar1=-1.0e29, scalar2=None, op0=ALU.max)
                S.I("dve", "tensor_scalar", [tk_b], [tk_b], out=selm, in0=impb, scalar1=thr[:, 0:1], scalar2=MASK_BIG, op0=ALU.is_ge, op1=ALU.mult)
                S.I("pe", "transpose", [tk_b, cst_b], [P.pbuf[7]], out=ps6b[:, t * 128:(t + 1) * 128], in_=selm, identity=identb)
            S.I("act", "activation", [P.pbuf[7]], [selT_b], out=selT, in_=ps6b[:, 0:512], func=AF.Copy)
            for h in range(6):
                hh = g * 6 + h
                ob = 3 + (st["o"] % 3)
                st["o"] += 1
                slots = []
                for i in range(8):
                    def post(pt, ptb, i=i):
                        S.I("dve", "scalar_tensor_tensor", [ptb, cst_b], [ptb], out=pt, in0=pt, scalar=1.0e30, in1=wmask[:, i, :],
                            op0=ALU.min, op1=ALU.mult)
                    slots.append(dict(k=KTw[0:96, i * 128:(i + 1) * 128], kb=[KTw_b], q=QTv[0:96, h, :], qb=[QT_b],
                                      v=Vw[:, i, :], vb=[Vw_b], ob=ob, first=(i == 0), last=(i == 7), scale=NSA_SCALE, post=post))

                def epi(ob=ob, hh=hh):
                    return epilogue(ob, hh, 2, qg, g, False)[1]
                slots[-1]["epi"] = epi
                run_slots(P, slots, PT, PT_b, st)
            nkb = 16 * qg + 16
            for trio in range(2):
                heads = [trio * 3 + i for i in range(3)]
                slots = []
                cur = {}
                for kb in range(nkb):
                    def pre(kb=kb, cur=cur):
                        si = st["sm"] % 2
                        st["sm"] += 1
                        S.I("pe", "matmul", [wide_b, selT_b], [P.pbuf[7]], out=P.ps(7), lhsT=Wide[:, kb * 128:(kb + 1) * 128],
                            rhs=selT, start=True, stop=True)
                        i = kb - 16 * qg
                        if i >= 0:
                            S.I("dve", "scalar_tensor_tensor", [cst_b, P.pbuf[7]], [smask_b[si]], out=smask[si], in0=qrel,
                                scalar=krel[:, i:i + 1], in1=P.ps(7), op0=ALU.is_ge, op1=ALU.mult)
                        else:
                            S.I("act", "activation", [P.pbuf[7]], [smask_b[si]], out=smask[si], in_=P.ps(7), func=AF.Copy)
                        cur["si"] = si
                    for ii, h in enumerate(heads):
                        def post(pt, ptb, cur=cur):
                            si = cur["si"]
                            S.I("dve", "tensor_tensor", [ptb, smask_b[si]], [ptb], out=pt, in0=pt, in1=smask[si], op=ALU.min)
                        d = dict(k=KTs[0:96, kb * 128:(kb + 1) * 128], kb=[KTs_b[kb // 16], KTa_b], q=QTv[0:96, h, :], qb=[QT_b],
                                 v=Vs[:, kb, :], vb=[Vs_b[kb // 16]], ob=3 + ii, first=(kb == 0), last=(kb == nkb - 1),
                                 scale=NSA_SCALE, post=post, m=128)
                        if ii == 0:
                            d["pre"] = pre
                        if kb == nkb - 1:
                            def epi(ob=3 + ii, hh=g * 6 + h):
                                return epilogue(ob, hh, 1, qg, g, False)[1]
                            d["epi"] = epi
                        slots.append(d)
                run_slots(P, slots, PT, PT_b, st)
            pend_flush(st)
            for pslot in range(3):
                pair = 3 * g + pslot
                for t in range(4):
                    S.I("pe", "transpose", [accp_b[pslot], cst_b], [P.pbuf[7]], out=P.ps(7)[:, t * 128:(t + 1) * 128],
                        in_=accp[pslot][:, t, :], identity=ident32)
                S.I("act", "activation", [P.pbuf[7]], [P.xT_b[qg][pair]], out=P.xT[:, pair, tk], in_=P.ps(7), func=AF.Copy)
    mkv = Wide[:, 0:2048]
    mkv_b = wide_b
    S.DMA("sp", [], [mkv_b], out=mkv, in_=dr["memkv"])
    kpad = mkv[:, 0:1024].rearrange("p (h m) -> p h m", h=4)
    vaug = mkv[:, 1024:2048].rearrange("p (h b c) -> p h b c", h=4, b=2)
    QTm = [KTs[:, 0:2048], KTs[:, 2048:4096]]
    QTm_b = [KTs_b[0], KTs_b[1]]
    osbm = [KTs[:, 4096:5120].bitcast(F32), KTs[:, 5120:6144].bitcast(F32)]
    osbm_b = [KTs_b[2], Buf("osbm1")]
    st2 = {"s": st["s"], "p": st["p"], "o": 0, "ob": 3, "e": 0, "maxpend": 1}
    mem_heads(P, dr, kpad, vaug, mkv_b, QTm, QTm_b, PT, PT_b, osbm, osbm_b, sel, cst_b, st2)
    pend_flush(st2)
    S.barrier()
    A.release(m)


BF = ml_dtypes.bfloat16


def alibi_slopes12():
    def pow2(k):
        start = 2.0 ** (-8.0 / k)
        return [start ** (i + 1) for i in range(k)]
    s = pow2(8) + pow2(16)[0::2][:4]
    return np.asarray(s, np.float32)


def split_bf(v, n):
    out = []
    r = np.asarray(v, np.float64)
    for _ in range(n):
        p = r.astype(np.float32).astype(BF)
        out.append(p)
        r = r - p.astype(np.float64)
    return out


def kaug_table(pos):
    hi, lo = split_bf(pos, 2)
    one = np.ones_like(hi)
    return np.stack([hi, lo, hi, lo, one, one, one], axis=0)


def nsa_consts_global():
    kaug = kaug_table(np.arange(T, dtype=np.float64))
    cpos = np.arange(512, dtype=np.float64) * 16 + 15.5
    cpos[511] = 0.0
    kaug_c = kaug_table(cpos)
    wide = (np.arange(128)[:, None] == (np.arange(T)[None, :] // 64)).astype(np.float32).astype(BF)
    cn = np.arange(512)
    cs = cn * 16
    ss = np.arange(128) * 64
    ov = np.clip(np.minimum(cs[:, None] + 32, ss[None, :] + 64) - np.maximum(cs[:, None], ss[None, :]), 0, None) / 32.0
    ov[511] = 0.0
    agg = ov.reshape(4, 128, 128).transpose(1, 0, 2).reshape(128, 512).astype(np.float32).astype(BF)
    ident32 = np.eye(128, dtype=np.float32)
    identb = np.eye(128, dtype=np.float32).astype(BF)
    dq = np.arange(512)[None, None, :] + 512 - 128 * np.arange(8)[None, :, None] - np.arange(128)[:, None, None]
    wmask = ((dq >= 0) & (dq < 512)).astype(np.uint8).reshape(128, 8 * 512)
    return dict(kaug=kaug, kaug_c=kaug_c, wide=wide, agg=agg, ident32=ident32, identb=identb, wmask=wmask)


def window_gather(kvT_all_b, vtok_all_b, j):
    ktw = np.zeros((2, 4, 64, 1024), BF)
    vw = np.zeros((2, 4, 1024, 64), BF)
    pos = np.full((4, 1024), -30000.0, np.float64)
    for qg in range(4):
        for i in range(8):
            rb = 16 * qg + 4 * j - 4 + i
            if rb < 0:
                continue
            ks = slice(rb * 128, (rb + 1) * 128)
            pos[qg, i * 128:(i + 1) * 128] = np.arange(rb * 128, (rb + 1) * 128)
            for g in range(2):
                ktw[g, qg, :, i * 128:(i + 1) * 128] = kvT_all_b[512 + g * 64:512 + (g + 1) * 64, ks]
                vw[g, qg, i * 128:(i + 1) * 128, :] = vtok_all_b[ks, 128 + g * 64:128 + (g + 1) * 64]
    kaugw = np.stack([kaug_table(pos[qg]) for qg in range(4)])
    return ktw, vw, kaugw


def nsa_consts_core(j):
    qpos = core_token_index(j).astype(np.int64)
    slopes = alibi_slopes12().astype(np.float64)
    qaug = np.zeros((12, 7, TLOC), BF)
    for h in range(12):
        mhi, mlo = split_bf(np.array([slopes[h]]), 2)
        mp = float(mhi[0]) + float(mlo[0])
        v = -8.0 * mp * qpos.astype(np.float64)
        vh, vm, vl = split_bf(v, 3)
        m8hi = (np.float32(8.0) * mhi.astype(np.float32)).astype(BF)[0]
        m8lo = (np.float32(8.0) * mlo.astype(np.float32)).astype(BF)[0]
        qaug[h, 0] = m8hi
        qaug[h, 1] = m8hi
        qaug[h, 2] = m8lo
        qaug[h, 3] = m8lo
        qaug[h, 4] = vh
        qaug[h, 5] = vm
        qaug[h, 6] = vl
    qrel = qpos[:512]
    cnn = (128 * np.arange(4))[None, :, None] + np.arange(128)[:, None, None]
    cmask = ((cnn <= 510) & (16 * cnn + 31 <= qpos[None, None, :])).astype(np.uint8)
    tq = qpos.reshape(16, 128).T
    jj = np.arange(128)[None, None, :]
    cur = (tq // 64)[:, :, None]
    forced = (jj == 0) | (jj == cur) | (jj == cur - 1)
    valid = jj * 64 <= tq[:, :, None]
    fb = np.where(valid, np.where(forced, 1.0e4, 0.0), NEG_BIG).astype(np.float32).reshape(128, 16 * 128)
    return dict(qaug=qaug, cmask=np.ascontiguousarray(cmask), fbias=fb)


def prep_cmp(cmp_pos, cmp_w1, cmp_b1, cmp_w2):
    w1 = np.stack([kmajor(cmp_w1[j]) for j in range(2)])
    w2 = np.stack([kmajor(cmp_w2[j]) for j in range(2)])
    b1 = np.ascontiguousarray(cmp_b1.reshape(2, 2, 128).transpose(0, 2, 1))
    pp = np.ascontiguousarray(cmp_pos.reshape(2, 16, 2, 64).transpose(0, 2, 3, 1).reshape(2, 128, 16))
    return w1, w2, b1, pp


def _mla_pre_io(P, pre):
    return {
        "w_in": P.din(pre + "w_in", [128, 8 * 672]), "w_uq": P.din(pre + "w_uq", [128, 2 * 1152]), "ng": P.din(pre + "ng", [128, 3]),
        "tabc": P.din("tabc", [128, TLOC]) if "tabc" not in P.dram else P.dram["tabc"],
        "tabs": P.din("tabs", [128, TLOC]) if "tabs" not in P.dram else P.dram["tabs"],
        "memT": P.din("memT", [8, 128, 256]) if "memT" not in P.dram else P.dram["memT"],
        "w_mkv": P.din(pre + "w_mkv", [128, 8 * 512]),
        "lat_c": P.dout("o_lat_c", [128, TLOC], BF16), "lat_r": P.dout("o_lat_r", [32, TLOC], BF16),
        "q_scr": P.dout("o_q_scr", [12, 96, TLOC], BF16), "qm": P.dout("o_qm", [2, 128, TLOC], BF16),
        "memkv": P.dout("o_memkv", [128, 2048], BF16),
    }


def _mla_attn_io(P, pre):
    return {
        "ckv_all": P.din("ckv_all", [128, T], BF16), "kr_all": P.din("kr_all", [32, T], BF16),
        "q_scr": P.din("i_q_scr", [12, 96, TLOC], BF16), "qm": P.din("i_qm", [2, 128, TLOC], BF16),
        "memkv": P.din("i_memkv", [128, 2048], BF16), "w_ukv": P.din(pre + "w_ukv", [128, 1536]),
        "qrel": P.din("qrel", [128, 512], F16), "krel": P.din("krel", [128, 16]), "sel": P.din("sel", [128, 256]),
        "w_out": P.din(pre + "w_out", [128, 8 * 1024]),
    }


def _ffn_io(P, l, i):
    return P.din("wgu%d%d" % (l, i), [NF, 128, 2048]), P.din("wd%d%d" % (l, i), [8, 128, NF * 128])


def build_phase(ph):
    nc = bass.Bass("TRN2", target_bir_lowering=False)
    with ExitStack() as stack:
        P = Prog(nc, stack)
        xT_d = P.din("xT", [8, 128, TLOC])
        lnp_d = P.din("lnp", [128, 192])
        P.alloc_state(lnp_d)
        if ph == 0:
            P.load_xT(xT_d)
            P.ffn(*_ffn_io(P, 0, 0), 0)
            mla_pre(P, _mla_pre_io(P, "l0_"))
        elif ph == 1:
            P.load_xT(xT_d)
            dr = _mla_attn_io(P, "l0_")
            mla_attn(P, dr)
            resid_ln_from_oT(P, dr["w_out"], 1)
            P.ffn(*_ffn_io(P, 0, 1), 2)
            P.ffn(*_ffn_io(P, 1, 0), 3)
            mla_pre(P, _mla_pre_io(P, "l1_"))
        elif ph == 2:
            P.load_xT(xT_d)
            dr = _mla_attn_io(P, "l1_")
            mla_attn(P, dr)
            resid_ln_from_oT(P, dr["w_out"], 4)
            P.ffn(*_ffn_io(P, 1, 1), 5)
            nsa_kv_stage(P, {"w_kv": P.din("w_kv", [128, 8 * 768]), "kvT": P.dout("o_kvT", [768, TLOC], BF16),
                             "vtok": P.dout("o_vtok", [TLOC, 256], BF16)})
        else:
            KTc = P.A.alloc(2 * 512, BF16).rearrange("p (g n) -> p g n", g=2)
            Vc = P.A.alloc(2 * 4 * 96, BF16).rearrange("p (g b c) -> p g b c", g=2, b=4)
            kc_b = Buf("kc")
            P.load_xT(xT_d)
            dr = {
                "kvT_all": P.din("kvT_all", [768, T], BF16), "vtok_all": P.din("vtok_all", [T, 256], BF16),
                "cmp_w1": P.din("cmp_w1", [2, 128, 16 * 256]), "cmp_w2": P.din("cmp_w2", [2, 128, 128]),
                "cmp_b1": P.din("cmp_b1", [2, 128, 2]), "cmp_pp": P.din("cmp_pp", [2, 128, 16]),
                "kaug": P.din("kaug", [7, T], BF16), "kaug_c": P.din("kaug_c", [7, 512], BF16),
                "wide": P.din("wide", [128, T], BF16), "agg": P.din("agg", [128, 512], BF16),
                "ident32": P.din("ident32", [128, 128]), "identb": P.din("identb", [128, 128], BF16),
                "qaug": P.din("qaug", [12, 7, TLOC], BF16), "wmask": P.din("wmask", [128, 8 * 512], U8),
                "cmask": P.din("cmask", [128, 4, TLOC], U8), "fbias": P.din("fbias", [128, 16, 128]),
                "qrel": P.din("qrel", [128, 512], F16), "krel": P.din("krel", [128, 16]), "sel": P.din("sel", [128, 256]),
                "memT": P.din("memT", [8, 128, 256]),
                "ktw": P.din("ktw", [2, 4, 64, 1024], BF16), "vw": P.din("vw", [2, 4, 1024, 64], BF16),
                "kaugw": P.din("kaugw", [4, 7, 1024], BF16),
                "q_scr": P.dscr("q_scr", [12, 64, TLOC], BF16), "qm": P.dscr("qm", [2, 128, TLOC], BF16),
                "gt": P.dscr("gt", [128, 16 * 36]), "memkv": P.dscr("memkv", [128, 2048], BF16),
            }
            nsa_compress(P, dr, KTc, Vc, kc_b)
            for l in (2, 3):
                P.ffn(*_ffn_io(P, l, 0), 3 * l)
                drl = dict(dr)
                drl["w_in"] = P.din("l%d_w_in" % l, [128, 8 * 1060])
                drl["w_mkv"] = P.din("l%d_w_mkv" % l, [128, 8 * 512])
                nsa_pre(P, drl)
                nsa_attn(P, drl, KTc, Vc, kc_b)
                resid_ln_from_oT(P, P.din("l%d_w_out" % l, [128, 8 * 1024]), 3 * l + 1)
                P.ffn(*_ffn_io(P, l, 1), 3 * l + 2)
        if ph < 3:
            P.store_xT(P.dout("o_x", [8, 128, TLOC]))
        else:
            P.store_xT(P.dout("out", [8, 128, TLOC]))
        P.S.barrier()
        block = stack.enter_context(nc.Block())
        P.S.emit(block)
    return nc


def _gather_tokens_T(parts, rows, dtype):
    outs = [np.zeros((rows, T), dtype) for _ in range(NB)]
    for c in range(NCORE):
        b, j = divmod(c, 4)
        outs[b][:, core_token_index(j)] = parts[c]
    return outs


def kernel(x, mem, ln_g, ln_b, ffn_w_gu, ffn_w_down, w_mem_kv, w_out, mla_w_in, mla_q_norm_g, mla_kv_norm_g,
           mla_w_uq, mla_w_ukv, nsa_w_in, nsa_w_kv, cmp_pos, cmp_w1, cmp_b1, cmp_w2):
    f = lambda a: np.ascontiguousarray(np.asarray(a, dtype=np.float32))
    x, mem = f(x), f(mem)
    lnp = prep_ln(f(ln_g), f(ln_b))
    ffw = {}
    for l in range(DEPTH):
        for i in range(2):
            ffw[(l, i)] = prep_ffn_weights(f(ffn_w_gu[l, i]), f(ffn_w_down[l, i]))
    memT = [np.ascontiguousarray(mem[b].T.reshape(8, 128, 256)) for b in range(NB)]
    tabs_ = [rope_tables(j) for j in range(4)]
    acon = [attn_consts(j) for j in range(4)]
    cores = list(range(NCORE))

    def mla_pre_in(l):
        ng = np.stack([mla_q_norm_g[l][:128], mla_q_norm_g[l][128:], mla_kv_norm_g[l]], axis=1).astype(np.float32)
        return {"l%d_w_in" % l: kmajor(f(mla_w_in[l])), "l%d_w_uq" % l: kmajor(f(mla_w_uq[l])), "l%d_ng" % l: np.ascontiguousarray(ng),
                "l%d_w_mkv" % l: kmajor(f(w_mem_kv[l]))}

    def percore_pre(c):
        b, j = divmod(c, 4)
        return {"tabc": tabs_[j][0], "tabs": tabs_[j][1], "memT": memT[b]}

    def mla_attn_in(l, prev, c, ckv_all, kr_all):
        b, j = divmod(c, 4)
        return {"ckv_all": ckv_all[b], "kr_all": kr_all[b], "i_q_scr": prev[c]["o_q_scr"], "i_qm": prev[c]["o_qm"],
                "i_memkv": prev[c]["o_memkv"], "l%d_w_ukv" % l: f(mla_w_ukv[l]), "qrel": acon[j][0], "krel": acon[j][1],
                "sel": acon[j][2], "l%d_w_out" % l: kmajor(f(w_out[l]))}

    def ffn_in(l, i):
        return {"wgu%d%d" % (l, i): ffw[(l, i)][0], "wd%d%d" % (l, i): ffw[(l, i)][1]}

    xs = shard_xT(x)
    com = mla_pre_in(0)
    com.update(ffn_in(0, 0))
    maps = []
    for c in cores:
        m_ = {"xT": xs[c], "lnp": lnp}
        m_.update(com)
        m_.update(percore_pre(c))
        maps.append(m_)
    rr0 = run_bass_kernel_spmd(build_phase(0), maps, core_ids=cores, trace=_TRACE)
    r0 = rr0.results
    if _TRACE:
        print('PHASE 0 exec_ns', rr0.exec_time_ns, flush=True)
    if _DBG is not None:
        _DBG(0, r0)
    ckv_all = _gather_tokens_T([r0[c]["o_lat_c"] for c in cores], 128, BF)
    kr_all = _gather_tokens_T([r0[c]["o_lat_r"] for c in cores], 32, BF)
    com = mla_pre_in(1)
    com.update(ffn_in(0, 1))
    com.update(ffn_in(1, 0))
    maps = []
    for c in cores:
        m_ = {"xT": r0[c]["o_x"], "lnp": lnp}
        m_.update(com)
        m_.update(percore_pre(c))
        m_.update(mla_attn_in(0, r0, c, ckv_all, kr_all))
        maps.append(m_)
    rr1 = run_bass_kernel_spmd(build_phase(1), maps, core_ids=cores, trace=_TRACE)
    r1 = rr1.results
    if _TRACE:
        print('PHASE 1 exec_ns', rr1.exec_time_ns, flush=True)
    if _DBG is not None:
        _DBG(1, r1)
    ckv_all = _gather_tokens_T([r1[c]["o_lat_c"] for c in cores], 128, BF)
    kr_all = _gather_tokens_T([r1[c]["o_lat_r"] for c in cores], 32, BF)
    com = ffn_in(1, 1)
    com["w_kv"] = kmajor(f(nsa_w_kv))
    maps = []
    for c in cores:
        m_ = {"xT": r1[c]["o_x"], "lnp": lnp}
        m_.update(com)
        m_.update(mla_attn_in(1, r1, c, ckv_all, kr_all))
        maps.append(m_)
    rr2 = run_bass_kernel_spmd(build_phase(2), maps, core_ids=cores, trace=_TRACE)
    r2 = rr2.results
    if _TRACE:
        print('PHASE 2 exec_ns', rr2.exec_time_ns, flush=True)
    if _DBG is not None:
        _DBG(2, r2)
    kvT_all = _gather_tokens_T([r2[c]["o_kvT"] for c in cores], 768, BF)
    vt = _gather_tokens_T([np.ascontiguousarray(np.asarray(r2[c]["o_vtok"]).T) for c in cores], 256, BF)
    vtok_all = [np.ascontiguousarray(v.T) for v in vt]
    gc = nsa_consts_global()
    w1, w2, b1, pp = prep_cmp(f(cmp_pos), f(cmp_w1), f(cmp_b1), f(cmp_w2))
    com = {"cmp_w1": w1, "cmp_w2": w2, "cmp_b1": b1, "cmp_pp": pp, "kaug": gc["kaug"], "kaug_c": gc["kaug_c"],
           "wide": gc["wide"], "agg": gc["agg"], "ident32": gc["ident32"], "identb": gc["identb"], "wmask": gc["wmask"]}
    for l in (2, 3):
        com.update(ffn_in(l, 0))
        com.update(ffn_in(l, 1))
        com["l%d_w_in" % l] = kmajor(f(nsa_w_in[l - 2]))
        com["l%d_w_mkv" % l] = kmajor(f(w_mem_kv[l]))
        com["l%d_w_out" % l] = kmajor(f(w_out[l]))
    ccon = [nsa_consts_core(j) for j in range(4)]
    maps = []
    for c in cores:
        b, j = divmod(c, 4)
        ktw_, vw_, kaugw_ = window_gather(kvT_all[b], vtok_all[b], j)
        m_ = {"xT": r2[c]["o_x"], "lnp": lnp, "kvT_all": kvT_all[b], "vtok_all": vtok_all[b], "qaug": ccon[j]["qaug"],
              "ktw": ktw_, "vw": vw_, "kaugw": kaugw_,
              "cmask": ccon[j]["cmask"], "fbias": ccon[j]["fbias"].reshape(128, 16, 128),
              "qrel": acon[j][0], "krel": acon[j][1], "sel": acon[j][2], "memT": memT[b]}
        m_.update(com)
        maps.append(m_)
    rr3 = run_bass_kernel_spmd(build_phase(3), maps, core_ids=cores, trace=_TRACE)
    r3 = rr3.results
    if _TRACE:
        print('PHASE 3 exec_ns', rr3.exec_time_ns, flush=True)
    if _DBG is not None:
        _DBG(3, r3)
    return unshard_xT([np.asarray(r3[c]["out"]) for c in cores])
```

```python
import numpy as np
import ml_dtypes
import concourse.bass as bass
import concourse.mybir as mybir
from concourse.bass_utils import run_bass_kernel_spmd
from contextlib import ExitStack

F32 = mybir.dt.float32
BF16 = mybir.dt.bfloat16
F16 = mybir.dt.float16
AF = mybir.ActivationFunctionType
ALU = mybir.AluOpType

D = 1024
T = 8192
NB = 2
NCORE = 8
TLOC = 2048
NG = TLOC // 512
DFF = 2816
NF = DFF // 128
DEPTH = 4
DN_ALPHA = (2 * DEPTH) ** 0.25
LN_EPS = 1e-5
RMS_EPS = 1e-6

ZIG = [0, 7, 8, 15]


def global_block(j, i):
    s, r = divmod(i, 4)
    return 16 * s + 4 * j + r


def core_token_index(j):
    return np.concatenate([np.arange(128) + 128 * global_block(j, i) for i in range(16)])


class Buf:
    __slots__ = ("name", "w", "r")

    def __init__(self, name):
        self.name = name
        self.w = None
        self.r = {}


class Sched:
    ENGS = ("pe", "act", "dve", "pool", "sp")
    NDMA = 24

    def __init__(self, nc, stack):
        self.nc = nc
        self.sem = {}
        for e in ("pe", "act", "dve", "pool"):
            self.sem[e] = stack.enter_context(nc.semaphore("s_" + e))
        for k in range(self.NDMA):
            self.sem[("d", k)] = stack.enter_context(nc.semaphore("s_d%d" % k))
        self.cnt = {k: 0 for k in self.sem}
        self.ops = {e: [] for e in self.ENGS}
        self.seen = {e: {} for e in self.ENGS}
        self.dma_rr = 0
        self.nops = 0

    def _collect(self, eng, reads, writes):
        waits = {}

        def add(ev):
            if ev is None:
                return
            k, v = ev
            if k == eng == "pe":
                return
            if waits.get(k, 0) < v:
                waits[k] = v

        for b in reads:
            add(b.w)
        for b in writes:
            add(b.w)
            for k, v in b.r.items():
                add((k, v))
        out = []
        seen = self.seen[eng]
        for k, v in waits.items():
            if seen.get(k, 0) < v:
                seen[k] = v
                out.append((k, v))
        return out

    def _commit(self, ev, reads, writes):
        k, v = ev
        for b in reads:
            if b.r.get(k, 0) < v:
                b.r[k] = v
        for b in writes:
            b.w = ev
            b.r = {}

    def op(self, eng, fn, reads=(), writes=()):
        waits = self._collect(eng, reads, writes)
        self.cnt[eng] += 1
        ev = (eng, self.cnt[eng])
        self.ops[eng].append((waits, fn, (eng, 1)))
        self._commit(ev, reads, writes)
        self.nops += 1
        return ev

    def dma(self, eng, fn, reads=(), writes=()):
        k = ("d", self.dma_rr)
        self.dma_rr = (self.dma_rr + 1) % self.NDMA
        waits = self._collect(eng, reads, writes)
        prev = self.cnt[k]
        if prev and self.seen[eng].get(k, 0) < prev:
            self.seen[eng][k] = prev
            waits.append((k, prev))
        self.cnt[k] += 16
        ev = (k, self.cnt[k])
        self.ops[eng].append((waits, fn, (k, 16)))
        self._commit(ev, reads, writes)
        self.nops += 1
        return ev

    def I(self, eng, method, reads=(), writes=(), **kw):
        return self.op(eng, lambda e: getattr(e, method)(**kw), reads, writes)

    def DMA(self, eng, reads=(), writes=(), **kw):
        return self.dma(eng, lambda e: e.dma_start(**kw), reads, writes)

    def wait_all(self, eng):
        waits = []
        for k, v in self.cnt.items():
            if v and k != eng and self.seen[eng].get(k, 0) < v:
                self.seen[eng][k] = v
                waits.append((k, v))
        if waits:
            self.ops[eng].append((waits, None, None))

    def barrier(self):
        for e in self.ENGS:
            self.wait_all(e)

    def snapshot(self):
        return dict(self.cnt)

    def barrier_to(self, snap):
        for eng in self.ENGS:
            waits = []
            for k, v in snap.items():
                if v and k != eng and self.seen[eng].get(k, 0) < v:
                    self.seen[eng][k] = v
                    waits.append((k, v))
            if waits:
                self.ops[eng].append((waits, None, None))

    def emit(self, block):
        sem = self.sem

        def run(engine, lst):
            for waits, fn, inc in lst:
                for k, v in waits:
                    engine.wait_ge(sem[k], v)
                if fn is not None:
                    ins = fn(engine)
                    ins.then_inc(sem[inc[0]], inc[1])

        ops = self.ops

        @block.tensor
        def _(e):
            run(e, ops["pe"])

        @block.scalar
        def _(e):
            run(e, ops["act"])

        @block.vector
        def _(e):
            run(e, ops["dve"])

        @block.gpsimd
        def _(e):
            run(e, ops["pool"])

        @block.sync
        def _(e):
            run(e, ops["sp"])


class Arena:
    def __init__(self, base_bf16, nbytes):
        self.base = base_bf16
        self.nbytes = nbytes
        self.top = 0
        self.peak = 0

    def alloc(self, nelem, dtype):
        sz = 4 if dtype == F32 else 2
        nb = (nelem * sz + 63) // 64 * 64
        off = self.top
        self.top += nb
        self.peak = max(self.peak, self.top)
        assert self.top <= getattr(self, "limit", self.nbytes), ("arena overflow", self.top, getattr(self, "limit", self.nbytes))
        v = self.base[:, off // 2:(off + nelem * sz) // 2]
        if dtype != BF16:
            v = v.bitcast(dtype)
        return v

    def alloc_fixed(self, off, nelem, dtype):
        sz = 4 if dtype == F32 else 2
        assert off % 64 == 0 and off + nelem * sz <= self.nbytes
        v = self.base[:, off // 2:(off + nelem * sz) // 2]
        if dtype != BF16:
            v = v.bitcast(dtype)
        return v

    def mark(self):
        return self.top

    def release(self, m):
        self.top = m


class Prog:
    def __init__(self, nc, stack, arena_bytes=207 * 1024):
        self.nc = nc
        self.S = Sched(nc, stack)
        arena_t = stack.enter_context(nc.sbuf_tensor("arena", [128, arena_bytes // 2], BF16))
        self.A = Arena(arena_t[:, :], arena_bytes)
        self.psum = []
        self.pbuf = []
        for i in range(8):
            p = stack.enter_context(nc.psum_tensor("ps%d" % i, [128, 512], F32))
            self.psum.append(p)
            self.pbuf.append(Buf("ps%d" % i))
        self.dram = {}

    def ps(self, i):
        return self.psum[i][:, :]

    def din(self, name, shape, dtype=F32):
        t = self.nc.dram_tensor(name, list(shape), dtype, kind="ExternalInput").ap()
        self.dram[name] = t
        return t

    def dscr(self, name, shape, dtype=F32):
        t = self.nc.dram_tensor(name, list(shape), dtype, kind="Internal").ap()
        self.dram[name] = t
        return t

    def dout(self, name, shape, dtype=F32):
        t = self.nc.dram_tensor(name, list(shape), dtype, kind="ExternalOutput").ap()
        self.dram[name] = t
        return t

    def alloc_state(self, lnp_dram):
        A, S = self.A, self.S
        self.x32 = A.alloc(8 * TLOC, F32).rearrange("p (k t) -> p k t", k=8)
        self.xT = A.alloc(8 * TLOC, BF16).rearrange("p (k t) -> p k t", k=8)
        self.x32_b = [[Buf("x32_%d_%d" % (g, c)) for c in range(8)] for g in range(NG)]
        self.xT_b = [[Buf("xT_%d_%d" % (g, c)) for c in range(8)] for g in range(NG)]
        self.ones32 = A.alloc(128, F32)
        self.ones_b = Buf("ones")
        S.op("dve", lambda e: e.memset(self.ones32, 1.0 / 1024.0), writes=[self.ones_b])
        self.lnp = A.alloc(2 * 12 * 8, F32).rearrange("p (a l k) -> p a l k", a=2, l=12)
        self.lnp_b = Buf("lnp")
        S.dma("sp", lambda e: e.dma_start(out=self.lnp, in_=lnp_dram.rearrange("p (a l k) -> p a l k", a=2, l=12)),
              writes=[self.lnp_b])

    def load_xT(self, xT_dram):
        S = self.S
        for k in range(8):
            for g in range(NG):
                dst = self.x32[:, k, g * 512:(g + 1) * 512]
                src = xT_dram[k, :, g * 512:(g + 1) * 512]
                S.dma("sp", lambda e, dst=dst, src=src: e.dma_start(out=dst, in_=src), writes=[self.x32_b[g][k]])
                xt = self.xT[:, k, g * 512:(g + 1) * 512]
                S.op("act", lambda e, dst=dst, xt=xt: e.activation(out=xt, in_=dst, func=AF.Copy),
                     reads=[self.x32_b[g][k]], writes=[self.xT_b[g][k]])

    def store_xT(self, out_dram):
        S = self.S
        for k in range(8):
            for g in range(NG):
                src = self.x32[:, k, g * 512:(g + 1) * 512]
                dst = out_dram[k, :, g * 512:(g + 1) * 512]
                S.dma("sp", lambda e, dst=dst, src=src: e.dma_start(out=dst, in_=src), reads=[self.x32_b[g][k]])

    LN_TOP = 6 * 2048

    def alloc_ln_tmp(self):
        if getattr(self, "_lntmp", None) is not None:
            return self._lntmp
        A = self.A
        base = A.nbytes - self.LN_TOP
        f = lambda i: A.alloc_fixed(base + i * 2048, 512, F32)
        d = {}
        d["sq"] = [f(0), f(1)]
        d["sq_b"] = [Buf("lnsq0"), Buf("lnsq1")]
        d["mean"] = f(2)
        d["rstd"] = f(3)
        d["mr_b"] = Buf("lnmr")
        d["t"] = [f(4), f(5)]
        d["t_b"] = [Buf("lnt0"), Buf("lnt1")]
        d["cnt"] = 0
        self._lntmp = d
        return d

    def resid_chunk(self, g, c, ybank, tmp):
        S = self.S
        xs = self.x32[:, c, g * 512:(g + 1) * 512]
        S.op("dve", lambda e: e.scalar_tensor_tensor(out=xs, in0=xs, scalar=DN_ALPHA, in1=self.ps(ybank),
                                                     op0=ALU.mult, op1=ALU.add),
             reads=[self.pbuf[ybank], self.x32_b[g][c]], writes=[self.x32_b[g][c]])
        i = tmp["cnt"] % 2
        tmp["cnt"] += 1
        sq, sq_b = tmp["sq"][i], tmp["sq_b"][i]
        S.op("act", lambda e: e.activation(out=sq, in_=xs, func=AF.Square), reads=[self.x32_b[g][c]], writes=[sq_b])

        def stats(sbank):
            S.op("pe", lambda e: e.matmul(self.ps(sbank), lhsT=self.ones32, rhs=xs, start=(c == 0), stop=(c == 7)),
                 reads=[self.ones_b, self.x32_b[g][c]], writes=[self.pbuf[sbank]])
            S.op("pe", lambda e: e.matmul(self.ps(sbank + 1), lhsT=self.ones32, rhs=sq, start=(c == 0), stop=(c == 7)),
                 reads=[self.ones_b, sq_b], writes=[self.pbuf[sbank + 1]])
        return stats

    def ln_finalize(self, g, sbank, ln_idx, tmp):
        for step in self.ln_steps(g, sbank, ln_idx, tmp):
            step()

    def ln_steps(self, g, sbank, ln_idx, tmp):
        yield lambda: self._ln_stats(g, sbank, tmp)
        for c in range(8):
            yield lambda c=c: self._ln_chunk(g, c, ln_idx, tmp)

    def _ln_stats(self, g, sbank, tmp):
        S = self.S
        mean, rstd, mr_b = tmp["mean"], tmp["rstd"], tmp["mr_b"]
        S.op("act", lambda e: e.activation(out=mean, in_=self.ps(sbank), func=AF.Copy),
             reads=[self.pbuf[sbank]], writes=[mr_b])
        S.op("dve", lambda e: e.tensor_tensor(out=rstd, in0=mean, in1=mean, op=ALU.mult), reads=[mr_b], writes=[mr_b])
        S.op("dve", lambda e: e.tensor_tensor(out=rstd, in0=self.ps(sbank + 1), in1=rstd, op=ALU.subtract),
             reads=[mr_b, self.pbuf[sbank + 1]], writes=[mr_b])
        S.op("dve", lambda e: e.tensor_scalar(out=rstd, in0=rstd, scalar1=LN_EPS, scalar2=None, op0=ALU.add),
             reads=[mr_b], writes=[mr_b])
        S.op("act", lambda e: e.activation(out=rstd, in_=rstd, func=AF.Sqrt), reads=[mr_b], writes=[mr_b])
        S.op("dve", lambda e: e.reciprocal(out=rstd, in_=rstd), reads=[mr_b], writes=[mr_b])

    def _ln_chunk(self, g, c, ln_idx, tmp):
        S = self.S
        mean, rstd, mr_b = tmp["mean"], tmp["rstd"], tmp["mr_b"]
        if True:
            xs = self.x32[:, c, g * 512:(g + 1) * 512]
            xt = self.xT[:, c, g * 512:(g + 1) * 512]
            i = c % 2
            t, t_b = tmp["t"][i], tmp["t_b"][i]
            S.op("dve", lambda e, t=t, xs=xs: e.tensor_tensor(out=t, in0=xs, in1=mean, op=ALU.subtract),
                 reads=[self.x32_b[g][c], mr_b], writes=[t_b])
            S.op("dve", lambda e, t=t: e.tensor_tensor(out=t, in0=t, in1=rstd, op=ALU.mult), reads=[t_b, mr_b], writes=[t_b])
            gs = self.lnp[:, 0, ln_idx, c:c + 1]
            bs = self.lnp[:, 1, ln_idx, c:c + 1]
            S.op("dve", lambda e, t=t, xs=xs, gs=gs, bs=bs: e.tensor_scalar(out=xs, in0=t, scalar1=gs, scalar2=bs,
                                                                            op0=ALU.mult, op1=ALU.add),
                 reads=[t_b, self.lnp_b], writes=[self.x32_b[g][c]])
            S.op("act", lambda e, xs=xs, xt=xt: e.activation(out=xt, in_=xs, func=AF.Copy),
                 reads=[self.x32_b[g][c]], writes=[self.xT_b[g][c]])

    def ffn(self, wgu_dram, wd_dram, ln_idx, relaxed=False):
        S, A = self.S, self.A
        m = A.mark()
        A.limit = A.nbytes - self.LN_TOP
        NSUB = 2
        tmp = self.alloc_ln_tmp()
        hT = A.alloc(NF * NSUB * 512, BF16).rearrange("p (j t) -> p j t", j=NF)

        hT_b = [[Buf("hT%d_%d" % (j, s)) for s in range(NSUB)] for j in range(NF)]
        NW = 3
        wgu = [A.alloc(8 * 256, BF16).rearrange("p (k c) -> p k c", k=8) for _ in range(NW)]
        wgu_b = [Buf("wgu%d" % i) for i in range(NW)]
        wd = [A.alloc(NF * 128, BF16).rearrange("p (j n) -> p j n", j=NF) for _ in range(NW)]
        wd_b = [Buf("wd%d" % i) for i in range(NW)]
        sg = [A.alloc(512, F32) for _ in range(2)]
        sg_b = [Buf("sg0"), Buf("sg1")]
        cnt = 0
        wcnt = 0
        dcnt = 0
        ln_pending = []
        for p in range(TLOC // (NSUB * 512)):
            for j in range(NF):
                if j >= 1 and ln_pending:
                    ln_pending.pop(0)()
                w = wcnt % NW
                wcnt += 1
                if not (FFN_EXP == "skipdma" and p == 1):
                    S.dma("pool", lambda e, w=w, j=j: e.dma_start(
                        out=wgu[w], in_=wgu_dram[j].rearrange("p (k c) -> p k c", k=8), max_dma_last_dim=4096),
                        writes=[wgu_b[w]])
                for s in range(NSUB):
                    g = p * NSUB + s
                    pg, pu = (cnt % 2) * 2, (cnt % 2) * 2 + 1
                    for half, pb in ((0, pg), (1, pu)):
                        for k in range(8):
                            S.op("pe", lambda e, w=w, k=k, half=half, pb=pb, g=g: e.matmul(
                                self.ps(pb), lhsT=wgu[w][:, k, half * 128:(half + 1) * 128],
                                rhs=self.xT[:, k, g * 512:(g + 1) * 512], start=(k == 0), stop=(k == 7)),
                                reads=[wgu_b[w], self.xT_b[g][k]], writes=[self.pbuf[pb]])
                    si = cnt % 2
                    S.op("act", lambda e, si=si, pg=pg: e.activation(out=sg[si], in_=self.ps(pg), func=AF.Silu),
                         reads=[self.pbuf[pg]], writes=[sg_b[si]])
                    S.op("dve", lambda e, si=si, pu=pu, j=j, s=s: e.scalar_tensor_tensor(
                        out=hT[:, j, s * 512:(s + 1) * 512], in0=sg[si], scalar=0.5, in1=self.ps(pu),
                        op0=ALU.mult, op1=ALU.mult),
                        reads=[sg_b[si], self.pbuf[pu]], writes=[hT_b[j][s]])
                    cnt += 1
            pending = None
            for c in range(8):
                w = dcnt % NW
                dcnt += 1
                if not (FFN_EXP == "skipdma" and p == 1):
                    S.dma("pool", lambda e, w=w, c=c: e.dma_start(
                        out=wd[w], in_=wd_dram[c].rearrange("p (j n) -> p j n", j=NF), max_dma_last_dim=4096),
                        writes=[wd_b[w]])
                for s in range(NSUB):
                    g = p * NSUB + s
                    yb = (c * NSUB + s) % 2
                    for j in range(NF):
                        S.op("pe", lambda e, w=w, j=j, s=s, yb=yb: e.matmul(
                            self.ps(yb), lhsT=wd[w][:, j, :], rhs=hT[:, j, s * 512:(s + 1) * 512],
                            start=(j == 0), stop=(j == NF - 1)),
                            reads=[wd_b[w], hT_b[j][s]], writes=[self.pbuf[yb]])
                    if pending is not None:
                        pending[0](pending[1])
                    st = self.resid_chunk(g, c, yb, tmp)
                    pending = (st, 4 + 2 * s)
            pending[0](pending[1])
            for s in range(NSUB):
                ln_pending.extend(self.ln_steps(p * NSUB + s, 4 + 2 * s, ln_idx, tmp))
        snap = S.snapshot()
        for step in ln_pending:
            step()
        if relaxed:
            S.barrier_to(snap)
        else:
            S.barrier()
        A.release(m)


def prep_ffn_weights(w_gu, w_down):
    a = w_gu.reshape(8, 128, 2, NF, 128).transpose(3, 1, 0, 2, 4).reshape(NF, 128, 8 * 256)
    b = w_down.reshape(NF, 128, 8, 128).transpose(2, 1, 0, 3).reshape(8, 128, NF * 128)
    return np.ascontiguousarray(a), np.ascontiguousarray(b)


def prep_ln(ln_g, ln_b):
    g = ln_g.reshape(12, 8, 128).transpose(2, 0, 1)
    b = ln_b.reshape(12, 8, 128).transpose(2, 0, 1)
    return np.ascontiguousarray(np.stack([g, b], axis=1).reshape(128, 2 * 12 * 8))


def shard_xT(x):
    outs = []
    for c in range(NCORE):
        b, j = divmod(c, 4)
        idx = core_token_index(j)
        xs = x[b][idx]
        outs.append(np.ascontiguousarray(xs.T.reshape(8, 128, TLOC)))
    return outs


def unshard_xT(outs):
    y = np.empty((NB, T, D), np.float32)
    for c in range(NCORE):
        b, j = divmod(c, 4)
        idx = core_token_index(j)
        y[b][idx] = outs[c].reshape(D, TLOC).T
    return y


MLA_SCALE = 96.0 ** -0.5
MEM_SCALE = 64.0 ** -0.5
ROPE_THETA = 10000.0
CAST = dict(max_dma_last_dim=4096)
DEBUG_STOP = None
_DBG = None
FFN_EXP = None
_TRACE = False


def rstd_from_psum(P, sbank, out, out_b, mult, eps):
    S = P.S
    S.I("dve", "tensor_scalar", [P.pbuf[sbank]], [out_b], out=out, in0=P.ps(sbank), scalar1=mult, scalar2=eps,
        op0=ALU.mult, op1=ALU.add)
    S.I("act", "activation", [out_b], [out_b], out=out, in_=out, func=AF.Sqrt)
    S.I("dve", "reciprocal", [out_b], [out_b], out=out, in_=out)


def mem_kv(P, dr, A):
    S = P.S
    memT = A.alloc(8 * 256, BF16).rearrange("p (k m) -> p k m", k=8)
    memT_b = Buf("memT")
    S.DMA("pool", [], [memT_b], out=memT, in_=dr["memT"].rearrange("k p m -> p k m"), **CAST)
    w_mkv = A.alloc(8 * 512, BF16).rearrange("p (k c) -> p k c", k=8)
    w_mkv_b = Buf("w_mkv")
    S.DMA("pool", [], [w_mkv_b], out=w_mkv, in_=dr["w_mkv"].rearrange("p (k c) -> p k c", k=8), **CAST)
    mkv = A.alloc(2048, BF16)
    mkv_b = Buf("mkv")
    S.I("dve", "memset", [], [mkv_b], ap=mkv, constant=0.0)
    kpad = mkv[:, 0:1024].rearrange("p (h m) -> p h m", h=4)
    vaug = mkv[:, 1024:2048].rearrange("p (h b c) -> p h b c", h=4, b=2)
    for pr in range(2):
        for k in range(8):
            S.I("pe", "matmul", [w_mkv_b, memT_b], [P.pbuf[0]], out=P.ps(0)[:, 0:256],
                lhsT=w_mkv[:, k, pr * 128:(pr + 1) * 128], rhs=memT[:, k, :], start=(k == 0), stop=(k == 7))
        S.I("act", "activation", [P.pbuf[0]], [mkv_b], out=kpad[0:64, 2 * pr, :], in_=P.ps(0)[0:64, 0:256], func=AF.Copy)
        S.I("act", "activation", [P.pbuf[0]], [mkv_b], out=kpad[64:128, 2 * pr + 1, :], in_=P.ps(0)[64:128, 0:256], func=AF.Copy)
    for kb in range(2):
        for k in range(8):
            S.I("pe", "matmul", [w_mkv_b, memT_b], [P.pbuf[1]], out=P.ps(1)[:, 0:256],
                lhsT=memT[:, k, kb * 128:(kb + 1) * 128], rhs=w_mkv[:, k, 256:512], start=(k == 0), stop=(k == 7))
        for h in range(4):
            c0 = 0 if h % 2 == 0 else 64
            S.I("act", "activation", [P.pbuf[1]], [mkv_b], out=vaug[:, h, kb, c0:c0 + 64],
                in_=P.ps(1)[:, h * 64:(h + 1) * 64], func=AF.Copy)
    for h in range(4):
        oc = 64 if h % 2 == 0 else 0
        S.I("dve", "memset", [], [mkv_b], ap=vaug[:, h, :, oc:oc + 1], constant=1.0)
    S.DMA("sp", [mkv_b], [], out=dr["memkv"], in_=mkv)


def mla_pre(P, dr):
    S, A = P.S, P.A
    m = A.mark()
    A.limit = A.nbytes - P.LN_TOP
    w_in = A.alloc(8 * 672, BF16).rearrange("p (k c) -> p k c", k=8)
    w_in_b = Buf("w_in")
    S.DMA("pool", [], [w_in_b], out=w_in, in_=dr["w_in"].rearrange("p (k c) -> p k c", k=8), **CAST)
    w_rot = A.alloc(8 * 128, BF16).rearrange("p (k c) -> p k c", k=8)
    w_rot_b = Buf("w_rot")
    S.I("dve", "memset", [], [w_rot_b], ap=w_rot, constant=0.0)
    S.I("dve", "tensor_scalar", [w_in_b], [w_rot_b], out=w_rot[:, :, 0:16], in0=w_in[:, :, 400:416], scalar1=-1.0,
        scalar2=None, op0=ALU.mult)
    S.I("dve", "tensor_copy", [w_in_b], [w_rot_b], out=w_rot[:, :, 16:32], in_=w_in[:, :, 384:400])
    w_uq = A.alloc(2 * 1152, BF16).rearrange("p (k c) -> p k c", k=2)
    w_uq_b = Buf("w_uq")
    S.DMA("pool", [], [w_uq_b], out=w_uq, in_=dr["w_uq"].rearrange("p (k c) -> p k c", k=2), **CAST)
    w_uqr = A.alloc(2 * 1152, BF16).rearrange("p (k c) -> p k c", k=2)
    w_uqr_b = Buf("w_uqr")
    S.I("dve", "memset", [], [w_uqr_b], ap=w_uqr, constant=0.0)
    for k in range(2):
        src = w_uq[:, k, :].rearrange("p (h c) -> p h c", h=12)
        dst = w_uqr[:, k, :].rearrange("p (h c) -> p h c", h=12)
        S.I("dve", "tensor_scalar", [w_uq_b], [w_uqr_b], out=dst[:, :, 64:80], in0=src[:, :, 80:96], scalar1=-1.0,
            scalar2=None, op0=ALU.mult)
        S.I("dve", "tensor_copy", [w_uq_b], [w_uqr_b], out=dst[:, :, 80:96], in_=src[:, :, 64:80])
    ng = A.alloc(3, F32)
    ng_b = Buf("ng")
    S.DMA("sp", [], [ng_b], out=ng, in_=dr["ng"])
    tabc = A.alloc(TLOC, BF16)
    tabs = A.alloc(TLOC, BF16)
    tab_b = Buf("tab")
    S.DMA("pool", [], [tab_b], out=tabc, in_=dr["tabc"], **CAST)
    S.DMA("pool", [], [tab_b], out=tabs, in_=dr["tabs"], **CAST)
    onesb = A.alloc(128, BF16)
    onesb_b = Buf("onesb")
    S.I("dve", "memset", [], [onesb_b], ap=onesb, constant=1.0)

    if DEBUG_STOP == "setup":
        S.barrier(); A.release(m); return
    mem_kv(P, dr, A)
    if DEBUG_STOP == "memkv":
        S.barrier(); A.release(m); return

    cqn = A.alloc(2 * TLOC, BF16).rearrange("p (k t) -> p k t", k=2)
    cqn_b = [Buf("cqn%d" % g) for g in range(NG)]
    cq_sb = A.alloc(2 * 512, F32).rearrange("p (k t) -> p k t", k=2)
    cq_sb_b = Buf("cq_sb")
    sqb = [A.alloc(512, BF16) for _ in range(2)]
    sqb_b = [Buf("sqb0"), Buf("sqb1")]
    rr = A.alloc(512, F32)
    rr_b = Buf("rr")
    ckv_sb = A.alloc(512, F32)
    ckv_sb_b = Buf("ckv_sb")
    latc = A.alloc(TLOC, BF16)
    latc_b = [Buf("latc%d" % g) for g in range(NG)]
    latr = A.alloc(TLOC, BF16)
    latr_b = [Buf("latr%d" % g) for g in range(NG)]
    qmT = A.alloc(2 * TLOC, BF16).rearrange("p (k t) -> p k t", k=2)
    qmT_b = [Buf("qmT%d" % g) for g in range(NG)]
    t1 = [A.alloc(512, F32) for _ in range(2)]
    t1_b = [Buf("t1a"), Buf("t1b")]
    t2 = [A.alloc(512, F32) for _ in range(2)]
    t2_b = [Buf("t2a"), Buf("t2b")]
    NQ = 3
    qst = [A.alloc(512, BF16) for _ in range(NQ)]
    qst_b = [Buf("qst%d" % i) for i in range(NQ)]

    def proj(g, bank, wt, wb, c0, c1):
        tk = slice(g * 512, (g + 1) * 512)
        for k in range(8):
            S.I("pe", "matmul", [wb, P.xT_b[g][k]], [P.pbuf[bank]], out=P.ps(bank), lhsT=wt[:, k, c0:c1],
                rhs=P.xT[:, k, tk], start=(k == 0), stop=(k == 7))

    for g in range(NG):
        tk = slice(g * 512, (g + 1) * 512)
        for k2 in range(2):
            proj(g, k2, w_in, w_in_b, k2 * 128, (k2 + 1) * 128)
            S.I("act", "activation", [P.pbuf[k2]], [sqb_b[k2]], out=sqb[k2], in_=P.ps(k2), func=AF.Square)
            S.I("act", "activation", [P.pbuf[k2]], [cq_sb_b], out=cq_sb[:, k2, :], in_=P.ps(k2), func=AF.Copy)
        for k2 in range(2):
            S.I("pe", "matmul", [onesb_b, sqb_b[k2]], [P.pbuf[2]], out=P.ps(2), lhsT=onesb, rhs=sqb[k2],
                start=(k2 == 0), stop=(k2 == 1))
        rstd_from_psum(P, 2, rr, rr_b, 1.0 / 256.0, RMS_EPS)
        for k2 in range(2):
            S.I("dve", "scalar_tensor_tensor", [cq_sb_b, ng_b, rr_b], [cqn_b[g]], out=cqn[:, k2, tk], in0=cq_sb[:, k2, :],
                scalar=ng[:, k2:k2 + 1], in1=rr, op0=ALU.mult, op1=ALU.mult)
        proj(g, 3, w_in, w_in_b, 256, 384)
        S.I("act", "activation", [P.pbuf[3]], [sqb_b[0]], out=sqb[0], in_=P.ps(3), func=AF.Square)
        S.I("act", "activation", [P.pbuf[3]], [ckv_sb_b], out=ckv_sb, in_=P.ps(3), func=AF.Copy)
        S.I("pe", "matmul", [onesb_b, sqb_b[0]], [P.pbuf[2]], out=P.ps(2), lhsT=onesb, rhs=sqb[0], start=True, stop=True)
        rstd_from_psum(P, 2, rr, rr_b, 1.0 / 128.0, RMS_EPS)
        S.I("dve", "scalar_tensor_tensor", [ckv_sb_b, ng_b, rr_b], [latc_b[g]], out=latc[:, tk], in0=ckv_sb,
            scalar=ng[:, 2:3], in1=rr, op0=ALU.mult, op1=ALU.mult)
        S.DMA("sp", [latc_b[g]], [], out=dr["lat_c"][:, tk], in_=latc[:, tk])
        proj(g, 4, w_in, w_in_b, 384, 512)
        proj(g, 5, w_rot, w_rot_b, 0, 128)
        S.I("dve", "tensor_tensor", [P.pbuf[4], tab_b], [t1_b[0]], out=t1[0][0:32, :], in0=P.ps(4)[0:32, :],
            in1=tabc[0:32, tk], op=ALU.mult)
        S.I("dve", "tensor_tensor", [P.pbuf[5], tab_b], [t2_b[0]], out=t2[0][0:32, :], in0=P.ps(5)[0:32, :],
            in1=tabs[0:32, tk], op=ALU.mult)
        S.I("dve", "tensor_tensor", [t1_b[0], t2_b[0]], [latr_b[g]], out=latr[0:32, tk], in0=t1[0][0:32, :],
            in1=t2[0][0:32, :], op=ALU.add)
        S.DMA("sp", [latr_b[g]], [], out=dr["lat_r"][:, tk], in_=latr[0:32, tk])
        for k2 in range(2):
            proj(g, 6 + k2, w_in, w_in_b, 416 + k2 * 128, 416 + (k2 + 1) * 128)
            S.I("act", "activation", [P.pbuf[6 + k2]], [qmT_b[g]], out=qmT[:, k2, tk], in_=P.ps(6 + k2), func=AF.Copy)
            S.DMA("sp", [qmT_b[g]], [], out=dr["qm"][k2, :, tk], in_=qmT[:, k2, tk])
    if DEBUG_STOP == "groups":
        S.barrier(); A.release(m); return
    qi = 0
    for g in range(NG):
        tk = slice(g * 512, (g + 1) * 512)
        for h in range(12):
            b0 = (qi % 2) * 2
            for bank, wt, wb in ((b0, w_uq, w_uq_b), (b0 + 1, w_uqr, w_uqr_b)):
                for k in range(2):
                    S.I("pe", "matmul", [wb, cqn_b[g]], [P.pbuf[bank]], out=P.ps(bank)[0:96, :],
                        lhsT=wt[:, k, h * 96:(h + 1) * 96], rhs=cqn[:, k, tk], start=(k == 0), stop=(k == 1))
            q = qst[qi % NQ]
            qb = qst_b[qi % NQ]
            i2 = qi % 2
            S.I("act", "activation", [P.pbuf[b0]], [qb], out=q[0:64, :], in_=P.ps(b0)[0:64, :], func=AF.Copy)
            if DEBUG_STOP != "qnorope":
                S.I("dve", "tensor_tensor", [P.pbuf[b0], tab_b], [t1_b[i2]], out=t1[i2][64:96, :], in0=P.ps(b0)[64:96, :],
                    in1=tabc[64:96, tk], op=ALU.mult)
                S.I("dve", "tensor_tensor", [P.pbuf[b0 + 1], tab_b], [t2_b[i2]], out=t2[i2][64:96, :],
                    in0=P.ps(b0 + 1)[64:96, :], in1=tabs[64:96, tk], op=ALU.mult)
                if False:
                    S.I("pool", "tensor_tensor", [t1_b[i2], t2_b[i2]], [qb], out=q[64:96, :], in0=t1[i2][64:96, :],
                        in1=t2[i2][64:96, :], op=ALU.add)
                else:
                    S.I("dve", "tensor_tensor", [t1_b[i2], t2_b[i2]], [qb], out=q[64:96, :], in0=t1[i2][64:96, :],
                        in1=t2[i2][64:96, :], op=ALU.add)
            if DEBUG_STOP != "qnodma":
                S.DMA("sp", [qb], [], out=dr["q_scr"][h, :, tk], in_=q[0:96, :])
            qi += 1
            if DEBUG_STOP == "q1":
                break
        if DEBUG_STOP == "q1":
            break
    S.barrier()
    A.release(m)


def resid_ln_from_oT(P, dr_w_out, ln_idx):
    S, A = P.S, P.A
    m = A.mark()
    A.limit = A.nbytes - P.LN_TOP
    w_out = A.alloc(8 * 1024, BF16).rearrange("p (k c) -> p k c", k=8)
    w_out_b = Buf("w_out")
    S.DMA("pool", [], [w_out_b], out=w_out, in_=dr_w_out.rearrange("p (k c) -> p k c", k=8), **CAST)
    tmp = P.alloc_ln_tmp()
    lnq = []
    for g in range(NG):
        tk = slice(g * 512, (g + 1) * 512)
        sb = 4 + 2 * (g % 2)
        pending = None
        for c in range(8):
            yb = c % 2
            for _ in range(2 if c == 0 else 1):
                if lnq:
                    lnq.pop(0)()
            for pr in range(8):
                S.I("pe", "matmul", [w_out_b, P.xT_b[g][pr]], [P.pbuf[yb]], out=P.ps(yb),
                    lhsT=w_out[:, pr, c * 128:(c + 1) * 128], rhs=P.xT[:, pr, tk], start=(pr == 0), stop=(pr == 7))
            if pending is not None:
                pending(sb)
            pending = P.resid_chunk(g, c, yb, tmp)
        pending(sb)
        for step in lnq:
            step()
        lnq = list(P.ln_steps(g, sb, ln_idx, tmp))
    for step in lnq:
        step()
    S.barrier()
    A.release(m)


DEFER = 6


def pend_tick(st):
    pend = st.setdefault("pending", [])
    for ent in list(pend):
        ent[0] -= 1
        if ent[0] <= 0:
            pend.remove(ent)
            ent[2]()


def pend_flush(st, bank=None, keep=0):
    pend = st.setdefault("pending", [])
    for ent in list(pend):
        if bank is None or ent[1] == bank:
            if bank is None and len(pend) <= keep:
                break
            pend.remove(ent)
            ent[2]()


def pend_add(st, ob, fn, maxpend):
    pend = st.setdefault("pending", [])
    while len(pend) >= maxpend:
        ent = pend.pop(0)
        ent[2]()
    pend.append([DEFER, ob, fn])


def attn_epilogue(P, ob, osb, osb_b, sel, sel_b, par, pair, g):
    S = P.S
    tk = slice(g * 512, (g + 1) * 512)
    dr_ = 64 if par == 0 else 0
    r0 = 0 if par == 0 else 64
    S.I("act", "activation", [P.pbuf[ob]], [osb_b], out=osb, in_=P.ps(ob), func=AF.Copy)
    S.I("dve", "reciprocal", [osb_b], [osb_b], out=osb[dr_:dr_ + 1, :], in_=osb[dr_:dr_ + 1, :])

    def rest():
        S.I("pe", "matmul", [sel_b, osb_b], [P.pbuf[7]], out=P.ps(7), lhsT=sel[:, par, :], rhs=osb, start=True, stop=True)
        S.I("dve", "tensor_tensor", [osb_b, P.pbuf[7]], [P.xT_b[g][pair]], out=P.xT[r0:r0 + 64, pair, tk],
            in0=osb[r0:r0 + 64, :], in1=P.ps(7)[r0:r0 + 64, :], op=ALU.mult)
    return rest


def attn_slots(P, slots, kt_fn, q_fn, v_fn, scale, mask_fn, PT, PT_b, epi_fn, st):
    S = P.S
    n = len(slots)
    info = [None] * n
    SB = st.get("Sbanks", [0, 1, 2])
    L = len(SB) - 1
    for i in range(n + L):
        pend_tick(st)
        if st.get("bg") is not None and i % 5 == 2:
            step = next(st["bg"], None)
            if step is None:
                st["bg"] = None
            else:
                step()
        if i < n:
            g, kb, first, last = slots[i]
            sb = SB[st["s"] % len(SB)]
            st["s"] += 1
            pi = st["p"] % len(PT)
            st["p"] += 1
            kt, kt_bufs = kt_fn(kb)
            q, q_bufs = q_fn(g)
            S.I("pe", "matmul", kt_bufs + q_bufs, [P.pbuf[sb]], out=P.ps(sb), lhsT=kt, rhs=q, start=True, stop=True)
            S.I("act", "activation", [P.pbuf[sb]], [PT_b[pi]], out=PT[pi], in_=P.ps(sb), func=AF.Exp, scale=scale)
            if mask_fn is not None:
                mask_fn(g, kb, PT[pi], PT_b[pi])
            info[i] = pi
        j = i - L
        if j >= 0:
            g, kb, first, last = slots[j]
            if first:
                st["ob"] = 3 + (st["o"] % 2)
                st["o"] += 1
                pend_flush(st, bank=st["ob"])
            ob = st["ob"]
            v, v_bufs = v_fn(kb)
            pi = info[j]
            S.I("pe", "matmul", v_bufs + [PT_b[pi]], [P.pbuf[ob]], out=P.ps(ob), lhsT=v, rhs=PT[pi], start=first, stop=last)
            if last:
                rest = epi_fn(g, ob)
                if rest is not None:
                    pend_add(st, ob, rest, st.get("maxpend", 1))


def mem_heads(P, dr, kpad, vaug, mkv_b, QT, QT_b, PT, PT_b, osb, osb_b, sel, sel_b, st):
    S = P.S
    for hm in range(4):
        par = hm % 2
        qi = hm % 2
        S.DMA("sp", [], [QT_b[qi]], out=QT[qi], in_=dr["qm"][hm // 2])
        slots = []
        for g in range(NG):
            for kb in range(2):
                slots.append((g, kb, kb == 0, kb == 1))

        def kt_fn(kb, hm=hm):
            return kpad[:, hm, kb * 128:(kb + 1) * 128], [mkv_b]

        def q_fn(g, qi=qi):
            return QT[qi][:, g * 512:(g + 1) * 512], [QT_b[qi]]

        def v_fn(kb, hm=hm):
            return vaug[:, hm, kb, :], [mkv_b]

        def epi_fn(g, ob, par=par, pair=6 + hm // 2):
            e = st["e"] % len(osb)
            st["e"] += 1
            return attn_epilogue(P, ob, osb[e], osb_b[e], sel, sel_b, par, pair, g)

        attn_slots(P, slots, kt_fn, q_fn, v_fn, MEM_SCALE, None, PT, PT_b, epi_fn, st)


def mla_attn(P, dr):
    S, A = P.S, P.A
    m = A.mark()
    A.limit = A.nbytes
    ckv = A.alloc(T, BF16)
    ckv_b = [Buf("ckv%d" % i) for i in range(16)]
    KT2 = [A.alloc(T, BF16) for _ in range(2)]
    KT2_b = [[Buf("KT%d_%d" % (p_, i)) for i in range(16)] for p_ in range(2)]
    KTr_b = Buf("KTr")
    for i in range(16):
        ks = slice(i * 512, (i + 1) * 512)
        S.DMA("sp", [], [ckv_b[i]], out=ckv[:, ks], in_=dr["ckv_all"][:, ks])
    for p_ in range(2):
        for i in range(4):
            ks = slice(i * 2048, (i + 1) * 2048)
            S.DMA("sp", [], [KTr_b], out=KT2[p_][64:96, ks], in_=dr["kr_all"][:, ks])
    w_ukv = A.alloc(1536, BF16)
    w_ukv_b = Buf("w_ukv")
    S.DMA("pool", [], [w_ukv_b], out=w_ukv, in_=dr["w_ukv"], **CAST)
    V = [A.alloc(64 * 128, BF16).rearrange("p (b c) -> p b c", b=64) for _ in range(2)]
    V_b = [[Buf("V%d_%d" % (p_, i)) for i in range(8)] for p_ in range(2)]
    for p_ in range(2):
        S.I("pool", "memset", [], V_b[p_], ap=V[p_], constant=0.0)
        oc = 64 if p_ == 0 else 0
        S.I("pool", "memset", [], V_b[p_], ap=V[p_][:, :, oc:oc + 1], constant=1.0)
    mkv = A.alloc(2048, BF16)
    mkv_b = Buf("mkv")
    S.DMA("sp", [], [mkv_b], out=mkv, in_=dr["memkv"])
    kpad = mkv[:, 0:1024].rearrange("p (h m) -> p h m", h=4)
    vaug = mkv[:, 1024:2048].rearrange("p (h b c) -> p h b c", h=4, b=2)
    QT = [A.alloc(TLOC, BF16) for _ in range(2)]
    QT_b = [Buf("QT0"), Buf("QT1")]
    qrel = A.alloc(512, F16)
    krel = A.alloc(16, F32)
    sel = A.alloc(256, F32).rearrange("p (a c) -> p a c", a=2)
    cst_b = Buf("cst")
    S.DMA("sp", [], [cst_b], out=qrel, in_=dr["qrel"])
    S.DMA("sp", [], [cst_b], out=krel, in_=dr["krel"])
    S.DMA("sp", [], [cst_b], out=sel, in_=dr["sel"].rearrange("p (a c) -> p a c", a=2))
    PT = [A.alloc(512, BF16) for _ in range(4)]
    PT_b = [Buf("PT%d" % i) for i in range(4)]
    osb = [A.alloc(512, F32) for _ in range(4)]
    osb_b = [Buf("osb%d" % i) for i in range(4)]
    st = {"s": 0, "p": 0, "o": 0, "ob": 3, "e": 0, "prod": 0, "maxpend": 3, "Sbanks": [0, 1, 2, 5]}

    def mask_fn(g, kb, pt, pt_b):
        i = kb - 16 * g
        if i < 0:
            return
        S.I("dve", "scalar_tensor_tensor", [cst_b, pt_b], [pt_b], out=pt, in0=qrel, scalar=krel[:, i:i + 1], in1=pt,
            op0=ALU.is_ge, op1=ALU.mult)

    def prod_gen(h):
        par = h % 2
        Vh, Vh_b = V[par], V_b[par]
        KT, KT_b = KT2[par], KT2_b[par]
        c0 = 0 if par == 0 else 64
        for kc in range(16):
            def stepk(kc=kc):
                pb = 6
                ks = slice(kc * 512, (kc + 1) * 512)
                S.I("pe", "matmul", [w_ukv_b, ckv_b[kc]], [P.pbuf[pb]], out=P.ps(pb), lhsT=w_ukv[:, h * 128:(h + 1) * 128],
                    rhs=ckv[:, ks], start=True, stop=True)
                if kc % 2 == 0:
                    S.I("act", "activation", [P.pbuf[pb]], [KT_b[kc]], out=KT[0:64, ks], in_=P.ps(pb)[0:64, :], func=AF.Copy)
                else:
                    S.I("dve", "tensor_copy", [P.pbuf[pb]], [KT_b[kc]], out=KT[0:64, ks], in_=P.ps(pb)[0:64, :])
            yield stepk
        for kg in range(8):
            def stepv(kg=kg):
                pb = 6
                for i in range(8):
                    kb = kg * 8 + i
                    S.I("pe", "matmul", [w_ukv_b, ckv_b[kb // 4]], [P.pbuf[pb]], out=P.ps(pb)[:, i * 64:(i + 1) * 64],
                        lhsT=ckv[:, kb * 128:(kb + 1) * 128], rhs=w_ukv[:, h * 128 + 64:(h + 1) * 128], start=True, stop=True)
                src_ = P.ps(pb).rearrange("p (b c) -> p b c", b=8)
                dst = Vh[:, kg * 8:(kg + 1) * 8, c0:c0 + 64]
                if kg % 2 == 0:
                    S.I("dve", "tensor_copy", [P.pbuf[pb]], [Vh_b[kg]], out=dst, in_=src_)
                else:
                    S.I("act", "activation", [P.pbuf[pb]], [Vh_b[kg]], out=dst, in_=src_, func=AF.Copy)
            yield stepv

    for step in prod_gen(0):
        step()
    for h in range(12):
        par = h % 2
        Vh, Vh_b = V[par], V_b[par]
        KT, KT_b = KT2[par], KT2_b[par]
        st["bg"] = prod_gen(h + 1) if h + 1 < 12 else None
        qi = h % 2
        S.DMA("sp", [], [QT_b[qi]], out=QT[qi][0:96, :], in_=dr["q_scr"][h])
        slots = []
        for g in range(NG):
            nkb = 16 * g + 16
            for kb in range(nkb):
                slots.append((g, kb, kb == 0, kb == nkb - 1))

        def kt_fn(kb, KT=KT, KT_b=KT_b):
            return KT[0:96, kb * 128:(kb + 1) * 128], [KT_b[kb // 4], KTr_b]

        def q_fn(g, qi=qi):
            return QT[qi][0:96, g * 512:(g + 1) * 512], [QT_b[qi]]

        def v_fn(kb, Vh=Vh, Vh_b=Vh_b):
            return Vh[:, kb, :], [Vh_b[kb // 8]]

        def epi_fn(g, ob, par=par, pair=h // 2):
            e = st["e"] % len(osb)
            st["e"] += 1
            return attn_epilogue(P, ob, osb[e], osb_b[e], sel, cst_b, par, pair, g)

        attn_slots(P, slots, kt_fn, q_fn, v_fn, MLA_SCALE, mask_fn, PT, PT_b, epi_fn, st)
        if st["bg"] is not None:
            for step in st["bg"]:
                step()
            st["bg"] = None
    mem_heads(P, dr, kpad, vaug, mkv_b, QT, QT_b, PT, PT_b, osb, osb_b, sel, cst_b, st)
    pend_flush(st)
    S.barrier()
    A.release(m)


def kmajor(w):
    K_, N_ = w.shape
    return np.ascontiguousarray(w.reshape(K_ // 128, 128, N_).transpose(1, 0, 2).reshape(128, (K_ // 128) * N_))


def rope_tables(j):
    pos = core_token_index(j).astype(np.float32)
    freq = ROPE_THETA ** (-np.arange(16, dtype=np.float32) / 16)
    ang = pos[None, :] * freq[:, None]
    c = np.zeros((128, TLOC), np.float32)
    s = np.zeros((128, TLOC), np.float32)
    for base in (0, 64):
        c[base:base + 16] = np.cos(ang)
        c[base + 16:base + 32] = np.cos(ang)
        s[base:base + 16] = np.sin(ang)
        s[base + 16:base + 32] = np.sin(ang)
    return c, s


def attn_consts(j):
    qpos = core_token_index(j)[:512]
    qrel = np.broadcast_to(qpos[None, :].astype(np.float16), (128, 512)).copy()
    krel = (128.0 * np.arange(16)[None, :] + np.arange(128)[:, None]).astype(np.float32)
    sel = np.zeros((128, 2, 128), np.float32)
    sel[64, 0, :] = 1.0
    sel[0, 1, :] = 1.0
    return qrel, krel, sel.reshape(128, 256)


U8 = mybir.dt.uint8
NSA_SCALE = 0.125
NEG_BIG = -1.0e30
MASK_BIG = 3.0e38


def run_slots(P, slots, PT, PT_b, st):
    S = P.S
    n = len(slots)
    info = [None] * n
    SB = st.get("Sbanks", [0, 1, 2])
    L = len(SB) - 1
    for i in range(n + L):
        pend_tick(st)
        if i < n:
            s = slots[i]
            if s.get("pre") is not None:
                s["pre"]()
            sb = SB[st["s"] % len(SB)]
            st["s"] += 1
            if s.get("pt") is not None:
                pt, ptb = s["pt"], s["ptb"]
            else:
                pi = st["p"] % len(PT)
                st["p"] += 1
                pt, ptb = PT[pi], PT_b[pi]
            S.I("pe", "matmul", s["kb"] + s["qb"], [P.pbuf[sb]], out=P.ps(sb), lhsT=s["k"], rhs=s["q"], start=True, stop=True)
            S.I("act", "activation", [P.pbuf[sb]], [ptb], out=pt, in_=P.ps(sb), func=AF.Exp, scale=s["scale"])
            if s.get("post") is not None:
                s["post"](pt, ptb)
            info[i] = (pt, ptb)
        j = i - L
        if j >= 0:
            s = slots[j]
            pt, ptb = info[j]
            ob = s["ob"]
            if s["first"]:
                pend_flush(st, bank=ob)
            S.I("pe", "matmul", s["vb"] + [ptb], [P.pbuf[ob]], out=P.ps(ob)[0:s.get("m", 96), :], lhsT=s["v"], rhs=pt,
                start=s["first"], stop=s["last"])
            if s["last"] and s.get("epi") is not None:
                rest = s["epi"]()
                if rest is not None:
                    pend_add(st, ob, rest, 3)


def nsa_kv_stage(P, dr):
    S, A = P.S, P.A
    m = A.mark()
    A.limit = A.nbytes - P.LN_TOP
    w_kv = A.alloc(8 * 768, BF16).rearrange("p (k c) -> p k c", k=8)
    w_kv_b = Buf("w_kv")
    S.DMA("pool", [], [w_kv_b], out=w_kv, in_=dr["w_kv"].rearrange("p (k c) -> p k c", k=8), **CAST)
    stg = [A.alloc(512, BF16) for _ in range(3)]
    stg_b = [Buf("stg%d" % i) for i in range(3)]
    n = 0
    for g in range(NG):
        tk = slice(g * 512, (g + 1) * 512)
        for c in range(6):
            bank = n % 4
            for k in range(8):
                S.I("pe", "matmul", [w_kv_b, P.xT_b[g][k]], [P.pbuf[bank]], out=P.ps(bank), lhsT=w_kv[:, k, c * 128:(c + 1) * 128],
                    rhs=P.xT[:, k, tk], start=(k == 0), stop=(k == 7))
            si = n % 3
            if n % 2 == 0:
                S.I("act", "activation", [P.pbuf[bank]], [stg_b[si]], out=stg[si], in_=P.ps(bank), func=AF.Copy)
            else:
                S.I("dve", "tensor_copy", [P.pbuf[bank]], [stg_b[si]], out=stg[si], in_=P.ps(bank))
            S.DMA("sp", [stg_b[si]], [], out=dr["kvT"][c * 128:(c + 1) * 128, tk], in_=stg[si])
            n += 1
    for t in range(16):
        g = t // 4
        for bi, br in enumerate((1, 2)):
            bank = 4 + (n % 2)
            for k in range(8):
                S.I("pe", "matmul", [w_kv_b, P.xT_b[g][k]], [P.pbuf[bank]], out=P.ps(bank)[:, 0:128],
                    lhsT=P.xT[:, k, t * 128:(t + 1) * 128], rhs=w_kv[:, k, br * 256 + 128:br * 256 + 256],
                    start=(k == 0), stop=(k == 7))
            si = n % 3
            S.I("act", "activation", [P.pbuf[bank]], [stg_b[si]], out=stg[si][:, 0:128], in_=P.ps(bank)[:, 0:128], func=AF.Copy)
            S.DMA("sp", [stg_b[si]], [], out=dr["vtok"][t * 128:(t + 1) * 128, bi * 128:(bi + 1) * 128], in_=stg[si][:, 0:128])
            n += 1
    S.barrier()
    A.release(m)


def nsa_compress(P, dr, KTc, Vc, kc_b):
    S, A = P.S, P.A
    m = A.mark()
    A.limit = A.nbytes
    S.I("dve", "memset", [], [kc_b], ap=KTc, constant=0.0)
    S.I("dve", "memset", [], [kc_b], ap=Vc, constant=0.0)
    S.I("dve", "memset", [], [kc_b], ap=Vc[:, :, :, 64:65], constant=1.0)
    for g in range(2):
        S.DMA("sp", [], [kc_b], out=KTc[64:71, g, :], in_=dr["kaug_c"])
    zp = A.alloc(T, BF16)
    zp_b = Buf("zp")
    w1 = A.alloc(16 * 256, BF16).rearrange("p (l c) -> p l c", l=16)
    w1_b = Buf("w1")
    w2 = A.alloc(2 * 128, BF16).rearrange("p (h c) -> p h c", h=2)
    w2_b = Buf("w2")
    pp = A.alloc(16, BF16)
    b1 = A.alloc(2, F32)
    posb = A.alloc(2, F32)
    pb_b = Buf("posb")
    hid = [A.alloc(512, BF16) for _ in range(2)]
    hid_b = [Buf("hid0"), Buf("hid1")]
    xg = A.alloc(512, F32)
    ug = A.alloc(512, F32)
    xg_b = Buf("xg")
    S.I("dve", "memset", [], [zp_b], ap=zp[:, T - 64:T], constant=0.0)
    for hc in range(2):
        S.I("dve", "memset", [], [hid_b[hc]], ap=hid[hc], constant=0.0)
    for j in range(2):
        S.I("dve", "memset", [], [w2_b], ap=w2, constant=0.0)
        S.DMA("pool", [], [w1_b], out=w1, in_=dr["cmp_w1"][j].rearrange("p (l c) -> p l c", l=16), **CAST)
        S.DMA("pool", [], [w2_b], out=w2[:, :, 0:64], in_=dr["cmp_w2"][j].rearrange("p (h c) -> p h c", h=2), **CAST)
        S.DMA("pool", [], [pb_b], out=pp, in_=dr["cmp_pp"][j], **CAST)
        S.DMA("sp", [], [pb_b], out=b1, in_=dr["cmp_b1"][j])
        for hc in range(2):
            for l in range(16):
                S.I("pe", "matmul", [w1_b, pb_b], [P.pbuf[6]], out=P.ps(6)[:, 0:1], lhsT=w1[:, l, hc * 128:(hc + 1) * 128],
                    rhs=pp[:, l:l + 1], start=(l == 0), stop=(l == 15))
            S.I("dve", "tensor_tensor", [P.pbuf[6], pb_b], [pb_b], out=posb[:, hc:hc + 1], in0=P.ps(6)[:, 0:1],
                in1=b1[:, hc:hc + 1], op=ALU.add)
        for g in range(2):
            r0 = j * 128 + g * 64
            for q4 in range(4):
                ks = slice(q4 * 2048, (q4 + 1) * 2048)
                S.DMA("sp", [], [zp_b], out=zp[0:64, ks], in_=dr["kvT_all"][r0:r0 + 64, ks])
                hi = min((q4 + 1) * 2048, T - 1)
                S.DMA("sp", [], [zp_b], out=zp[64:128, q4 * 2048:hi], in_=dr["kvT_all"][r0:r0 + 64, q4 * 2048 + 1:hi + 1])
            for hc in range(2):
                bank = hc
                for l in range(16):
                    rhs = zp[:, 2 * l:2 * l + 16 * 510 + 1:16]
                    S.I("pe", "matmul", [w1_b, zp_b], [P.pbuf[bank]], out=P.ps(bank)[:, 0:511], lhsT=w1[:, l, hc * 128:(hc + 1) * 128],
                        rhs=rhs, start=(l == 0), stop=(l == 15))
                S.I("dve", "tensor_scalar", [P.pbuf[bank], pb_b], [xg_b], out=xg[:, 0:511], in0=P.ps(bank)[:, 0:511],
                    scalar1=posb[:, hc:hc + 1], scalar2=None, op0=ALU.add)
                S.I("dve", "tensor_tensor", [xg_b], [xg_b], out=ug[:, 0:511], in0=xg[:, 0:511], in1=xg[:, 0:511], op=ALU.mult)
                S.I("dve", "tensor_scalar", [xg_b], [xg_b], out=ug[:, 0:511], in0=ug[:, 0:511], scalar1=0.044715, scalar2=1.0,
                    op0=ALU.mult, op1=ALU.add)
                S.I("dve", "tensor_tensor", [xg_b], [xg_b], out=ug[:, 0:511], in0=ug[:, 0:511], in1=xg[:, 0:511], op=ALU.mult)
                S.I("act", "activation", [xg_b], [xg_b], out=ug[:, 0:511], in_=ug[:, 0:511], func=AF.Tanh, scale=0.7978845608028654)
                S.I("dve", "tensor_scalar", [xg_b], [xg_b], out=ug[:, 0:511], in0=ug[:, 0:511], scalar1=1.0, scalar2=0.5,
                    op0=ALU.add, op1=ALU.mult)
                S.I("dve", "tensor_tensor", [xg_b], [hid_b[hc]], out=hid[hc][:, 0:511], in0=ug[:, 0:511], in1=xg[:, 0:511], op=ALU.mult)
            if j == 0:
                for hc in range(2):
                    S.I("pe", "matmul", [w2_b, hid_b[hc]], [P.pbuf[2]], out=P.ps(2), lhsT=w2[:, hc, :], rhs=hid[hc],
                        start=(hc == 0), stop=(hc == 1))
                S.I("act", "activation", [P.pbuf[2]], [kc_b], out=KTc[0:64, g, 0:511], in_=P.ps(2)[0:64, 0:511], func=AF.Copy)
            else:
                for nb in range(4):
                    for hc in range(2):
                        S.I("pe", "matmul", [w2_b, hid_b[hc]], [P.pbuf[3]], out=P.ps(3)[:, nb * 64:(nb + 1) * 64],
                            lhsT=hid[hc][:, nb * 128:(nb + 1) * 128], rhs=w2[:, hc, 0:64], start=(hc == 0), stop=(hc == 1))
                S.I("act", "activation", [P.pbuf[3]], [kc_b], out=Vc[:, g, :, 0:64],
                    in_=P.ps(3)[:, 0:256].rearrange("p (n c) -> p n c", n=4), func=AF.Copy)
    S.barrier()
    A.release(m)


def nsa_pre(P, dr):
    S, A = P.S, P.A
    m = A.mark()
    A.limit = A.nbytes - P.LN_TOP
    w_in = A.alloc(8 * 1060, BF16).rearrange("p (k c) -> p k c", k=8)
    w_in_b = Buf("w_in")
    S.DMA("pool", [], [w_in_b], out=w_in, in_=dr["w_in"].rearrange("p (k c) -> p k c", k=8), **CAST)
    mem_kv(P, dr, A)
    qst = [A.alloc(512, BF16) for _ in range(3)]
    qst_b = [Buf("qst%d" % i) for i in range(3)]
    gt = A.alloc(16 * 36, F32).rearrange("p (t c) -> p t c", t=16)
    gt_b = Buf("gt")
    n = 0
    for g in range(NG):
        tk = slice(g * 512, (g + 1) * 512)
        for hh in range(14):
            c0 = hh * 64 if hh < 12 else 804 + (hh - 12) * 128
            bank = n % 4
            for k in range(8):
                S.I("pe", "matmul", [w_in_b, P.xT_b[g][k]], [P.pbuf[bank]], out=P.ps(bank), lhsT=w_in[:, k, c0:c0 + 128],
                    rhs=P.xT[:, k, tk], start=(k == 0), stop=(k == 7))
            si = n % 3
            if n % 2 == 0:
                S.I("act", "activation", [P.pbuf[bank]], [qst_b[si]], out=qst[si], in_=P.ps(bank), func=AF.Copy)
            else:
                S.I("dve", "tensor_copy", [P.pbuf[bank]], [qst_b[si]], out=qst[si], in_=P.ps(bank))
            if hh < 12:
                S.DMA("sp", [qst_b[si]], [], out=dr["q_scr"][hh, :, tk], in_=qst[si][0:64, :])
            else:
                S.DMA("sp", [qst_b[si]], [], out=dr["qm"][hh - 12, :, tk], in_=qst[si])
            n += 1
    for t in range(16):
        g = t // 4
        bank = 4 + t % 2
        for k in range(8):
            S.I("pe", "matmul", [w_in_b, P.xT_b[g][k]], [P.pbuf[bank]], out=P.ps(bank)[:, 0:36],
                lhsT=P.xT[:, k, t * 128:(t + 1) * 128], rhs=w_in[:, k, 768:804], start=(k == 0), stop=(k == 7))
        S.I("act", "activation", [P.pbuf[bank]], [gt_b], out=gt[:, t, :], in_=P.ps(bank)[:, 0:36], func=AF.Sigmoid)
    S.DMA("sp", [gt_b], [], out=dr["gt"], in_=gt.rearrange("p t c -> p (t c)"))
    S.barrier()
    A.release(m)


def nsa_attn(P, dr, KTc, Vc, kc_b):
    S, A = P.S, P.A
    m = A.mark()
    A.limit = A.nbytes
    KTs = A.alloc(T, BF16)
    KTs_b = [Buf("KTs%d" % i) for i in range(4)]
    KTa_b = Buf("KTsaug")
    Vs = A.alloc(64 * 128, BF16).rearrange("p (b c) -> p b c", b=64)
    Vs_b = [Buf("Vs%d" % i) for i in range(4)]
    KTw = A.alloc(8 * 128, BF16)
    KTw_b = Buf("KTw")
    Vw = A.alloc(8 * 96, BF16).rearrange("p (b c) -> p b c", b=8)
    Vw_b = Buf("Vw")
    Wide = A.alloc(T, BF16)
    wmask = A.alloc(8 * 512 // 2, BF16).bitcast(U8).rearrange("p (i n) -> p i n", i=8)
    cmask = A.alloc(4 * 512 // 2, BF16).bitcast(U8).rearrange("p (i n) -> p i n", i=4)
    cmask_b = Buf("cmask")
    qrel = A.alloc(512, F16)
    krel = A.alloc(16, F32)
    sel = A.alloc(256, F32).rearrange("p (a c) -> p a c", a=2)
    agg = A.alloc(4 * 128, BF16).rearrange("p (i n) -> p i n", i=4)
    ident32 = A.alloc(128, F32)
    identb = A.alloc(128, BF16)
    cst_b = Buf("cst")
    wide_b = Buf("wide")
    S.DMA("sp", [], [wide_b], out=Wide, in_=dr["wide"])
    for dst, name in ((qrel, "qrel"), (krel, "krel"), (ident32, "ident32"), (identb, "identb")):
        S.DMA("sp", [], [cst_b], out=dst, in_=dr[name])
    S.DMA("sp", [], [cst_b], out=wmask, in_=dr["wmask"].rearrange("p (i n) -> p i n", i=8))
    S.DMA("sp", [], [cst_b], out=sel, in_=dr["sel"].rearrange("p (a c) -> p a c", a=2))
    S.DMA("sp", [], [cst_b], out=agg, in_=dr["agg"].rearrange("p (i n) -> p i n", i=4))
    selT = A.alloc(512, BF16)
    selT_b = Buf("selT")
    smask = [A.alloc(512, BF16) for _ in range(2)]
    smask_b = [Buf("smask0"), Buf("smask1")]
    PT = [A.alloc(512, BF16) for _ in range(4)]
    PT_b = [Buf("PT%d" % i) for i in range(4)]
    PTcs = [A.alloc(4 * 512, BF16).rearrange("p (i n) -> p i n", i=4) for _ in range(2)]
    PTcs_b = [[Buf("PTc%d_%d" % (k_, i)) for i in range(4)] for k_ in range(2)]
    QT = A.alloc(6 * 512, BF16)
    QTv = QT.rearrange("p (h n) -> p h n", h=6)
    QT_b = Buf("QT")
    gt = A.alloc(16 * 36, F32).rearrange("p (t c) -> p t c", t=16)
    gt_b = Buf("gt")
    S.DMA("sp", [], [gt_b], out=gt, in_=dr["gt"].rearrange("p (t c) -> p t c", t=16))
    osb = [A.alloc(512, BF16) for _ in range(4)]
    osb_b = [Buf("osb%d" % i) for i in range(4)]
    accp = [A.alloc(4 * 128, F32).rearrange("p (t c) -> p t c", t=4) for _ in range(3)]
    accp_b = [Buf("accp%d" % i) for i in range(3)]
    impacc = A.alloc(4 * 128, F32).rearrange("p (t c) -> p t c", t=4)
    impacc_b = Buf("impacc")
    fbias = A.alloc(4 * 128, F32).rearrange("p (t c) -> p t c", t=4)
    fbias_b = Buf("fbias")
    NRD = 8
    rd = [A.alloc(4, F32) for _ in range(NRD)]
    rd_b = [Buf("rd%d" % i) for i in range(NRD)]
    fsc = [A.alloc(4, F32) for _ in range(2)]
    fsc_b = [Buf("fsc0"), Buf("fsc1")]
    impb = A.alloc(128, F32)
    impb2 = A.alloc(128, F32)
    m8 = A.alloc(16, F32)
    thr = A.alloc(1, F32)
    selm = A.alloc(128, BF16)
    tk_b = Buf("topk")
    st = {"s": 0, "p": 0, "o": 0, "ob": 3, "e": 0, "rd": 0, "f": 0, "sm": 0, "Sbanks": [0, 1, 2, 6]}

    S.I("pool", "memset", [], [KTa_b], ap=KTs[64:96, :], constant=0.0)
    S.I("pool", "memset", [], [QT_b], ap=QT[64:96, :], constant=0.0)
    S.DMA("sp", [], [KTa_b], out=KTs[64:71, :], in_=dr["kaug"])
    S.I("pool", "memset", [], Vs_b, ap=Vs, constant=0.0)
    S.I("pool", "memset", [], Vs_b, ap=Vs[:, :, 64:65], constant=1.0)
    S.I("pool", "memset", [], [Vw_b], ap=Vw, constant=0.0)
    S.I("pool", "memset", [], [Vw_b], ap=Vw[:, :, 64:65], constant=1.0)
    S.I("pool", "memset", [], [KTw_b], ap=KTw, constant=0.0)

    def epilogue(ob, hh, br, qg, g, first_branch, defer=True):
        e = st["e"] % 4
        st["e"] += 1
        ri = st["rd"] % NRD
        st["rd"] += 1
        S.I("act", "activation", [P.pbuf[ob]], [osb_b[e]], out=osb[e][0:72, :], in_=P.ps(ob)[0:72, :], func=AF.Copy)

        def rest():
            epilogue_rest(e, ri, hh, br, qg, g, first_branch)
        if defer:
            return ri, rest
        rest()
        return ri, None

    def epilogue_rest(e, ri, hh, br, qg, g, first_branch):
        fi = st["f"] % 2
        st["f"] += 1
        ps7b = P.ps(7).bitcast(BF16)
        ps7 = ps7b[:, 0:288].rearrange("p (t c) -> p t c", t=4)
        for t in range(4):
            S.I("pe", "transpose", [osb_b[e], cst_b], [P.pbuf[7]], out=ps7b[:, t * 72:(t + 1) * 72],
                in_=osb[e][0:72, t * 128:(t + 1) * 128], identity=identb[0:72, 0:72])
        S.I("dve", "tensor_scalar", [P.pbuf[7]], [rd_b[ri]], out=rd[ri], in0=ps7[:, :, 64], scalar1=1e-30, scalar2=None,
            op0=ALU.max)
        S.I("dve", "reciprocal", [rd_b[ri]], [rd_b[ri]], out=rd[ri], in_=rd[ri])
        S.I("dve", "tensor_tensor", [rd_b[ri], gt_b], [fsc_b[fi]], out=fsc[fi], in0=rd[ri],
            in1=gt[:, 4 * qg:4 * qg + 4, hh * 3 + br], op=ALU.mult)
        pslot = (hh // 2) - 3 * g
        c0 = 64 * (hh % 2)
        for t in range(4):
            dst = accp[pslot][:, t, c0:c0 + 64]
            if first_branch:
                S.I("dve", "tensor_scalar", [P.pbuf[7], fsc_b[fi]], [accp_b[pslot]], out=dst, in0=ps7[:, t, 0:64],
                    scalar1=fsc[fi][:, t:t + 1], scalar2=None, op0=ALU.mult)
            else:
                S.I("dve", "scalar_tensor_tensor", [P.pbuf[7], fsc_b[fi], accp_b[pslot]], [accp_b[pslot]], out=dst,
                    in0=ps7[:, t, 0:64], scalar=fsc[fi][:, t:t + 1], in1=dst, op0=ALU.mult, op1=ALU.add)

    for g in range(2):
        for i in range(4):
            ks = slice(i * 2048, (i + 1) * 2048)
            S.DMA("sp", [], [KTs_b[i]], out=KTs[0:64, ks], in_=dr["kvT_all"][256 + g * 64:256 + (g + 1) * 64, ks])
            S.DMA("sp", [], [Vs_b[i]], out=Vs[:, i * 16:(i + 1) * 16, 0:64],
                  in_=dr["vtok_all"][ks, g * 64:(g + 1) * 64].rearrange("(b p) c -> p b c", p=128))
        for qg in range(NG):
            tk = slice(qg * 512, (qg + 1) * 512)
            S.DMA("sp", [], [QT_b], out=QTv[0:64, :, :], in_=dr["q_scr"][g * 6:(g + 1) * 6, :, tk].rearrange("h d n -> d h n"))
            S.DMA("sp", [], [QT_b], out=QTv[64:71, :, :], in_=dr["qaug"][g * 6:(g + 1) * 6, :, tk].rearrange("h d n -> d h n"))
            S.DMA("sp", [], [KTw_b], out=KTw[0:64, :], in_=dr["ktw"][g, qg])
            S.DMA("sp", [], [KTw_b], out=KTw[64:71, :], in_=dr["kaugw"][qg])
            S.DMA("sp", [], [Vw_b], out=Vw[:, :, 0:64], in_=dr["vw"][g, qg].rearrange("(b p) c -> p b c", p=128))
            S.DMA("sp", [], [cmask_b], out=cmask, in_=dr["cmask"][:, :, tk])
            S.DMA("sp", [], [fbias_b], out=fbias, in_=dr["fbias"][:, 4 * qg:4 * qg + 4, :])
            st["Sbanks"] = [0, 1, 2]
            for h in range(6):
                hh = g * 6 + h
                ob = 3 + (st["o"] % 3)
                st["o"] += 1
                hold = {}
                slots = []
                PTc, PTc_b = PTcs[h % 2], PTcs_b[h % 2]
                for nb in range(4):
                    def post(pt, ptb, nb=nb):
                        S.I("dve", "scalar_tensor_tensor", [ptb, cmask_b], [ptb], out=pt, in0=pt, scalar=1.0e30, in1=cmask[:, nb, :],
                            op0=ALU.min, op1=ALU.mult)
                    slots.append(dict(k=KTc[0:96, g, nb * 128:(nb + 1) * 128], kb=[kc_b], q=QTv[0:96, h, :], qb=[QT_b],
                                      v=Vc[:, g, nb, :], vb=[kc_b], ob=ob, first=(nb == 0), last=(nb == 3), scale=NSA_SCALE,
                                      post=post, pt=PTc[:, nb, :], ptb=PTc_b[nb]))

                ps6 = P.ps(6).rearrange("p (t c) -> p t c", t=4)

                def epi(ob=ob, hh=hh, h=h, ps6=ps6):
                    ri, rest = epilogue(ob, hh, 0, qg, g, True)

                    def rest2():
                        rest()
                        for t in range(4):
                            if h == 0:
                                S.I("dve", "tensor_scalar", [P.pbuf[6], rd_b[ri]], [impacc_b], out=impacc[:, t, :], in0=ps6[:, t, :],
                                    scalar1=rd[ri][:, t:t + 1], scalar2=None, op0=ALU.mult)
                            else:
                                S.I("dve", "scalar_tensor_tensor", [P.pbuf[6], rd_b[ri], impacc_b], [impacc_b], out=impacc[:, t, :],
                                    in0=ps6[:, t, :], scalar=rd[ri][:, t:t + 1], in1=impacc[:, t, :], op0=ALU.mult, op1=ALU.add)
                    return rest2
                slots[-1]["epi"] = epi
                run_slots(P, slots, PT, PT_b, st)
                pend_flush(st, keep=1)
                for t in range(4):
                    for nb in range(4):
                        S.I("pe", "matmul", [PTc_b[nb], cst_b], [P.pbuf[6]], out=ps6[:, t, :], lhsT=PTc[:, nb, t * 128:(t + 1) * 128],
                            rhs=agg[:, nb, :], start=(nb == 0), stop=(nb == 3))
            pend_flush(st)
            st["Sbanks"] = [0, 1, 2, 6]
            ps6b = P.ps(7).bitcast(BF16)
            for t in range(4):
                S.I("dve", "tensor_tensor", [impacc_b, fbias_b, tk_b], [tk_b], out=impb, in0=impacc[:, t, :], in1=fbias[:, t, :], op=ALU.add)
                S.I("dve", "max", [tk_b], [tk_b], out=m8[:, 0:8], in_=impb)
                S.I("dve", "match_replace", [tk_b], [tk_b], out=impb2, in_to_replace=m8[:, 0:8], in_values=impb, imm_value=NEG_BIG)
                S.I("dve", "max", [tk_b], [tk_b], out=m8[:, 8:16], in_=impb2)
                S.I("dve", "tensor_scalar", [tk_b], [tk_b], out=thr, in0=m8[:, 15:16], scalar1=-1.0e29, scalar2=None, op0=ALU.max)
                S.I("dve", "tensor_scalar", [tk_b], [tk_b], out=selm, in0=impb, scalar1=thr[:, 0:1], scalar2=MASK_BIG, op0=ALU.is_ge, op1=ALU.mult)
                S.I("pe", "transpose", [tk_b, cst_b], [P.pbuf[7]], out=ps6b[:, t * 128:(t + 1) * 128], in_=selm, identity=identb)
            S.I("act", "activation", [P.pbuf[7]], [selT_b], out=selT, in_=ps6b[:, 0:512], func=AF.Copy)
            for h in range(6):
                hh = g * 6 + h
                ob = 3 + (st["o"] % 3)
                st["o"] += 1
                slots = []
                for i in range(8):
                    def post(pt, ptb, i=i):
                        S.I("dve", "scalar_tensor_tensor", [ptb, cst_b], [ptb], out=pt, in0=pt, scalar=1.0e30, in1=wmask[:, i, :],
                            op0=ALU.min, op1=ALU.mult)
                    slots.append(dict(k=KTw[0:96, i * 128:(i + 1) * 128], kb=[KTw_b], q=QTv[0:96, h, :], qb=[QT_b],
                                      v=Vw[:, i, :], vb=[Vw_b], ob=ob, first=(i == 0), last=(i == 7), scale=NSA_SCALE, post=post))

                def epi(ob=ob, hh=hh):
                    return epilogue(ob, hh, 2, qg, g, False)[1]
                slots[-1]["epi"] = epi
                run_slots(P, slots, PT, PT_b, st)
            nkb = 16 * qg + 16
            for trio in range(2):
                heads = [trio * 3 + i for i in range(3)]
                slots = []
                cur = {}
                for kb in range(nkb):
                    def pre(kb=kb, cur=cur):
                        si = st["sm"] % 2
                        st["sm"] += 1
                        S.I("pe", "matmul", [wide_b, selT_b], [P.pbuf[7]], out=P.ps(7), lhsT=Wide[:, kb * 128:(kb + 1) * 128],
                            rhs=selT, start=True, stop=True)
                        i = kb - 16 * qg
                        if i >= 0:
                            S.I("dve", "scalar_tensor_tensor", [cst_b, P.pbuf[7]], [smask_b[si]], out=smask[si], in0=qrel,
                                scalar=krel[:, i:i + 1], in1=P.ps(7), op0=ALU.is_ge, op1=ALU.mult)
                        elif kb % 2 == 0:
                            S.I("act", "activation", [P.pbuf[7]], [smask_b[si]], out=smask[si], in_=P.ps(7), func=AF.Copy)
                        else:
                            S.I("dve", "tensor_copy", [P.pbuf[7]], [smask_b[si]], out=smask[si], in_=P.ps(7))
                        cur["si"] = si
                    for ii, h in enumerate(heads):
                        def post(pt, ptb, cur=cur):
                            si = cur["si"]
                            S.I("dve", "tensor_tensor", [ptb, smask_b[si]], [ptb], out=pt, in0=pt, in1=smask[si], op=ALU.min)
                        d = dict(k=KTs[0:96, kb * 128:(kb + 1) * 128], kb=[KTs_b[kb // 16], KTa_b], q=QTv[0:96, h, :], qb=[QT_b],
                                 v=Vs[:, kb, :], vb=[Vs_b[kb // 16]], ob=3 + ii, first=(kb == 0), last=(kb == nkb - 1),
                                 scale=NSA_SCALE, post=post, m=128)
                        if ii == 0:
                            d["pre"] = pre
                        if kb == nkb - 1:
                            def epi(ob=3 + ii, hh=g * 6 + h):
                                return epilogue(ob, hh, 1, qg, g, False)[1]
                            d["epi"] = epi
                        slots.append(d)
                run_slots(P, slots, PT, PT_b, st)
            pend_flush(st)
            for pslot in range(3):
                pair = 3 * g + pslot
                for t in range(4):
                    S.I("pe", "transpose", [accp_b[pslot], cst_b], [P.pbuf[7]], out=P.ps(7)[:, t * 128:(t + 1) * 128],
                        in_=accp[pslot][:, t, :], identity=ident32)
                S.I("act", "activation", [P.pbuf[7]], [P.xT_b[qg][pair]], out=P.xT[:, pair, tk], in_=P.ps(7), func=AF.Copy)
    mkv = Wide[:, 0:2048]
    mkv_b = wide_b
    S.DMA("sp", [], [mkv_b], out=mkv, in_=dr["memkv"])
    kpad = mkv[:, 0:1024].rearrange("p (h m) -> p h m", h=4)
    vaug = mkv[:, 1024:2048].rearrange("p (h b c) -> p h b c", h=4, b=2)
    QTm = [KTs[:, 0:2048], KTs[:, 2048:4096]]
    QTm_b = [KTs_b[0], KTs_b[1]]
    osbm = [KTs[:, 4096:5120].bitcast(F32), KTs[:, 5120:6144].bitcast(F32)]
    osbm_b = [KTs_b[2], Buf("osbm1")]
    st2 = {"s": st["s"], "p": st["p"], "o": 0, "ob": 3, "e": 0, "maxpend": 1}
    mem_heads(P, dr, kpad, vaug, mkv_b, QTm, QTm_b, PT, PT_b, osbm, osbm_b, sel, cst_b, st2)
    pend_flush(st2)
    S.barrier()
    A.release(m)


BF = ml_dtypes.bfloat16


def alibi_slopes12():
    def pow2(k):
        start = 2.0 ** (-8.0 / k)
        return [start ** (i + 1) for i in range(k)]
    s = pow2(8) + pow2(16)[0::2][:4]
    return np.asarray(s, np.float32)


def split_bf(v, n):
    out = []
    r = np.asarray(v, np.float64)
    for _ in range(n):
        p = r.astype(np.float32).astype(BF)
        out.append(p)
        r = r - p.astype(np.float64)
    return out


def kaug_table(pos):
    hi, lo = split_bf(pos, 2)
    one = np.ones_like(hi)
    return np.stack([hi, lo, hi, lo, one, one, one], axis=0)


def nsa_consts_global():
    kaug = kaug_table(np.arange(T, dtype=np.float64))
    cpos = np.arange(512, dtype=np.float64) * 16 + 15.5
    cpos[511] = 0.0
    kaug_c = kaug_table(cpos)
    wide = (np.arange(128)[:, None] == (np.arange(T)[None, :] // 64)).astype(np.float32).astype(BF)
    cn = np.arange(512)
    cs = cn * 16
    ss = np.arange(128) * 64
    ov = np.clip(np.minimum(cs[:, None] + 32, ss[None, :] + 64) - np.maximum(cs[:, None], ss[None, :]), 0, None) / 32.0
    ov[511] = 0.0
    agg = ov.reshape(4, 128, 128).transpose(1, 0, 2).reshape(128, 512).astype(np.float32).astype(BF)
    ident32 = np.eye(128, dtype=np.float32)
    identb = np.eye(128, dtype=np.float32).astype(BF)
    dq = np.arange(512)[None, None, :] + 512 - 128 * np.arange(8)[None, :, None] - np.arange(128)[:, None, None]
    wmask = ((dq >= 0) & (dq < 512)).astype(np.uint8).reshape(128, 8 * 512)
    return dict(kaug=kaug, kaug_c=kaug_c, wide=wide, agg=agg, ident32=ident32, identb=identb, wmask=wmask)


def window_gather(kvT_all_b, vtok_all_b, j):
    ktw = np.zeros((2, 4, 64, 1024), BF)
    vw = np.zeros((2, 4, 1024, 64), BF)
    pos = np.full((4, 1024), -30000.0, np.float64)
    for qg in range(4):
        for i in range(8):
            rb = 16 * qg + 4 * j - 4 + i
            if rb < 0:
                continue
            ks = slice(rb * 128, (rb + 1) * 128)
            pos[qg, i * 128:(i + 1) * 128] = np.arange(rb * 128, (rb + 1) * 128)
            for g in range(2):
                ktw[g, qg, :, i * 128:(i + 1) * 128] = kvT_all_b[512 + g * 64:512 + (g + 1) * 64, ks]
                vw[g, qg, i * 128:(i + 1) * 128, :] = vtok_all_b[ks, 128 + g * 64:128 + (g + 1) * 64]
    kaugw = np.stack([kaug_table(pos[qg]) for qg in range(4)])
    return ktw, vw, kaugw


def nsa_consts_core(j):
    qpos = core_token_index(j).astype(np.int64)
    slopes = alibi_slopes12().astype(np.float64)
    qaug = np.zeros((12, 7, TLOC), BF)
    for h in range(12):
        mhi, mlo = split_bf(np.array([slopes[h]]), 2)
        mp = float(mhi[0]) + float(mlo[0])
        v = -8.0 * mp * qpos.astype(np.float64)
        vh, vm, vl = split_bf(v, 3)
        m8hi = (np.float32(8.0) * mhi.astype(np.float32)).astype(BF)[0]
        m8lo = (np.float32(8.0) * mlo.astype(np.float32)).astype(BF)[0]
        qaug[h, 0] = m8hi
        qaug[h, 1] = m8hi
        qaug[h, 2] = m8lo
        qaug[h, 3] = m8lo
        qaug[h, 4] = vh
        qaug[h, 5] = vm
        qaug[h, 6] = vl
    qrel = qpos[:512]
    cnn = (128 * np.arange(4))[None, :, None] + np.arange(128)[:, None, None]
    cmask = ((cnn <= 510) & (16 * cnn + 31 <= qpos[None, None, :])).astype(np.uint8)
    tq = qpos.reshape(16, 128).T
    jj = np.arange(128)[None, None, :]
    cur = (tq // 64)[:, :, None]
    forced = (jj == 0) | (jj == cur) | (jj == cur - 1)
    valid = jj * 64 <= tq[:, :, None]
    fb = np.where(valid, np.where(forced, 1.0e4, 0.0), NEG_BIG).astype(np.float32).reshape(128, 16 * 128)
    return dict(qaug=qaug, cmask=np.ascontiguousarray(cmask), fbias=fb)


def prep_cmp(cmp_pos, cmp_w1, cmp_b1, cmp_w2):
    w1 = np.stack([kmajor(cmp_w1[j]) for j in range(2)])
    w2 = np.stack([kmajor(cmp_w2[j]) for j in range(2)])
    b1 = np.ascontiguousarray(cmp_b1.reshape(2, 2, 128).transpose(0, 2, 1))
    pp = np.ascontiguousarray(cmp_pos.reshape(2, 16, 2, 64).transpose(0, 2, 3, 1).reshape(2, 128, 16))
    return w1, w2, b1, pp


def _mla_pre_io(P, pre):
    return {
        "w_in": P.din(pre + "w_in", [128, 8 * 672]), "w_uq": P.din(pre + "w_uq", [128, 2 * 1152]), "ng": P.din(pre + "ng", [128, 3]),
        "tabc": P.din("tabc", [128, TLOC]) if "tabc" not in P.dram else P.dram["tabc"],
        "tabs": P.din("tabs", [128, TLOC]) if "tabs" not in P.dram else P.dram["tabs"],
        "memT": P.din("memT", [8, 128, 256]) if "memT" not in P.dram else P.dram["memT"],
        "w_mkv": P.din(pre + "w_mkv", [128, 8 * 512]),
        "lat_c": P.dout("o_lat_c", [128, TLOC], BF16), "lat_r": P.dout("o_lat_r", [32, TLOC], BF16),
        "q_scr": P.dout("o_q_scr", [12, 96, TLOC], BF16), "qm": P.dout("o_qm", [2, 128, TLOC], BF16),
        "memkv": P.dout("o_memkv", [128, 2048], BF16),
    }


def _mla_attn_io(P, pre):
    return {
        "ckv_all": P.din("ckv_all", [128, T], BF16), "kr_all": P.din("kr_all", [32, T], BF16),
        "q_scr": P.din("i_q_scr", [12, 96, TLOC], BF16), "qm": P.din("i_qm", [2, 128, TLOC], BF16),
        "memkv": P.din("i_memkv", [128, 2048], BF16), "w_ukv": P.din(pre + "w_ukv", [128, 1536]),
        "qrel": P.din("qrel", [128, 512], F16), "krel": P.din("krel", [128, 16]), "sel": P.din("sel", [128, 256]),
        "w_out": P.din(pre + "w_out", [128, 8 * 1024]),
    }


def _ffn_io(P, l, i):
    return P.din("wgu%d%d" % (l, i), [NF, 128, 2048]), P.din("wd%d%d" % (l, i), [8, 128, NF * 128])


def build_phase(ph):
    nc = bass.Bass("TRN2", target_bir_lowering=False)
    with ExitStack() as stack:
        P = Prog(nc, stack)
        xT_d = P.din("xT", [8, 128, TLOC])
        lnp_d = P.din("lnp", [128, 192])
        P.alloc_state(lnp_d)
        if ph == 0:
            P.load_xT(xT_d)
            P.ffn(*_ffn_io(P, 0, 0), 0, relaxed=True)
            mla_pre(P, _mla_pre_io(P, "l0_"))
        elif ph == 1:
            P.load_xT(xT_d)
            dr = _mla_attn_io(P, "l0_")
            mla_attn(P, dr)
            resid_ln_from_oT(P, dr["w_out"], 1)
            P.ffn(*_ffn_io(P, 0, 1), 2, relaxed=True)
            P.ffn(*_ffn_io(P, 1, 0), 3, relaxed=True)
            mla_pre(P, _mla_pre_io(P, "l1_"))
        elif ph == 2:
            P.load_xT(xT_d)
            dr = _mla_attn_io(P, "l1_")
            mla_attn(P, dr)
            resid_ln_from_oT(P, dr["w_out"], 4)
            P.ffn(*_ffn_io(P, 1, 1), 5, relaxed=True)
            nsa_kv_stage(P, {"w_kv": P.din("w_kv", [128, 8 * 768]), "kvT": P.dout("o_kvT", [768, TLOC], BF16),
                             "vtok": P.dout("o_vtok", [TLOC, 256], BF16)})
        else:
            KTc = P.A.alloc(2 * 512, BF16).rearrange("p (g n) -> p g n", g=2)
            Vc = P.A.alloc(2 * 4 * 96, BF16).rearrange("p (g b c) -> p g b c", g=2, b=4)
            kc_b = Buf("kc")
            P.load_xT(xT_d)
            dr = {
                "kvT_all": P.din("kvT_all", [768, T], BF16), "vtok_all": P.din("vtok_all", [T, 256], BF16),
                "cmp_w1": P.din("cmp_w1", [2, 128, 16 * 256]), "cmp_w2": P.din("cmp_w2", [2, 128, 128]),
                "cmp_b1": P.din("cmp_b1", [2, 128, 2]), "cmp_pp": P.din("cmp_pp", [2, 128, 16]),
                "kaug": P.din("kaug", [7, T], BF16), "kaug_c": P.din("kaug_c", [7, 512], BF16),
                "wide": P.din("wide", [128, T], BF16), "agg": P.din("agg", [128, 512], BF16),
                "ident32": P.din("ident32", [128, 128]), "identb": P.din("identb", [128, 128], BF16),
                "qaug": P.din("qaug", [12, 7, TLOC], BF16), "wmask": P.din("wmask", [128, 8 * 512], U8),
                "cmask": P.din("cmask", [128, 4, TLOC], U8), "fbias": P.din("fbias", [128, 16, 128]),
                "qrel": P.din("qrel", [128, 512], F16), "krel": P.din("krel", [128, 16]), "sel": P.din("sel", [128, 256]),
                "memT": P.din("memT", [8, 128, 256]),
                "ktw": P.din("ktw", [2, 4, 64, 1024], BF16), "vw": P.din("vw", [2, 4, 1024, 64], BF16),
                "kaugw": P.din("kaugw", [4, 7, 1024], BF16),
                "q_scr": P.dscr("q_scr", [12, 64, TLOC], BF16), "qm": P.dscr("qm", [2, 128, TLOC], BF16),
                "gt": P.dscr("gt", [128, 16 * 36]), "memkv": P.dscr("memkv", [128, 2048], BF16),
            }
            nsa_compress(P, dr, KTc, Vc, kc_b)
            for l in (2, 3):
                P.ffn(*_ffn_io(P, l, 0), 3 * l, relaxed=True)
                drl = dict(dr)
                drl["w_in"] = P.din("l%d_w_in" % l, [128, 8 * 1060])
                drl["w_mkv"] = P.din("l%d_w_mkv" % l, [128, 8 * 512])
                nsa_pre(P, drl)
                nsa_attn(P, drl, KTc, Vc, kc_b)
                resid_ln_from_oT(P, P.din("l%d_w_out" % l, [128, 8 * 1024]), 3 * l + 1)
                P.ffn(*_ffn_io(P, l, 1), 3 * l + 2, relaxed=True)
        if ph < 3:
            P.store_xT(P.dout("o_x", [8, 128, TLOC]))
        else:
            P.store_xT(P.dout("out", [8, 128, TLOC]))
        P.S.barrier()
        block = stack.enter_context(nc.Block())
        P.S.emit(block)
    return nc


def _gather_tokens_T(parts, rows, dtype):
    outs = [np.zeros((rows, T), dtype) for _ in range(NB)]
    for c in range(NCORE):
        b, j = divmod(c, 4)
        outs[b][:, core_token_index(j)] = parts[c]
    return outs


def kernel(x, mem, ln_g, ln_b, ffn_w_gu, ffn_w_down, w_mem_kv, w_out, mla_w_in, mla_q_norm_g, mla_kv_norm_g,
           mla_w_uq, mla_w_ukv, nsa_w_in, nsa_w_kv, cmp_pos, cmp_w1, cmp_b1, cmp_w2):
    f = lambda a: np.ascontiguousarray(np.asarray(a, dtype=np.float32))
    x, mem = f(x), f(mem)
    lnp = prep_ln(f(ln_g), f(ln_b))
    ffw = {}
    for l in range(DEPTH):
        for i in range(2):
            ffw[(l, i)] = prep_ffn_weights(f(ffn_w_gu[l, i]), f(ffn_w_down[l, i]))
    memT = [np.ascontiguousarray(mem[b].T.reshape(8, 128, 256)) for b in range(NB)]
    tabs_ = [rope_tables(j) for j in range(4)]
    acon = [attn_consts(j) for j in range(4)]
    cores = list(range(NCORE))

    def mla_pre_in(l):
        ng = np.stack([mla_q_norm_g[l][:128], mla_q_norm_g[l][128:], mla_kv_norm_g[l]], axis=1).astype(np.float32)
        return {"l%d_w_in" % l: kmajor(f(mla_w_in[l])), "l%d_w_uq" % l: kmajor(f(mla_w_uq[l])), "l%d_ng" % l: np.ascontiguousarray(ng),
                "l%d_w_mkv" % l: kmajor(f(w_mem_kv[l]))}

    def percore_pre(c):
        b, j = divmod(c, 4)
        return {"tabc": tabs_[j][0], "tabs": tabs_[j][1], "memT": memT[b]}

    def mla_attn_in(l, prev, c, ckv_all, kr_all):
        b, j = divmod(c, 4)
        return {"ckv_all": ckv_all[b], "kr_all": kr_all[b], "i_q_scr": prev[c]["o_q_scr"], "i_qm": prev[c]["o_qm"],
                "i_memkv": prev[c]["o_memkv"], "l%d_w_ukv" % l: f(mla_w_ukv[l]), "qrel": acon[j][0], "krel": acon[j][1],
                "sel": acon[j][2], "l%d_w_out" % l: kmajor(f(w_out[l]))}

    def ffn_in(l, i):
        return {"wgu%d%d" % (l, i): ffw[(l, i)][0], "wd%d%d" % (l, i): ffw[(l, i)][1]}

    xs = shard_xT(x)
    com = mla_pre_in(0)
    com.update(ffn_in(0, 0))
    maps = []
    for c in cores:
        m_ = {"xT": xs[c], "lnp": lnp}
        m_.update(com)
        m_.update(percore_pre(c))
        maps.append(m_)
    rr0 = run_bass_kernel_spmd(build_phase(0), maps, core_ids=cores, trace=_TRACE)
    r0 = rr0.results
    if _TRACE:
        print('PHASE 0 exec_ns', rr0.exec_time_ns, flush=True)
    if _DBG is not None:
        _DBG(0, r0)
    ckv_all = _gather_tokens_T([r0[c]["o_lat_c"] for c in cores], 128, BF)
    kr_all = _gather_tokens_T([r0[c]["o_lat_r"] for c in cores], 32, BF)
    com = mla_pre_in(1)
    com.update(ffn_in(0, 1))
    com.update(ffn_in(1, 0))
    maps = []
    for c in cores:
        m_ = {"xT": r0[c]["o_x"], "lnp": lnp}
        m_.update(com)
        m_.update(percore_pre(c))
        m_.update(mla_attn_in(0, r0, c, ckv_all, kr_all))
        maps.append(m_)
    rr1 = run_bass_kernel_spmd(build_phase(1), maps, core_ids=cores, trace=_TRACE)
    r1 = rr1.results
    if _TRACE:
        print('PHASE 1 exec_ns', rr1.exec_time_ns, flush=True)
    if _DBG is not None:
        _DBG(1, r1)
    ckv_all = _gather_tokens_T([r1[c]["o_lat_c"] for c in cores], 128, BF)
    kr_all = _gather_tokens_T([r1[c]["o_lat_r"] for c in cores], 32, BF)
    com = ffn_in(1, 1)
    com["w_kv"] = kmajor(f(nsa_w_kv))
    maps = []
    for c in cores:
        m_ = {"xT": r1[c]["o_x"], "lnp": lnp}
        m_.update(com)
        m_.update(mla_attn_in(1, r1, c, ckv_all, kr_all))
        maps.append(m_)
    rr2 = run_bass_kernel_spmd(build_phase(2), maps, core_ids=cores, trace=_TRACE)
    r2 = rr2.results
    if _TRACE:
        print('PHASE 2 exec_ns', rr2.exec_time_ns, flush=True)
    if _DBG is not None:
        _DBG(2, r2)
    kvT_all = _gather_tokens_T([r2[c]["o_kvT"] for c in cores], 768, BF)
    vt = _gather_tokens_T([np.ascontiguousarray(np.asarray(r2[c]["o_vtok"]).T) for c in cores], 256, BF)
    vtok_all = [np.ascontiguousarray(v.T) for v in vt]
    gc = nsa_consts_global()
    w1, w2, b1, pp = prep_cmp(f(cmp_pos), f(cmp_w1), f(cmp_b1), f(cmp_w2))
    com = {"cmp_w1": w1, "cmp_w2": w2, "cmp_b1": b1, "cmp_pp": pp, "kaug": gc["kaug"], "kaug_c": gc["kaug_c"],
           "wide": gc["wide"], "agg": gc["agg"], "ident32": gc["ident32"], "identb": gc["identb"], "wmask": gc["wmask"]}
    for l in (2, 3):
        com.update(ffn_in(l, 0))
        com.update(ffn_in(l, 1))
        com["l%d_w_in" % l] = kmajor(f(nsa_w_in[l - 2]))
        com["l%d_w_mkv" % l] = kmajor(f(w_mem_kv[l]))
        com["l%d_w_out" % l] = kmajor(f(w_out[l]))
    ccon = [nsa_consts_core(j) for j in range(4)]
    maps = []
    for c in cores:
        b, j = divmod(c, 4)
        ktw_, vw_, kaugw_ = window_gather(kvT_all[b], vtok_all[b], j)
        m_ = {"xT": r2[c]["o_x"], "lnp": lnp, "kvT_all": kvT_all[b], "vtok_all": vtok_all[b], "qaug": ccon[j]["qaug"],
              "ktw": ktw_, "vw": vw_, "kaugw": kaugw_,
              "cmask": ccon[j]["cmask"], "fbias": ccon[j]["fbias"].reshape(128, 16, 128),
              "qrel": acon[j][0], "krel": acon[j][1], "sel": acon[j][2], "memT": memT[b]}
        m_.update(com)
        maps.append(m_)
    rr3 = run_bass_kernel_spmd(build_phase(3), maps, core_ids=cores, trace=_TRACE)
    r3 = rr3.results
    if _TRACE:
        print('PHASE 3 exec_ns', rr3.exec_time_ns, flush=True)
    if _DBG is not None:
        _DBG(3, r3)
    return unshard_xT([np.asarray(r3[c]["out"]) for c in cores])
```
